# Optimizing a Trainium2 kernel written in Bass

```python
import math
import jax, jax.numpy as jnp
from jax import lax
import numpy as np

D_MODEL = 1024
BATCH = 8
SEQ = 4096
DEPTH = 1

A_HEADS = 8
A_HEAD_DIM = 64
A_WIDTH = A_HEADS * A_HEAD_DIM
DECAY_LORA = 64
ICLR_LORA = 64
DECAY_SCALE = 0.6065306597
GN_EPS = 64e-5
B_HEADS = 8
QK_NOPE_DIM = 64
QK_ROPE_DIM = 32
V_HEAD_DIM = 64
Q_LORA_RANK = 256
KV_LORA_RANK = 128
B_WIDTH = B_HEADS * V_HEAD_DIM
ROPE_THETA = 10000.0
Q_BLOCK = 128
NORM_EPS = 1e-6
N_BRANCHES = 2
SHIFT_COLS = 3 * A_WIDTH + DECAY_LORA + ICLR_LORA
IN_COLS = SHIFT_COLS + A_WIDTH + Q_LORA_RANK + KV_LORA_RANK + QK_ROPE_DIM + B_WIDTH + N_BRANCHES * D_MODEL

kernel_name = 'hybrid_rwkv7_mla_gated_block'


def rmsnorm(x, g, eps=NORM_EPS):
    xf = x.astype(jnp.float32)
    y = xf * lax.rsqrt(jnp.mean(xf * xf, axis=-1, keepdims=True) + eps)
    return (y * g.astype(jnp.float32)).astype(x.dtype)


def split_cols(t, sizes):
    idx = np.cumsum(sizes)[:-1].tolist()
    return jnp.split(t, idx, axis=-1)


def token_shift(feat, mu):
    prev = jnp.pad(feat, ((0, 0), (1, 0), (0, 0)))[:, :-1, :]
    return feat + mu * (prev - feat)


def rwkv7_branch(feat, z, mu_shift, w0, w_decay_up, a0, w_iclr_up, k_k, k_a, r_k, gn_gain, gn_bias, w_out_a):
    bsz, seq, _ = feat.shape
    dt = feat.dtype
    f = token_shift(feat, mu_shift)
    r, k, v, wd, ad = split_cols(f, [A_WIDTH, A_WIDTH, A_WIDTH, DECAY_LORA, ICLR_LORA])
    w_logit = (w0 + jnp.tanh(wd) @ w_decay_up).astype(jnp.float32)
    w = jnp.exp(-DECAY_SCALE * jax.nn.sigmoid(w_logit))
    a = jax.nn.sigmoid((a0 + ad @ w_iclr_up).astype(jnp.float32))
    hs = lambda t: t.reshape(bsz, seq, A_HEADS, A_HEAD_DIM).astype(jnp.float32)
    r, k, v, w, a = hs(r), hs(k), hs(v), hs(w), hs(a)
    kk = k * k_k.reshape(A_HEADS, A_HEAD_DIM)
    kk = kk / jnp.maximum(jnp.linalg.norm(kk, axis=-1, keepdims=True), 1e-12)
    k = k * (1.0 + (a - 1.0) * k_a.reshape(A_HEADS, A_HEAD_DIM))

    def step(S, inp):
        r_t, w_t, k_t, v_t, kk_t, a_t = inp
        sa = jnp.einsum('bhvk,bhk->bhv', S, -kk_t)
        S = S * w_t[:, :, None, :] + sa[..., None] * (kk_t * a_t)[:, :, None, :] + v_t[..., None] * k_t[:, :, None, :]
        return S, jnp.einsum('bhvk,bhk->bhv', S, r_t)

    tm = lambda t: jnp.moveaxis(t, 1, 0)
    S0 = jnp.zeros((bsz, A_HEADS, A_HEAD_DIM, A_HEAD_DIM), jnp.float32)
    _, y = lax.scan(step, S0, (tm(r), tm(w), tm(k), tm(v), tm(kk), tm(a)))
    y = jnp.moveaxis(y, 0, 1)
    mean = jnp.mean(y, axis=-1, keepdims=True)
    var = jnp.mean(jnp.square(y - mean), axis=-1, keepdims=True)
    y = (y - mean) * lax.rsqrt(var + GN_EPS)
    y = y * gn_gain.reshape(A_HEADS, A_HEAD_DIM) + gn_bias.reshape(A_HEADS, A_HEAD_DIM)
    y = y + jnp.sum(r * k * r_k, axis=-1, keepdims=True) * v
    y = y.reshape(bsz, seq, A_WIDTH).astype(dt) * jax.nn.silu(z)
    return y @ w_out_a


def rope(t, cos, sin):
    t1, t2 = jnp.split(t, 2, axis=-1)
    return jnp.concatenate([t1 * cos - t2 * sin, t2 * cos + t1 * sin], axis=-1)


def mla_branch(c_q, c_kv, k_pe, z, positions, g_q, w_uq, g_kv, w_ukv, w_out_b):
    bsz, seq, _ = c_q.shape
    dt = c_q.dtype
    q = (rmsnorm(c_q, g_q) @ w_uq).reshape(bsz, seq, B_HEADS, QK_NOPE_DIM + QK_ROPE_DIM)
    q_nope, q_pe = q[..., :QK_NOPE_DIM], q[..., QK_NOPE_DIM:]
    kv = (rmsnorm(c_kv, g_kv) @ w_ukv).reshape(bsz, seq, B_HEADS, QK_NOPE_DIM + V_HEAD_DIM)
    k_nope, v = kv[..., :QK_NOPE_DIM], kv[..., QK_NOPE_DIM:]
    inv_freq = ROPE_THETA ** (-jnp.arange(0, QK_ROPE_DIM, 2, dtype=jnp.float32) / QK_ROPE_DIM)
    ang = positions.astype(jnp.float32)[..., None] * inv_freq
    cos, sin = jnp.cos(ang).astype(dt), jnp.sin(ang).astype(dt)
    q_pe = rope(q_pe, cos[:, :, None, :], sin[:, :, None, :])
    k_pe = rope(k_pe, cos, sin)
    scale = 1.0 / math.sqrt(QK_NOPE_DIM + QK_ROPE_DIM)
    nb = seq // Q_BLOCK
    blk = lambda t: jnp.moveaxis(t.reshape(bsz, nb, Q_BLOCK, *t.shape[2:]), 1, 0)
    key_idx = jnp.arange(seq)

    def attend(args):
        qn, qp, b = args
        s = jnp.einsum('bqhd,bkhd->bhqk', qn, k_nope) + jnp.einsum('bqhr,bkr->bhqk', qp, k_pe)
        s = s.astype(jnp.float32) * scale
        q_idx = b * Q_BLOCK + jnp.arange(Q_BLOCK)
        s = jnp.where(key_idx[None, :] <= q_idx[:, None], s, jnp.finfo(jnp.float32).min)
        p = jax.nn.softmax(s, axis=-1).astype(dt)
        return jnp.einsum('bhqk,bkhd->bqhd', p, v)

    o = lax.map(attend, (blk(q_nope), blk(q_pe), jnp.arange(nb)))
    o = jnp.moveaxis(o, 0, 1).reshape(bsz, seq, B_WIDTH)
    return (o * jax.nn.silu(z)) @ w_out_b


def setup_inputs(seed: int = 0) -> dict:
    key = jax.random.key(seed)
    ks = jax.random.split(key, 24)
    n = lambda i, shape, s=1.0: s * jax.random.normal(ks[i], shape, jnp.float32)
    x = n(0, (BATCH, SEQ, D_MODEL))
    positions = (jnp.arange(SEQ, dtype=jnp.int32)[None, :]
                 + jax.random.randint(ks[1], (BATCH, 1), 0, 1024, dtype=jnp.int32))
    return {
        'x': x,
        'positions': positions,
        'g_pre': 1.0 + n(2, (D_MODEL,), 0.05),
        'w_in': n(3, (D_MODEL, IN_COLS), D_MODEL ** -0.5),
        'b_gate': n(4, (N_BRANCHES * D_MODEL,), 0.01),
        'mu_shift': jax.random.uniform(ks[5], (SHIFT_COLS,), jnp.float32),
        'w0': n(6, (A_WIDTH,), 0.5),
        'w_decay_up': n(7, (DECAY_LORA, A_WIDTH), 0.1),
        'a0': n(8, (A_WIDTH,), 0.5),
        'w_iclr_up': n(9, (ICLR_LORA, A_WIDTH), 0.5 * ICLR_LORA ** -0.5),
        'k_k': 0.85 + n(10, (A_WIDTH,), 0.05),
        'k_a': 1.0 + n(11, (A_WIDTH,), 0.05),
        'r_k': n(12, (A_HEADS, A_HEAD_DIM), 0.1),
        'gn_gain': 1.0 + n(13, (A_WIDTH,), 0.05),
        'gn_bias': n(14, (A_WIDTH,), 0.01),
        'w_out_a': n(15, (A_WIDTH, D_MODEL), A_WIDTH ** -0.5),
        'g_q': 1.0 + n(16, (Q_LORA_RANK,), 0.05),
        'w_uq': n(17, (Q_LORA_RANK, B_HEADS * (QK_NOPE_DIM + QK_ROPE_DIM)), Q_LORA_RANK ** -0.5),
        'g_kv': 1.0 + n(18, (KV_LORA_RANK,), 0.05),
        'w_ukv': n(19, (KV_LORA_RANK, B_HEADS * (QK_NOPE_DIM + V_HEAD_DIM)), KV_LORA_RANK ** -0.5),
        'w_out_b': n(20, (B_WIDTH, D_MODEL), B_WIDTH ** -0.5),
        'w_o': n(21, (D_MODEL, D_MODEL), D_MODEL ** -0.5),
        'g_post': 1.0 + n(22, (D_MODEL,), 0.05),
    }


def reference(x, positions, g_pre, w_in, b_gate, mu_shift, w0, w_decay_up, a0, w_iclr_up, k_k, k_a, r_k,
              gn_gain, gn_bias, w_out_a, g_q, w_uq, g_kv, w_ukv, w_out_b, w_o, g_post):
    h = x
    for _ in range(DEPTH):
        u = rmsnorm(h, g_pre)
        proj = u @ w_in
        feat_a, z_a, c_q, c_kv, k_pe, z_b, gate_logits = split_cols(
            proj, [SHIFT_COLS, A_WIDTH, Q_LORA_RANK, KV_LORA_RANK, QK_ROPE_DIM, B_WIDTH, N_BRANCHES * D_MODEL])
        y_a = rwkv7_branch(feat_a, z_a, mu_shift, w0, w_decay_up, a0, w_iclr_up, k_k, k_a, r_k,
                           gn_gain, gn_bias, w_out_a)
        y_b = mla_branch(c_q, c_kv, k_pe, z_b, positions, g_q, w_uq, g_kv, w_ukv, w_out_b)
        gates = jax.nn.sigmoid(gate_logits + b_gate)
        g_a, g_b = gates[..., :D_MODEL], gates[..., D_MODEL:]
        merged = g_a * y_a + g_b * y_b
        h = h + rmsnorm(merged @ w_o, g_post)
    return h
```

```python
import numpy as np
from contextlib import ExitStack
import concourse.bass as bass
import concourse.mybir as mybir
from concourse.bass_utils import run_bass_kernel_spmd

F32 = mybir.dt.float32
BF16 = mybir.dt.bfloat16
I32 = mybir.dt.int32
AF = mybir.ActivationFunctionType
ALU = mybir.AluOpType
AX = mybir.AxisListType

D = 1024
NC_ = 8
HEADS = 8
DECAY_SCALE = 0.6065306597
GN_EPS = 64e-5
NORM_EPS = 1e-6
ROPE_THETA = 10000.0
IN_COLS = 5152
C_R, C_K, C_V, C_WD, C_AD = 0, 512, 1024, 1536, 1600
C_ZA = 1664
C_CQ = 2176
C_CKV = 2432
C_KPE = 2560
C_ZB = 2592
C_GATE = 3104


class _Rec:
    def __getattr__(self, name):
        def f(*a, **k):
            self.__dict__["call"] = (name, a, k)
            return self
        return f


class Sched:
    ENG = ["pe", "dve", "act", "pool", "sp"]

    def __init__(self, nc, es, n_dma_sems=16):
        self.nc = nc
        self.sem = {e: es.enter_context(nc.semaphore("s_" + e)) for e in self.ENG}
        self.cnt = {e: 0 for e in self.ENG}
        self.dma_sems = [es.enter_context(nc.semaphore("s_dma%d" % i)) for i in range(n_dma_sems)]
        self.dma_cnt = [0] * n_dma_sems
        self.dma_rr = 0
        self.n_sw = 4
        self.dma_rr_sw = 0
        self.waited = {e: {} for e in self.ENG}
        self.ops = []
        self.writers = {}
        self.readers = {}
        self.nops = 0

    max_ops = None
    log = None

    def op(self, eng, fn, reads=(), writes=(), dma=False, force=False):
        if self.max_ops is not None and len(self.ops) >= self.max_ops and not force:
            return None
        idx = len(self.ops)
        deps = set()
        for k in reads:
            deps.update(self.writers.get(k, ()))
        for k in writes:
            deps.update(self.readers.get(k, ()))
            deps.update(self.writers.get(k, ()))
        for k in writes:
            if self.readers.get(k):
                self.writers[k] = [idx]
                self.readers[k] = []
            else:
                lst = self.writers.setdefault(k, [])
                lst.append(idx)
                if len(lst) > 40:
                    del lst[0]
        for k in reads:
            lst = self.readers.setdefault(k, [])
            lst.append(idx)
            if len(lst) > 40:
                del lst[0]
        rec = _Rec()
        fn(rec)
        call = rec.call
        fn2 = lambda e_, call=call: getattr(e_, call[0])(*call[1], **call[2])
        self.ops.append(dict(eng=eng, fn=fn2, dma=dma, deps=deps, need_inc=False, idx=idx, name=call[0]))
        if self.log is not None:
            self.log.append((idx, eng, call[0], [k for k in writes]))
        return idx

    def finalize_and_emit(self, block, barrier=True):
        ops = self.ops
        self.nops += len(ops)
        for o in ops:
            pd = []
            for d in o["deps"]:
                p = ops[d]
                if p["eng"] == "pe" and o["eng"] == "pe" and not p["dma"] and not o["dma"]:
                    continue
                pd.append(p)
                p["need_inc"] = True
            o["pdeps"] = pd
        for o in ops:
            if o["dma"]:
                if o["eng"] == "pool":
                    s = self.dma_rr_sw % self.n_sw
                    self.dma_rr_sw += 1
                else:
                    s = self.n_sw + self.dma_rr % (len(self.dma_sems) - self.n_sw)
                    self.dma_rr += 1
                o["prev_val"] = self.dma_cnt[s]
                self.dma_cnt[s] += 16
                o["sem"] = self.dma_sems[s]
                o["sem_key"] = "dma%d" % s
                o["val"] = self.dma_cnt[s]
            else:
                if o["need_inc"]:
                    self.cnt[o["eng"]] += 1
                o["sem"] = self.sem[o["eng"]]
                o["sem_key"] = o["eng"]
                o["val"] = self.cnt[o["eng"]] if o["need_inc"] else None
        final = dict(self.cnt)
        final_dma = list(self.dma_cnt)
        by_eng = {e: [o for o in ops if o["eng"] == e] for e in self.ENG}

        def emit(e, eng):
            waited = self.waited[e]
            for o in by_eng[e]:
                need = {}
                for p in o["pdeps"]:
                    k = p["sem_key"]
                    if need.get(k, (None, 0))[1] < p["val"]:
                        need[k] = (p["sem"], p["val"])
                if o["dma"]:
                    k = o["sem_key"]
                    if o["prev_val"] > 0 and need.get(k, (None, 0))[1] < o["prev_val"]:
                        need[k] = (o["sem"], o["prev_val"])
                for k, (s, v) in need.items():
                    if waited.get(k, 0) < v:
                        eng.wait_ge(s, v)
                        waited[k] = v
                ins = o["fn"](eng)
                if o["dma"]:
                    ins.then_inc(o["sem"], 16)
                elif o["need_inc"]:
                    ins.then_inc(o["sem"], 1)
            if barrier:
                for k in self.ENG:
                    if k != e and final[k] > waited.get(k, 0):
                        eng.wait_ge(self.sem[k], final[k])
                        waited[k] = final[k]
                for i, v in enumerate(final_dma):
                    k = "dma%d" % i
                    if v > waited.get(k, 0):
                        eng.wait_ge(self.dma_sems[i], v)
                        waited[k] = v

        @block.tensor
        def _(eng):
            emit("pe", eng)

        @block.vector
        def _(eng):
            emit("dve", eng)

        @block.scalar
        def _(eng):
            emit("act", eng)

        @block.gpsimd
        def _(eng):
            emit("pool", eng)

        @block.sync
        def _(eng):
            emit("sp", eng)

        self.ops = []
        self.writers = {}
        self.readers = {}


class Ctx:
    pass


def _col_load(S, nc, dst, src_vec, ncols, key):
    S.op("sp", lambda e: e.dma_start(out=dst, in_=src_vec.rearrange("(c p) -> p c", p=128)), writes=[key], dma=True)


def make_u(S, C, st):
    t0 = st * 512
    bank = C.bank()
    for dc in range(NC_):
        xs = C.XS[C.xs_i % len(C.XS)]
        xk = ("xs", C.xs_i % len(C.XS))
        C.xs_i += 1
        S.op("sp", lambda e, xs=xs, dc=dc: e.dma_start(out=xs[:], in_=C.xT[dc * 128:(dc + 1) * 128, t0:t0 + 512]),
             writes=[xk], dma=True)
        sq = C.SQ[dc % 2]
        S.op("act", lambda e, xs=xs, sq=sq: e.activation(out=sq[:], in_=xs[:], func=AF.Square),
             reads=[xk], writes=[("sq", dc % 2)])
        S.op("pe", lambda e, sq=sq, dc=dc, bank=bank: e.matmul(C.PA[bank][:, :], lhsT=C.ones_bf[:, :], rhs=sq[:],
                                                             start=(dc == 0), stop=(dc == NC_ - 1)),
             reads=[("sq", dc % 2), "consts"], writes=[("pa", bank)])
        S.op("act", lambda e, xs=xs, dc=dc: e.activation(out=C.XG[:, dc, :], in_=xs[:], func=AF.Copy,
                                                        scale=C.gpre[:, dc:dc + 1]),
             reads=[xk, "vecs"], writes=[("xg", dc)])
    S.op("act", lambda e, bank=bank: e.activation(out=C.RSTD[:], in_=C.PA[bank][:, :], func=AF.Ln,
                                                  scale=1.0 / D, bias=C.eps_norm[:, 0:1]),
         reads=[("pa", bank), "consts"], writes=["rstd"])
    S.op("act", lambda e: e.activation(out=C.RSTD[:], in_=C.RSTD[:], func=AF.Exp, scale=-0.5),
         reads=["rstd"], writes=["rstd"])


def proj(S, C, W, col0, ncols=128):
    bank = C.bank()
    for dc in range(NC_):
        S.op("pe", lambda e, dc=dc, bank=bank: e.matmul(C.PA[bank][0:ncols, :], lhsT=W[:, dc, col0:col0 + ncols],
                                                       rhs=C.XG[:, dc, :], start=(dc == 0), stop=(dc == NC_ - 1)),
             reads=[("xg", dc), "w"], writes=[("pa", bank)])
    return bank


def phase_A(nc, S, C, T, dbg=None):
    NST = T // 512
    with ExitStack() as es:
        def sb(name, shape, dt):
            return es.enter_context(nc.sbuf_tensor(name, shape, dt))

        WA = sb("WA", [128, NC_, 2176], BF16)
        LORA = sb("LORA", [128, 512], BF16)
        C.XS = [sb("xs%d" % i, [128, 512], F32) for i in range(4)]
        C.xs_i = 0
        C.SQ = [sb("sq%d" % i, [128, 512], BF16) for i in range(2)]
        C.XG = sb("XG", [128, NC_, 512], BF16)
        C.RSTD = sb("RSTD", [128, 512], F32)
        MU = sb("MU", [128, 13], F32)
        W0 = sb("W0", [128, 4], F32)
        A0 = sb("A0", [128, 4], F32)
        KK_ = sb("KKv", [128, 4], F32)
        KA = sb("KA", [128, 4], F32)
        OMKA = sb("OMKA", [128, 4], F32)
        RK = sb("RK", [128, 4], F32)
        GNG = sb("GNG", [128, 512], F32)
        GNB = sb("GNB", [128, 512], F32)
        CARRY = sb("CARRY", [128, 13], F32)
        MX = sb("MX", [128, 4, 128], BF16)
        MZ = sb("MZ", [128, 8, 64], BF16)
        I2 = sb("I2", [128, 64], F32)
        HSEL = sb("HSEL", [128, 4, 8], BF16)
        SMASK = sb("SMASK", [128, 512], F32)
        TMPM = sb("TMPM", [128, 2, 64], F32)
        TMPM2 = sb("TMPM2", [128, 2, 64], F32)
        R3 = [sb("R%d" % i, [128, 516], F32) for i in range(3)]
        DD = [sb("DD%d" % i, [128, 512], F32) for i in range(1)]
        FT = [sb("FT%d" % i, [128, 512], F32) for i in range(11)]
        TA = sb("TA", [128, 512], BF16)
        SQK = sb("SQK", [128, 512], BF16)
        AR = sb("AR", [128, 4, 2, 512], BF16)
        BT = sb("BT", [128, 4, 512], BF16)
        KT = sb("KT", [128, 4, 512], BF16)
        VV = sb("VV", [128, 4, 512], BF16)
        RKR = sb("RKR", [128, 4, 512], BF16)
        SZ = sb("SZ", [128, 4, 512], BF16)
        WCS = sb("WCS", [128, 4, 8], F32)
        NB = [sb("NB%d" % i, [128, 8, 192], BF16) for i in range(2)]
        KB = [sb("KB%d" % i, [128, 8, 192], BF16) for i in range(2)]
        N1T = [sb("N1T%d" % i, [128, 8, 64], BF16) for i in range(2)]
        NP = [sb("NP%d" % i, [128, 8, 64], BF16) for i in range(2)]
        NPT = [sb("NPT%d" % i, [128, 8, 64], BF16) for i in range(2)]
        XT = [sb("XT%d" % i, [128, 8, 128], BF16) for i in range(2)]
        VMT = [sb("VMT%d" % i, [128, 8, 64], BF16) for i in range(2)]
        GY = [sb("GY%d" % i, [128, 4, 2, 64], BF16) for i in range(2)]
        GS = [sb("GS%d" % i, [128, 4, 2, 128], BF16) for i in range(2)]
        SLBD = [sb("SLBD%d" % i, [128, 4, 2, 128], F32) for i in range(1)]
        STT = [sb("ST%d" % i, [128, 4, 128], BF16) for i in range(2)]
        STMP = sb("STMP", [128, 4, 128], F32)
        YLOC = sb("YLOC", [128, 512], F32)
        YSB = sb("YSB", [128, 512], F32)
        YN = sb("YN", [128, 512], F32)
        YBF = sb("YBF", [128, 512], BF16)
        STAT = sb("STAT", [128, 48], F32)
        BON = sb("BON", [128, 8], F32)

        with nc.Block() as block:
            for i, (a, b) in enumerate([(0, 1088), (1088, 2176)]):
                S.op("pool", lambda e, a=a, b=b: e.dma_start(
                    out=WA[:, :, a:b], in_=C.w_in.rearrange("(c p) n -> p c n", p=128)[:, :, a:b]),
                    writes=["w"], dma=True)
            S.op("pool", lambda e: e.dma_start(out=LORA[0:64, :], in_=C.w_decay_up), writes=["w"], dma=True)
            S.op("pool", lambda e: e.dma_start(out=LORA[64:128, :], in_=C.w_iclr_up), writes=["w"], dma=True)
            _col_load(S, nc, MU[:, :], C.mu_shift, 13, "vecs")
            _col_load(S, nc, W0[:, :], C.w0, 4, "vecs")
            _col_load(S, nc, A0[:, :], C.a0, 4, "vecs")
            _col_load(S, nc, KK_[:, :], C.k_k, 4, "vecs")
            _col_load(S, nc, KA[:, :], C.k_a, 4, "vecs")
            _col_load(S, nc, RK[:, :], C.r_k.rearrange("h n -> (h n)"), 4, "vecs")
            S.op("sp", lambda e: e.dma_start(out=GNG[:], in_=C.gn_gain.partition_broadcast(128)), writes=["vecs"], dma=True)
            S.op("sp", lambda e: e.dma_start(out=GNB[:], in_=C.gn_bias.partition_broadcast(128)), writes=["vecs"], dma=True)
            S.op("dve", lambda e: e.tensor_scalar(out=OMKA[:], in0=KA[:], scalar1=-1.0, scalar2=1.0, op0=ALU.mult, op1=ALU.add),
                 reads=["vecs"], writes=["vecs2"])
            S.op("pool", lambda e: e.memset(CARRY[:], 0.0), writes=["carry"])
            for i in range(3):
                S.op("pool", lambda e, i=i: e.memset(R3[i][:], 0.0), writes=[("R", i)])
            S.op("pool", lambda e: e.memset(STT[0][:], 0.0), writes=[("st", 0)])
            for i in range(2):
                S.op("pool", lambda e, i=i: e.memset(GS[i][:], 0.0), writes=[("gs", i)])
                if i == 0:
                    S.op("pool", lambda e, i=i: e.memset(SLBD[i][:], 0.0), writes=["slbd"])
            S.op("pool", lambda e: e.memset(SMASK[:], 1.0), writes=["masks"])
            S.op("pool", lambda e: e.memset(SMASK[:].rearrange("p (c t) -> p c t", t=64)[:, :, 0:1], 0.0), writes=["masks"])
            S.op("pool", lambda e: e.memset(TMPM2[:], 1.0), writes=["tmpm2"])

            def sel(cmp_, sign):
                return lambda e: e.affine_select(out=TMPM[:], in_=TMPM2[:], pattern=[[64 * sign, 2], [sign, 64]],
                                                 compare_op=cmp_, fill=0.0, base=0, channel_multiplier=-sign)

            def halves(dst_fn, bshape):
                for half in range(2):
                    ps_ = slice(half * 64, half * 64 + 64)
                    src = TMPM[ps_, half, :]
                    if bshape is not None:
                        src = src.unsqueeze(1).to_broadcast([64, bshape, 64])
                    S.op("dve", lambda e, ps_=ps_, src=src: e.tensor_copy(out=dst_fn(ps_), in_=src), reads=["tmpm"], writes=["masks"])
            S.op("pool", sel(ALU.is_gt, 1), reads=["tmpm2"], writes=["tmpm"])
            halves(lambda ps_: MX[ps_, :, 0:64], 4)
            S.op("pool", sel(ALU.is_ge, 1), reads=["tmpm2", "masks"], writes=["tmpm"])
            halves(lambda ps_: MX[ps_, :, 64:128], 4)
            S.op("pool", sel(ALU.is_gt, -1), reads=["tmpm2", "masks"], writes=["tmpm"])
            halves(lambda ps_: MZ[ps_, :, :], 8)
            S.op("pool", sel(ALU.is_equal, 1), reads=["tmpm2", "masks"], writes=["tmpm"])
            halves(lambda ps_: I2[ps_, :], None)
            S.op("pool", lambda e: e.memset(HSEL[:], 0.0), writes=["masks"])
            for hp in range(4):
                for half in range(2):
                    S.op("pool", lambda e, hp=hp, half=half: e.memset(HSEL[half * 64:half * 64 + 64, hp, 2 * hp + half:2 * hp + half + 1], 1.0),
                         writes=["masks"])

            st_slot = 0
            for st in range(NST):
                make_u(S, C, st)
                def shifted(ft, col0, out_ap, okey):
                    bank = proj(S, C, WA, col0)
                    ri = ft % 3
                    Rb = R3[ri]
                    S.op("dve", lambda e: e.tensor_tensor(out=Rb[:, 1:513], in0=C.PA[bank][:, :], in1=C.RSTD[:], op=ALU.mult),
                         reads=[("pa", bank), "rstd"], writes=[("R", ri)])
                    S.op("pool", lambda e: e.tensor_copy(out=Rb[:, 0:1], in_=CARRY[:, ft:ft + 1]), reads=["carry"], writes=[("R", ri)])
                    S.op("pool", lambda e: e.tensor_copy(out=CARRY[:, ft:ft + 1], in_=Rb[:, 512:513]), reads=[("R", ri)], writes=["carry"])
                    di = 0
                    S.op("pool", lambda e: e.tensor_tensor(out=DD[di][:], in0=Rb[:, 0:512], in1=Rb[:, 1:513], op=ALU.subtract),
                         reads=[("R", ri)], writes=[("dd", di)])
                    S.op("dve", lambda e: e.scalar_tensor_tensor(out=out_ap, in0=DD[di][:], scalar=MU[:, ft:ft + 1], in1=Rb[:, 1:513],
                                                                op0=ALU.mult, op1=ALU.add),
                         reads=[("dd", di), ("R", ri), "vecs"], writes=[okey])

                shifted(12, C_WD, FT[0][:], ("ft", 0))
                S.op("act", lambda e: e.activation(out=TA[0:64, :], in_=FT[0][0:64, :], func=AF.Tanh), reads=[("ft", 0)], writes=["ta"])
                S.op("act", lambda e: e.activation(out=TA[64:128, :], in_=FT[0][64:128, :], func=AF.Copy), reads=[("ft", 0)], writes=["ta"])
                for hp in range(4):
                    fs = slice(hp * 128, hp * 128 + 128)
                    bw = C.bank()
                    S.op("pe", lambda e, bw=bw, fs=fs: e.matmul(C.PA[bw][:, :], lhsT=LORA[0:64, fs], rhs=TA[0:64, :], start=True, stop=True),
                         reads=["ta", "w"], writes=[("pa", bw)])
                    ba = C.bank()
                    S.op("pe", lambda e, ba=ba, fs=fs: e.matmul(C.PA[ba][:, :], lhsT=LORA[64:128, fs], rhs=TA[64:128, :], start=True, stop=True),
                         reads=["ta", "w"], writes=[("pa", ba)])
                    WS, AS, CUM, CX, WT, WINV, WEX, RR, KKK, KN = [FT[i] for i in range(1, 11)]
                    T1, T2, T3 = WS, CX, CUM
                    kf = lambda i: ("ft", {11: 1, 12: 4, 13: 3}.get(i, i))
                    S.op("act", lambda e, bw=bw, hp=hp: e.activation(out=WS[:], in_=C.PA[bw][:, :], func=AF.Sigmoid, bias=W0[:, hp:hp + 1]),
                         reads=[("pa", bw), "vecs"], writes=[kf(1)])
                    S.op("act", lambda e, ba=ba, hp=hp: e.activation(out=AS[:], in_=C.PA[ba][:, :], func=AF.Sigmoid, bias=A0[:, hp:hp + 1]),
                         reads=[("pa", ba), "vecs"], writes=[kf(2)])
                    S.op("dve", lambda e: e.tensor_tensor_scan(out=CUM[:], data0=SMASK[:], data1=WS[:], initial=0.0, op0=ALU.mult, op1=ALU.add),
                         reads=[kf(1), "masks"], writes=[kf(3)])
                    S.op("pool", lambda e: e.tensor_tensor(out=CX[:], in0=CUM[:], in1=WS[:], op=ALU.subtract), reads=[kf(3), kf(1)], writes=[kf(4)])
                    S.op("act", lambda e: e.activation(out=WT[:], in_=CUM[:], func=AF.Exp, scale=-DECAY_SCALE), reads=[kf(3)], writes=[kf(5)])
                    S.op("act", lambda e: e.activation(out=WINV[:], in_=CUM[:], func=AF.Exp, scale=DECAY_SCALE), reads=[kf(3)], writes=[kf(6)])
                    S.op("act", lambda e: e.activation(out=WEX[:], in_=CX[:], func=AF.Exp, scale=-DECAY_SCALE), reads=[kf(4)], writes=[kf(7)])
                    S.op("pool", lambda e, hp=hp: e.tensor_copy(out=WCS[:, hp, :], in_=WT[:].rearrange("p (c t) -> p c t", t=64)[:, :, 63]),
                         reads=[kf(5)], writes=["wcs"])
                    shifted(hp, C_R + hp * 128, RR[:], kf(8))
                    shifted(4 + hp, C_K + hp * 128, KKK[:], kf(9))
                    shifted(8 + hp, C_V + hp * 128, VV[:, hp, :], ("vv", hp))
                    S.op("act", lambda e, hp=hp: e.activation(out=SQK[:], in_=KKK[:], func=AF.Square, scale=KK_[:, hp:hp + 1]),
                         reads=[kf(9), "vecs"], writes=["sqk"])
                    bs = C.bank()
                    S.op("pe", lambda e, bs=bs: e.matmul(C.PA[bs][:, :], lhsT=C.bones[:, :], rhs=SQK[:], start=True, stop=True),
                         reads=["sqk", "consts"], writes=[("pa", bs)])
                    S.op("dve", lambda e, bs=bs: e.tensor_scalar(out=T1[:], in0=C.PA[bs][:, :], scalar1=1e-24, scalar2=None, op0=ALU.max),
                         reads=[("pa", bs)], writes=[kf(11)])
                    S.op("act", lambda e: e.activation(out=T1[:], in_=T1[:], func=AF.Ln), reads=[kf(11)], writes=[kf(11)])
                    S.op("act", lambda e: e.activation(out=T1[:], in_=T1[:], func=AF.Exp, scale=-0.5), reads=[kf(11)], writes=[kf(11)])
                    S.op("dve", lambda e, hp=hp: e.scalar_tensor_tensor(out=KN[:], in0=KKK[:], scalar=KK_[:, hp:hp + 1], in1=T1[:],
                                                                       op0=ALU.mult, op1=ALU.mult),
                         reads=[kf(9), kf(11), "vecs"], writes=[kf(10)])
                    S.op("dve", lambda e, hp=hp: e.scalar_tensor_tensor(out=AR[:, hp, 0, :], in0=KN[:], scalar=-1.0, in1=WEX[:],
                                                                       op0=ALU.mult, op1=ALU.mult),
                         reads=[kf(10), kf(7)], writes=[("ar", hp)])
                    S.op("pool", lambda e: e.tensor_tensor(out=T2[:], in0=KN[:], in1=AS[:], op=ALU.mult), reads=[kf(10), kf(2)], writes=[kf(12)])
                    S.op("pool", lambda e, hp=hp: e.tensor_tensor(out=BT[:, hp, :], in0=T2[:], in1=WINV[:], op=ALU.mult),
                         reads=[kf(12), kf(6)], writes=[("bt", hp)])
                    S.op("pool", lambda e, hp=hp: e.tensor_scalar(out=T3[:], in0=AS[:], scalar1=KA[:, hp:hp + 1], scalar2=OMKA[:, hp:hp + 1],
                                                                 op0=ALU.mult, op1=ALU.add),
                         reads=[kf(2), "vecs", "vecs2"], writes=[kf(13)])
                    S.op("pool", lambda e: e.tensor_tensor(out=T3[:], in0=T3[:], in1=KKK[:], op=ALU.mult), reads=[kf(13), kf(9)], writes=[kf(13)])
                    S.op("pool", lambda e, hp=hp: e.tensor_tensor(out=KT[:, hp, :], in0=T3[:], in1=WINV[:], op=ALU.mult),
                         reads=[kf(13), kf(6)], writes=[("kt", hp)])
                    S.op("dve", lambda e, hp=hp: e.scalar_tensor_tensor(out=RKR[:, hp, :], in0=T3[:], scalar=RK[:, hp:hp + 1], in1=RR[:],
                                                                       op0=ALU.mult, op1=ALU.mult),
                         reads=[kf(13), kf(8), "vecs"], writes=[("rkr", hp)])
                    S.op("pool", lambda e, hp=hp: e.tensor_tensor(out=AR[:, hp, 1, :], in0=RR[:], in1=WT[:], op=ALU.mult),
                         reads=[kf(8), kf(5)], writes=[("ar", hp)])
                    bz = proj(S, C, WA, C_ZA + hp * 128)
                    S.op("dve", lambda e, bz=bz: e.tensor_tensor(out=T2[:], in0=C.PA[bz][:, :], in1=C.RSTD[:], op=ALU.mult),
                         reads=[("pa", bz), "rstd"], writes=[kf(12)])
                    S.op("act", lambda e, hp=hp: e.activation(out=SZ[:, hp, :], in_=T2[:], func=AF.Silu), reads=[kf(12)], writes=[("sz", hp)])

                for j in range(4):
                    tb = j * 128
                    tt = st * 4 + j
                    sl_ = tt % 2
                    nb, kb, xt, vmt, gy, gs, slbd = NB[sl_], KB[sl_], XT[sl_], VMT[sl_], GY[sl_], GS[sl_], SLBD[0]
                    kn = lambda name: (name, sl_)
                    for rnd, pair in enumerate([((AR, 0), (BT, None)), ((KT, None), (VV, None))]):
                        for pi, (src, sub) in enumerate(pair):
                            for hp in range(4):
                                in_ap = src[:, hp, 0, tb:tb + 128] if sub is not None else src[:, hp, tb:tb + 128]
                                rk = {id(AR): ("ar", hp), id(BT): ("bt", hp), id(KT): ("kt", hp), id(VV): ("vv", hp)}[id(src)]
                                S.op("pe", lambda e, in_ap=in_ap, pi=pi, hp=hp: e.transpose(
                                    out=C.PT[:, pi * 512 + hp * 128: pi * 512 + hp * 128 + 128], in_=in_ap, identity=C.ident[:, :]),
                                    reads=[rk, "consts"], writes=["pt"])
                        if rnd == 0:
                            S.op("act", lambda e, xt=xt: e.activation(out=xt[:, :, 0:64], in_=C.PT[:, 0:512].rearrange("p (h k) -> p h k", k=64), func=AF.Copy),
                                 reads=["pt"], writes=[kn("xt")])
                            S.op("act", lambda e, nb=nb: e.activation(out=nb[:, :, 128:192], in_=C.PT[:, 512:1024].rearrange("p (h k) -> p h k", k=64), func=AF.Copy),
                                 reads=["pt"], writes=[kn("nb")])
                        else:
                            S.op("act", lambda e, kb=kb: e.activation(out=kb[:, :, 128:192], in_=C.PT[:, 0:512].rearrange("p (h k) -> p h k", k=64), func=AF.Copy),
                                 reads=["pt"], writes=[kn("kb")])
                            S.op("act", lambda e, vmt=vmt: e.activation(out=vmt[:, :, :], in_=C.PT[:, 512:1024].rearrange("p (h k) -> p h k", k=64), func=AF.Copy),
                                 reads=["pt"], writes=[kn("vmt")])
                    def rows(h):
                        return slice((h % 2) * 64, (h % 2) * 64 + 64)
                    for par in range(2):
                        for (lsrc, dst, dkey) in ((BT, nb, "nb"), (KT, kb, "kb")):
                            bank = C.bank()
                            pv = C.PA[bank][:, :].rearrange("p (h c) -> p h c", c=128)
                            for hp in range(4):
                                h = 2 * hp + par
                                for e_ in range(2):
                                    c0 = tb + e_ * 64
                                    S.op("pe", lambda e, lsrc=lsrc, h=h, hp=hp, e_=e_, c0=c0, pv=pv: e.matmul(
                                        pv[e_ * 64:e_ * 64 + 64, hp, :], lhsT=lsrc[rows(h), hp, c0:c0 + 64],
                                        rhs=AR[rows(h), hp, :, c0:c0 + 64], start=True, stop=True),
                                        reads=[("ar", hp), ("bt", hp) if lsrc is BT else ("kt", hp)], writes=[("pa", bank)])
                            dview = dst[:].rearrange("p (hp two) c -> p hp two c", two=2)[:, :, par, 0:128]
                            S.op("dve", lambda e, dview=dview, bank=bank: e.tensor_tensor(
                                out=dview, in0=C.PA[bank][:, :].rearrange("p (h c) -> p h c", c=128), in1=MX[:], op=ALU.mult),
                                reads=[("pa", bank), "masks"], writes=[kn(dkey)])
                    n1t = N1T[sl_]
                    for par in range(2):
                        bankz = C.bank()
                        pz = C.PA[bankz][:, 0:256].rearrange("p (h c) -> p h c", c=64)
                        for hp in range(4):
                            h = 2 * hp + par
                            for e_ in range(2):
                                c0 = tb + e_ * 64
                                S.op("pe", lambda e, h=h, hp=hp, e_=e_, c0=c0, pz=pz: e.matmul(
                                    pz[e_ * 64:e_ * 64 + 64, hp, :], lhsT=AR[rows(h), hp, 0, c0:c0 + 64], rhs=BT[rows(h), hp, c0:c0 + 64],
                                    start=True, stop=True), reads=[("ar", hp), ("bt", hp)], writes=[("pa", bankz)])
                        dview = n1t[:].rearrange("p (hp two) c -> p hp two c", two=2)[:, :, par, :]
                        S.op("dve", lambda e, dview=dview, pz=pz: e.tensor_tensor(out=dview, in0=pz, in1=MZ[:, 0:4, :], op=ALU.mult),
                             reads=[("pa", bankz), "masks"], writes=[kn("n1t")])
                    bankp = C.bank()
                    pp = C.PA[bankp][:, :].rearrange("p (h c) -> p h c", c=64)
                    for h in range(8):
                        for e_ in range(2):
                            er = slice(e_ * 64, e_ * 64 + 64)
                            S.op("pe", lambda e, h=h, er=er: e.matmul(pp[er, h, :], lhsT=kb[er, h, 0:64], rhs=vmt[er, h, :], start=True, stop=True),
                                 reads=[kn("kb"), kn("vmt")], writes=[("pa", bankp)])
                    S.op("act", lambda e, xt=xt, bankp=bankp: e.activation(out=xt[:, :, 64:128], in_=C.PA[bankp][:, :].rearrange("p (h c) -> p h c", c=64), func=AF.Copy),
                         reads=[("pa", bankp)], writes=[kn("xt")])
                    ncur, ntcur = (nb, slice(0, 64)), (n1t, slice(0, 64))
                    ncur_key, ntcur_key = kn("nb"), kn("n1t")
                    for lvl in range(6):
                        for g in range(2):
                            bank = C.bank()
                            pa_ = C.PA[bank][:, :].rearrange("p (h c) -> p h c", c=128)
                            for h4 in range(4):
                                h = g * 4 + h4
                                for e_ in range(2):
                                    er = slice(e_ * 64, e_ * 64 + 64)
                                    S.op("pe", lambda e, h=h, h4=h4, er=er, pa_=pa_, ncur=ncur: e.matmul(
                                        pa_[er, h4, :], lhsT=ncur[0][er, h, ncur[1]], rhs=xt[er, h, :], start=True, stop=True),
                                        reads=[ncur_key, kn("xt")], writes=[("pa", bank)])
                            S.op("dve", lambda e, g=g, bank=bank, xt=xt: e.tensor_tensor(
                                out=xt[:, g * 4:g * 4 + 4, :], in0=C.PA[bank][:, :].rearrange("p (h c) -> p h c", c=128),
                                in1=xt[:, g * 4:g * 4 + 4, :], op=ALU.add),
                                reads=[("pa", bank), kn("xt")], writes=[kn("xt")])
                        if lvl < 5:
                            nn, nnt = NP[lvl % 2], NPT[lvl % 2]
                            nnk, nntk = ("np", lvl % 2), ("npt", lvl % 2)
                            b1 = C.bank()
                            p1 = C.PA[b1][:, :].rearrange("p (h c) -> p h c", c=64)
                            for h in range(8):
                                for e_ in range(2):
                                    er = slice(e_ * 64, e_ * 64 + 64)
                                    S.op("pe", lambda e, h=h, er=er, p1=p1, ncur=ncur, ntcur=ntcur: e.matmul(
                                        p1[er, h, :], lhsT=ntcur[0][er, h, ntcur[1]], rhs=ncur[0][er, h, ncur[1]], start=True, stop=True),
                                        reads=[ncur_key, ntcur_key], writes=[("pa", b1)])
                            S.op("act", lambda e, nn=nn, b1=b1: e.activation(out=nn[:], in_=C.PA[b1][:, :].rearrange("p (h c) -> p h c", c=64), func=AF.Copy),
                                 reads=[("pa", b1)], writes=[nnk])
                            b2 = C.bank()
                            p2 = C.PA[b2][:, :].rearrange("p (h c) -> p h c", c=64)
                            for h in range(8):
                                for e_ in range(2):
                                    er = slice(e_ * 64, e_ * 64 + 64)
                                    S.op("pe", lambda e, h=h, er=er, p2=p2, ncur=ncur, ntcur=ntcur: e.matmul(
                                        p2[er, h, :], lhsT=ncur[0][er, h, ncur[1]], rhs=ntcur[0][er, h, ntcur[1]], start=True, stop=True),
                                        reads=[ncur_key, ntcur_key], writes=[("pa", b2)])
                            S.op("act", lambda e, nnt=nnt, b2=b2: e.activation(out=nnt[:], in_=C.PA[b2][:, :].rearrange("p (h c) -> p h c", c=64), func=AF.Copy),
                                 reads=[("pa", b2)], writes=[nntk])
                            ncur, ntcur = (nn, slice(0, 64)), (nnt, slice(0, 64))
                            ncur_key, ntcur_key = nnk, nntk
                    for e_ in range(2):
                        er = slice(e_ * 64, e_ * 64 + 64)
                        c0 = tb + e_ * 64
                        bank = C.bank()
                        pg = C.PA[bank][:, :].rearrange("p (h c) -> p h c", c=128)
                        for h in range(8):
                            hp = h // 2
                            S.op("pe", lambda e, h=h, hp=hp, er=er, pg=pg: e.matmul(
                                pg[rows(h), hp, :], lhsT=xt[er, h, 0:64], rhs=nb[er, h, 64:192], start=True, stop=True),
                                reads=[kn("xt"), kn("nb")], writes=[("pa", bank)])
                        S.op("dve", lambda e, e_=e_, c0=c0, pg=pg: e.tensor_tensor(
                            out=gy[:, :, e_, :], in0=pg[:, :, 0:64], in1=AR[:, :, 1, c0:c0 + 64], op=ALU.add),
                            reads=[("pa", bank)] + [("ar", hp) for hp in range(4)], writes=[kn("gy")])
                        for par in range(2):
                            pr = slice(par * 64, par * 64 + 64)
                            S.op("dve", lambda e, e_=e_, pg=pg, pr=pr, par=par: e.tensor_tensor(
                                out=gs[pr, :, e_, par * 64:par * 64 + 64], in0=pg[pr, :, 64:128],
                                in1=I2[pr, :].unsqueeze(1).to_broadcast([64, 4, 64]), op=ALU.add),
                                reads=[("pa", bank), "masks"], writes=[kn("gs")])
                        bl = C.bank()
                        psl = C.PA[bl][:, :].rearrange("p (h c) -> p h c", c=128)
                        for h in range(8):
                            hp = h // 2
                            par = h % 2
                            S.op("pe", lambda e, h=h, hp=hp, par=par, er=er, psl=psl: e.matmul(
                                psl[rows(h), hp, par * 64:par * 64 + 64], lhsT=nb[er, h, 128:192], rhs=xt[er, h, 64:128], start=True, stop=False),
                                reads=[kn("nb"), kn("xt")], writes=[("pa", bl)])
                            S.op("pe", lambda e, h=h, hp=hp, par=par, er=er, psl=psl: e.matmul(
                                psl[rows(h), hp, par * 64:par * 64 + 64], lhsT=kb[er, h, 128:192], rhs=vmt[er, h, :], start=False, stop=True),
                                reads=[kn("kb"), kn("vmt")], writes=[("pa", bl)])
                        for par in range(2):
                            pr = slice(par * 64, par * 64 + 64)
                            S.op("act", lambda e, e_=e_, psl=psl, pr=pr, par=par: e.activation(
                                out=slbd[pr, :, e_, par * 64:par * 64 + 64], in_=psl[pr, :, par * 64:par * 64 + 64], func=AF.Copy),
                                reads=[("pa", bl)], writes=["slbd"])
                    byl = C.bank()
                    pyl = C.PA[byl][:, :].rearrange("p (h c) -> p h c", c=64)
                    for e_ in range(2):
                        er = slice(e_ * 64, e_ * 64 + 64)
                        for h in range(8):
                            S.op("pe", lambda e, h=h, er=er: e.matmul(
                                pyl[er, h, :], lhsT=nb[er, h, 64:128], rhs=xt[er, h, 64:128], start=True, stop=False),
                                reads=[kn("nb"), kn("xt")], writes=[("pa", byl)])
                            S.op("pe", lambda e, h=h, er=er: e.matmul(
                                pyl[er, h, :], lhsT=kb[er, h, 64:128], rhs=vmt[er, h, :], start=False, stop=True),
                                reads=[kn("kb"), kn("vmt")], writes=[("pa", byl)])
                    S.op("act", lambda e: e.activation(out=YLOC[:], in_=C.PA[byl][:, :], func=AF.Copy), reads=[("pa", byl)], writes=["yloc"])
                    py = C.PY[:, :].rearrange("p (h c) -> p h c", c=128)
                    for e_ in range(2):
                        er = slice(e_ * 64, e_ * 64 + 64)
                        cl = j * 2 + e_
                        stc = STT[st_slot]
                        stn = STT[1 - st_slot]
                        psv = C.PS[:, :].rearrange("p (h c) -> p h c", c=128)
                        for hp in range(4):
                            S.op("pe", lambda e, hp=hp, er=er, stc=stc, e_=e_: e.matmul(
                                py[er, hp, :], lhsT=gy[:, hp, e_, :], rhs=stc[:, hp, :], start=True, stop=True),
                                reads=[kn("gy"), ("st", st_slot)], writes=["py"])
                        for hp in range(4):
                            S.op("pe", lambda e, hp=hp, stc=stc, e_=e_: e.matmul(
                                psv[:, hp, :], lhsT=gs[:, hp, e_, :], rhs=stc[:, hp, :], start=True, stop=True),
                                reads=[kn("gs"), ("st", st_slot)], writes=["ps"])
                        S.op("dve", lambda e, e_=e_: e.tensor_tensor(out=STMP[:], in0=psv, in1=slbd[:, :, e_, :], op=ALU.add),
                             reads=["ps", "slbd"], writes=["stmp"])
                        S.op("dve", lambda e, stn=stn, cl=cl: e.tensor_tensor(
                            out=stn[:], in0=STMP[:], in1=WCS[:, :, cl:cl + 1].to_broadcast([128, 4, 128]), op=ALU.mult),
                            reads=["stmp", "wcs"], writes=[("st", 1 - st_slot)])
                        st_slot = 1 - st_slot
                    S.op("dve", lambda e: e.tensor_tensor(out=YSB[:], in0=C.PY[:, :], in1=YLOC[:], op=ALU.add), reads=["py", "yloc"], writes=["ysb"])
                    S.op("act", lambda e: e.activation(out=YN[:], in_=YSB[:], func=AF.Square), reads=["ysb"], writes=["yn"])
                    S.op("dve", lambda e: e.tensor_reduce(out=STAT[:, 0:8], in_=YSB[:].rearrange("p (h c) -> p h c", c=64), op=ALU.add, axis=AX.X),
                         reads=["ysb"], writes=["stat0"])
                    S.op("dve", lambda e: e.tensor_reduce(out=STAT[:, 8:16], in_=YN[:].rearrange("p (h c) -> p h c", c=64), op=ALU.add, axis=AX.X),
                         reads=["yn"], writes=["stat1"])
                    S.op("dve", lambda e: e.tensor_scalar(out=STAT[:, 16:24], in0=STAT[:, 0:8], scalar1=1.0 / 64, scalar2=None, op0=ALU.mult),
                         reads=["stat0"], writes=["stat2"])
                    S.op("dve", lambda e: e.tensor_tensor(out=STAT[:, 24:32], in0=STAT[:, 16:24], in1=STAT[:, 16:24], op=ALU.mult),
                         reads=["stat2"], writes=["stat3"])
                    S.op("dve", lambda e: e.scalar_tensor_tensor(out=STAT[:, 32:40], in0=STAT[:, 8:16], scalar=1.0 / 64, in1=STAT[:, 24:32],
                                                                op0=ALU.mult, op1=ALU.subtract),
                         reads=["stat1", "stat3"], writes=["stat4"])
                    S.op("act", lambda e: e.activation(out=STAT[:, 32:40], in_=STAT[:, 32:40], func=AF.Ln, bias=C.eps_gn[:, 0:1]),
                         reads=["stat4", "consts"], writes=["stat4"])
                    S.op("act", lambda e: e.activation(out=STAT[:, 32:40], in_=STAT[:, 32:40], func=AF.Exp, scale=-0.5),
                         reads=["stat4"], writes=["stat4"])
                    S.op("dve", lambda e: e.scalar_tensor_tensor(out=STAT[:, 40:48], in0=STAT[:, 16:24], scalar=-1.0, in1=STAT[:, 32:40],
                                                                op0=ALU.mult, op1=ALU.mult),
                         reads=["stat2", "stat4"], writes=["stat5"])
                    v3 = lambda t: t[:].rearrange("p (h c) -> p h c", c=64)
                    S.op("dve", lambda e: e.tensor_tensor(out=v3(YN), in0=v3(YSB), in1=STAT[:, 32:40].unsqueeze(2).to_broadcast([128, 8, 64]), op=ALU.mult),
                         reads=["ysb", "stat4"], writes=["yn"])
                    S.op("pool", lambda e: e.tensor_tensor(out=v3(YN), in0=v3(YN), in1=STAT[:, 40:48].unsqueeze(2).to_broadcast([128, 8, 64]), op=ALU.add),
                         reads=["yn", "stat5"], writes=["yn"])
                    S.op("pool", lambda e: e.tensor_tensor(out=YN[:], in0=YN[:], in1=GNG[:], op=ALU.mult), reads=["yn", "vecs"], writes=["yn"])
                    S.op("pool", lambda e: e.tensor_tensor(out=YN[:], in0=YN[:], in1=GNB[:], op=ALU.add), reads=["yn", "vecs"], writes=["yn"])
                    bb_ = C.bank()
                    for hp in range(4):
                        S.op("pe", lambda e, hp=hp, bb_=bb_: e.matmul(C.PA[bb_][:, 0:8], lhsT=RKR[:, hp, tb:tb + 128], rhs=HSEL[:, hp, :],
                                                                     start=(hp == 0), stop=(hp == 3)),
                             reads=[("rkr", hp), "masks"], writes=[("pa", bb_)])
                    S.op("act", lambda e, bb_=bb_: e.activation(out=BON[:], in_=C.PA[bb_][:, 0:8], func=AF.Copy), reads=[("pa", bb_)], writes=["bon"])
                    S.op("dve", lambda e, vmt=vmt: e.tensor_tensor(out=v3(YLOC), in0=vmt[:], in1=BON[:].unsqueeze(2).to_broadcast([128, 8, 64]), op=ALU.mult),
                         reads=[kn("vmt"), "bon"], writes=["yloc"])
                    S.op("pool", lambda e: e.tensor_tensor(out=YBF[:], in0=YN[:], in1=YLOC[:], op=ALU.add), reads=["yn", "yloc"], writes=["ybf"])
                    for hp in range(4):
                        S.op("pe", lambda e, hp=hp: e.transpose(out=C.PT[:, hp * 128:hp * 128 + 128], in_=YBF[:, hp * 128:hp * 128 + 128], identity=C.ident[:, :]),
                             reads=["ybf", "consts"], writes=["pt"])
                    t_abs = st * 512 + tb
                    S.op("dve", lambda e, t_abs=t_abs, tb=tb: e.tensor_tensor(
                        out=C.YZ[:, :, t_abs:t_abs + 128], in0=C.PT[:, 0:512].rearrange("p (h c) -> p h c", c=128),
                        in1=SZ[:, :, tb:tb + 128], op=ALU.mult),
                        reads=["pt"] + [("sz", hp) for hp in range(4)], writes=["yz"])
            if dbg is not None and "yz" in dbg:
                S.op("sp", lambda e: e.dma_start(out=dbg["yz"].rearrange("(h p) t -> p h t", p=128), in_=C.YZ[:, :, 0:T]), reads=["yz"], dma=True, force=True)
            S.finalize_and_emit(block)


def phase_B(nc, S, C, T, dbg=None):
    NST = T // 512
    NKT = T // 128
    SCALE = 1.0 / float(np.sqrt(96.0))
    TWO_PI = float(2 * np.pi)
    with ExitStack() as es:
        def sb(name, shape, dt):
            return es.enter_context(nc.sbuf_tensor(name, shape, dt))
        WB = sb("WB", [128, NC_, 960], BF16)
        WUQ = sb("WUQ", [128, 2, 768], BF16)
        WUQS = sb("WUQS", [128, 2, 8, 32], BF16)
        WUKV = sb("WUKV", [128, 8, 128], BF16)
        C.XS = [sb("xsb%d" % i, [128, 512], F32) for i in range(2)]
        C.xs_i = 0
        C.SQ = [sb("sqb%d" % i, [128, 512], BF16) for i in range(2)]
        C.XG = sb("XGb", [128, NC_, 512], BF16)
        C.RSTD = sb("RSTDb", [128, 512], F32)
        GQ = sb("GQ", [128, 2], F32)
        GKV = sb("GKV", [128, 1], F32)
        INVF = sb("INVF", [128, 1], F32)
        KT = sb("KTb", [128, 4, T], BF16)
        VT = sb("VTb", [128, NKT, 4, 65], BF16)
        QT = sb("QTb", [128, 4, 512], BF16)
        SZB = sb("SZB", [128, 4, 512], BF16)
        CQ = [sb("CQ%d" % i, [128, 512], F32) for i in range(2)]
        CKV = sb("CKV", [128, 512], F32)
        CQN = sb("CQN", [128, 2, 512], BF16)
        CKVN = sb("CKVN", [128, 512], BF16)
        RQ = sb("RQ", [128, 512], F32)
        POSI = sb("POSI", [128, 512], I32)
        ANG = sb("ANG", [128, 512], F32)
        TR1 = sb("TR1", [128, 512], F32)
        TR2 = sb("TR2", [128, 512], F32)
        TRI = sb("TRI", [128, 512], I32)
        COS = sb("COS", [128, 512], F32)
        SIN = sb("SIN", [128, 512], F32)
        CR = sb("CR", [128, 512], F32)
        SR = sb("SR", [128, 512], F32)
        KR = sb("KR", [128, 512], BF16)
        PTB = [sb("PTB%d" % i, [128, 512], BF16) for i in range(3)]
        RDEN = sb("RDEN", [128, 512], F32)
        RDENB = sb("RDENB", [128, 512], BF16)
        BCS = sb("BCS", [128, 512], F32)
        OTMP = sb("OTMP", [128, 512], F32)
        OSTG = [sb("OSTG%d" % i, [128, 512], BF16) for i in range(2)]

        with nc.Block() as block:
            w_in_v = C.w_in.rearrange("(c p) n -> p c n", p=128)
            S.op("pool", lambda e: e.dma_start(out=WB[:, :, 0:928], in_=w_in_v[:, :, C_CQ:C_CQ + 928]), writes=["w"], dma=True)
            S.op("pool", lambda e: e.dma_start(out=WB[:, :, 928:944], in_=w_in_v[:, :, C_KPE + 16:C_KPE + 32]), writes=["w"], dma=True)
            S.op("pool", lambda e: e.dma_start(out=WB[:, :, 944:960], in_=w_in_v[:, :, C_KPE:C_KPE + 16]), writes=["w"], dma=True)
            S.op("pool", lambda e: e.tensor_scalar(out=WB[:, :, 928:944], in0=WB[:, :, 928:944], scalar1=-1.0, scalar2=None, op0=ALU.mult),
                 reads=["w"], writes=["w"])
            S.op("pool", lambda e: e.dma_start(out=WUQ[:], in_=C.w_uq.rearrange("(c p) n -> p c n", p=128)), writes=["w"], dma=True)
            S.op("pool", lambda e: e.dma_start(out=WUKV[:], in_=C.w_ukv.rearrange("p (h c) -> p h c", c=128)), writes=["w"], dma=True)
            wq4 = WUQ[:].rearrange("p c (h d) -> p c h d", d=96)
            S.op("pool", lambda e: e.tensor_scalar(out=WUQS[:, :, :, 0:16], in0=wq4[:, :, :, 80:96], scalar1=-1.0, scalar2=None, op0=ALU.mult),
                 reads=["w"], writes=["w2"])
            S.op("pool", lambda e: e.tensor_copy(out=WUQS[:, :, :, 16:32], in_=wq4[:, :, :, 64:80]), reads=["w"], writes=["w2"])
            _col_load(S, nc, GQ[:, :], C.g_q, 2, "vecs")
            _col_load(S, nc, GKV[:, :], C.g_kv, 1, "vecs")
            S.op("sp", lambda e: e.dma_start(out=INVF[64:96, :], in_=C.rope_invf.rearrange("(p o) -> p o", o=1)), writes=["vecs"], dma=True)
            S.op("pool", lambda e: e.memset(VT[:, :, :, 64:65], 1.0), writes=["vt"])
            rr = slice(64, 96)

            for hg in range(2):
                for st in range(NST):
                    t0 = st * 512
                    make_u(S, C, st)
                    S.op("sp", lambda e, t0=t0: e.dma_start(out=POSI[rr, :], in_=C.positions[t0:t0 + 512].partition_broadcast(32)),
                         writes=["posi"], dma=True)
                    S.op("pool", lambda e: e.tensor_copy(out=ANG[rr, :], in_=POSI[rr, :]), reads=["posi"], writes=["ang"])
                    S.op("pool", lambda e: e.tensor_scalar(out=ANG[rr, :], in0=ANG[rr, :], scalar1=INVF[rr, 0:1], scalar2=None, op0=ALU.mult),
                         reads=["ang", "vecs"], writes=["ang"])
                    for (dst, off, key) in ((SIN, 0.0, "sin"), (COS, 0.25, "cos")):
                        S.op("pool", lambda e, off=off: e.tensor_scalar(out=TR1[rr, :], in0=ANG[rr, :], scalar1=1.0 / TWO_PI, scalar2=off,
                                                                        op0=ALU.mult, op1=ALU.add), reads=["ang"], writes=["tr1"])
                        S.op("pool", lambda e: e.tensor_copy(out=TRI[rr, :], in_=TR1[rr, :]), reads=["tr1"], writes=["tri"])
                        S.op("pool", lambda e: e.tensor_copy(out=TR2[rr, :], in_=TRI[rr, :]), reads=["tri"], writes=["tr2"])
                        S.op("pool", lambda e: e.tensor_tensor(out=TR1[rr, :], in0=TR1[rr, :], in1=TR2[rr, :], op=ALU.subtract),
                             reads=["tr1", "tr2"], writes=["tr1"])
                        S.op("pool", lambda e: e.tensor_scalar(out=TR2[rr, :], in0=TR1[rr, :], scalar1=0.5, scalar2=None, op0=ALU.is_gt),
                             reads=["tr1"], writes=["tr2"])
                        S.op("pool", lambda e: e.tensor_tensor(out=TR1[rr, :], in0=TR1[rr, :], in1=TR2[rr, :], op=ALU.subtract),
                             reads=["tr1", "tr2"], writes=["tr1"])
                        S.op("pool", lambda e: e.tensor_scalar(out=TR2[rr, :], in0=TR1[rr, :], scalar1=-0.5, scalar2=None, op0=ALU.is_lt),
                             reads=["tr1"], writes=["tr2"])
                        S.op("pool", lambda e: e.tensor_tensor(out=TR1[rr, :], in0=TR1[rr, :], in1=TR2[rr, :], op=ALU.add),
                             reads=["tr1", "tr2"], writes=["tr1"])
                        S.op("act", lambda e, dst=dst: e.activation(out=dst[rr, :], in_=TR1[rr, :], func=AF.Sin, scale=TWO_PI),
                             reads=["tr1"], writes=[key])
                    for i in range(2):
                        b = proj(S, C, WB, i * 128)
                        S.op("dve", lambda e, b=b, i=i: e.tensor_tensor(out=CQ[i][:], in0=C.PA[b][:, :], in1=C.RSTD[:], op=ALU.mult),
                             reads=[("pa", b), "rstd"], writes=[("cq", i)])
                    bss = C.bank()
                    for i in range(2):
                        S.op("act", lambda e, i=i: e.activation(out=C.SQ[i][:], in_=CQ[i][:], func=AF.Square), reads=[("cq", i)], writes=[("sq", i)])
                        S.op("pe", lambda e, i=i, bss=bss: e.matmul(C.PA[bss][:, :], lhsT=C.ones_bf[:, :], rhs=C.SQ[i][:], start=(i == 0), stop=(i == 1)),
                             reads=[("sq", i), "consts"], writes=[("pa", bss)])
                    S.op("act", lambda e, bss=bss: e.activation(out=RQ[:], in_=C.PA[bss][:, :], func=AF.Ln, scale=1.0 / 256, bias=C.eps_norm[:, 0:1]),
                         reads=[("pa", bss), "consts"], writes=["rq"])
                    S.op("act", lambda e: e.activation(out=RQ[:], in_=RQ[:], func=AF.Exp, scale=-0.5), reads=["rq"], writes=["rq"])
                    for i in range(2):
                        S.op("dve", lambda e, i=i: e.scalar_tensor_tensor(out=CQN[:, i, :], in0=CQ[i][:], scalar=GQ[:, i:i + 1], in1=RQ[:],
                                                                         op0=ALU.mult, op1=ALU.mult),
                             reads=[("cq", i), "rq", "vecs"], writes=[("cqn", i)])
                    b = proj(S, C, WB, 256)
                    S.op("dve", lambda e, b=b: e.tensor_tensor(out=CKV[:], in0=C.PA[b][:, :], in1=C.RSTD[:], op=ALU.mult),
                         reads=[("pa", b), "rstd"], writes=["ckv"])
                    bss = C.bank()
                    S.op("act", lambda e: e.activation(out=C.SQ[0][:], in_=CKV[:], func=AF.Square), reads=["ckv"], writes=[("sq", 0)])
                    S.op("pe", lambda e, bss=bss: e.matmul(C.PA[bss][:, :], lhsT=C.ones_bf[:, :], rhs=C.SQ[0][:], start=True, stop=True),
                         reads=[("sq", 0), "consts"], writes=[("pa", bss)])
                    S.op("act", lambda e, bss=bss: e.activation(out=RQ[:], in_=C.PA[bss][:, :], func=AF.Ln, scale=1.0 / 128, bias=C.eps_norm[:, 0:1]),
                         reads=[("pa", bss), "consts"], writes=["rq"])
                    S.op("act", lambda e: e.activation(out=RQ[:], in_=RQ[:], func=AF.Exp, scale=-0.5), reads=["rq"], writes=["rq"])
                    S.op("dve", lambda e: e.scalar_tensor_tensor(out=CKVN[:], in0=CKV[:], scalar=GKV[:, 0:1], in1=RQ[:], op0=ALU.mult, op1=ALU.mult),
                         reads=["ckv", "rq", "vecs"], writes=["ckvn"])
                    S.op("pool", lambda e: e.tensor_tensor(out=CR[rr, :], in0=COS[rr, :], in1=C.RSTD[rr, :], op=ALU.mult), reads=["cos", "rstd"], writes=["cr"])
                    S.op("pool", lambda e: e.tensor_tensor(out=SR[rr, :], in0=SIN[rr, :], in1=C.RSTD[rr, :], op=ALU.mult), reads=["sin", "rstd"], writes=["sr"])
                    bk = C.bank()
                    bks = C.bank()
                    for dc in range(NC_):
                        S.op("pe", lambda e, dc=dc, bk=bk: e.matmul(C.PA[bk][64:96, :], lhsT=WB[:, dc, 384:416], rhs=C.XG[:, dc, :],
                                                                   start=(dc == 0), stop=(dc == NC_ - 1)), reads=[("xg", dc), "w"], writes=[("pa", bk)])
                    for dc in range(NC_):
                        S.op("pe", lambda e, dc=dc, bks=bks: e.matmul(C.PA[bks][64:96, :], lhsT=WB[:, dc, 928:960], rhs=C.XG[:, dc, :],
                                                                     start=(dc == 0), stop=(dc == NC_ - 1)), reads=[("xg", dc), "w"], writes=[("pa", bks)])
                    S.op("dve", lambda e, bk=bk: e.tensor_tensor(out=TR1[rr, :], in0=C.PA[bk][rr, :], in1=CR[rr, :], op=ALU.mult),
                         reads=[("pa", bk), "cr"], writes=["tr1"])
                    S.op("dve", lambda e, bks=bks: e.tensor_tensor(out=TR2[rr, :], in0=C.PA[bks][rr, :], in1=SR[rr, :], op=ALU.mult),
                         reads=[("pa", bks), "sr"], writes=["tr2"])
                    S.op("pool", lambda e: e.tensor_tensor(out=KR[rr, :], in0=TR1[rr, :], in1=TR2[rr, :], op=ALU.add), reads=["tr1", "tr2"], writes=["kr"])
                    S.op("pool", lambda e: e.tensor_scalar(out=CR[rr, :], in0=COS[rr, :], scalar1=SCALE, scalar2=None, op0=ALU.mult), reads=["cos"], writes=["cr"])
                    S.op("pool", lambda e: e.tensor_scalar(out=SR[rr, :], in0=SIN[rr, :], scalar1=SCALE, scalar2=None, op0=ALU.mult), reads=["sin"], writes=["sr"])
                    for hl in range(4):
                        h = hg * 4 + hl
                        bkn = C.bank()
                        S.op("pe", lambda e, h=h, bkn=bkn: e.matmul(C.PA[bkn][0:64, :], lhsT=WUKV[:, h, 0:64], rhs=CKVN[:], start=True, stop=True),
                             reads=["ckvn", "w"], writes=[("pa", bkn)])
                        S.op("act", lambda e, hl=hl, bkn=bkn, t0=t0: e.activation(out=KT[0:64, hl, t0:t0 + 512], in_=C.PA[bkn][0:64, :], func=AF.Copy),
                             reads=[("pa", bkn)], writes=[("kt", hl)])
                        S.op("pool", lambda e, hl=hl, t0=t0: e.tensor_copy(out=KT[rr, hl, t0:t0 + 512], in_=KR[rr, :]), reads=["kr"], writes=[("kt", hl)])
                        bq = C.bank()
                        for kc in range(2):
                            S.op("pe", lambda e, kc=kc, h=h, bq=bq: e.matmul(C.PA[bq][0:96, :], lhsT=WUQ[:, kc, h * 96:h * 96 + 96], rhs=CQN[:, kc, :],
                                                                            start=(kc == 0), stop=(kc == 1)),
                                 reads=[("cqn", kc), "w"], writes=[("pa", bq)])
                        bqs = C.bank()
                        for kc in range(2):
                            S.op("pe", lambda e, kc=kc, h=h, bqs=bqs: e.matmul(C.PA[bqs][64:96, :], lhsT=WUQS[:, kc, h, :], rhs=CQN[:, kc, :],
                                                                              start=(kc == 0), stop=(kc == 1)),
                                 reads=[("cqn", kc), "w2"], writes=[("pa", bqs)])
                        S.op("act", lambda e, hl=hl, bq=bq: e.activation(out=QT[0:64, hl, :], in_=C.PA[bq][0:64, :], func=AF.Copy, scale=SCALE),
                             reads=[("pa", bq)], writes=[("qt", hl)])
                        S.op("dve", lambda e, bq=bq: e.tensor_tensor(out=TR1[rr, :], in0=C.PA[bq][rr, :], in1=CR[rr, :], op=ALU.mult),
                             reads=[("pa", bq), "cr"], writes=["tr1"])
                        S.op("dve", lambda e, bqs=bqs: e.tensor_tensor(out=TR2[rr, :], in0=C.PA[bqs][rr, :], in1=SR[rr, :], op=ALU.mult),
                             reads=[("pa", bqs), "sr"], writes=["tr2"])
                        S.op("pool", lambda e, hl=hl: e.tensor_tensor(out=QT[rr, hl, :], in0=TR1[rr, :], in1=TR2[rr, :], op=ALU.add),
                             reads=["tr1", "tr2"], writes=[("qt", hl)])
                        bz = proj(S, C, WB, 416 + h * 64, ncols=64)
                        S.op("dve", lambda e, bz=bz: e.tensor_tensor(out=OTMP[0:64, :], in0=C.PA[bz][0:64, :], in1=C.RSTD[0:64, :], op=ALU.mult),
                             reads=[("pa", bz), "rstd"], writes=["otmp"])
                        S.op("act", lambda e, hl=hl: e.activation(out=SZB[0:64, hl, :], in_=OTMP[0:64, :], func=AF.Silu), reads=["otmp"], writes=[("szb", hl)])
                    for j in range(4):
                        kt = st * 4 + j
                        bv = C.bank()
                        S.op("pe", lambda e, j=j, bv=bv: e.matmul(C.PA[bv][:, 0:256].rearrange("p (h c) -> p h c", c=64),
                                                               lhsT=CKVN[:, j * 128:(j + 1) * 128], rhs=WUKV[:, hg * 4:hg * 4 + 4, 64:128], start=True, stop=True),
                             reads=["ckvn", "w"], writes=[("pa", bv)])
                        S.op("act", lambda e, kt=kt, bv=bv: e.activation(out=VT[:, kt, :, 0:64], in_=C.PA[bv][:, 0:256].rearrange("p (h c) -> p h c", c=64), func=AF.Copy),
                             reads=[("pa", bv)], writes=["vt"])
                    for hl in range(4):
                        h = hg * 4 + hl
                        hp, par = h // 2, h % 2
                        oacc, okey = (C.PY, "py") if hl % 2 == 0 else (C.PS, "ps")
                        nkt = st * 4 + 4
                        for kt in range(nkt):
                            bs_ = C.bank()
                            S.op("pe", lambda e, hl=hl, kt=kt, bs_=bs_: e.matmul(C.PA[bs_][:, :], lhsT=KT[0:96, hl, kt * 128:(kt + 1) * 128], rhs=QT[0:96, hl, :],
                                                                               start=True, stop=True),
                                 reads=[("kt", hl), ("qt", hl)], writes=[("pa", bs_)])
                            pi = C.ptb_i % 3
                            C.ptb_i += 1
                            ptb = PTB[pi]
                            S.op("act", lambda e, bs_=bs_, ptb=ptb: e.activation(out=ptb[:], in_=C.PA[bs_][:, :], func=AF.Exp),
                                 reads=[("pa", bs_)], writes=[("ptb", pi)])
                            d = kt - st * 4
                            if d >= 0:
                                S.op("pool", lambda e, ptb=ptb, d=d: e.affine_select(out=ptb[:], in_=ptb[:], pattern=[[1, 512]], compare_op=ALU.is_ge,
                                                                                    fill=0.0, base=-128 * d, channel_multiplier=-1),
                                     reads=[("ptb", pi)], writes=[("ptb", pi)])
                            S.op("pe", lambda e, hl=hl, kt=kt, ptb=ptb, oacc=oacc, nkt=nkt: e.matmul(oacc[0:65, :], lhsT=VT[:, kt, hl, :], rhs=ptb[:],
                                                                                              start=(kt == 0), stop=(kt == nkt - 1)),
                                 reads=["vt", ("ptb", pi)], writes=[okey])
                        S.op("act", lambda e, oacc=oacc: e.activation(out=RDEN[64:65, :], in_=oacc[64:65, :], func=AF.Ln), reads=[okey], writes=["rden"])
                        S.op("act", lambda e: e.activation(out=RDENB[64:65, :], in_=RDEN[64:65, :], func=AF.Exp, scale=-1.0), reads=["rden"], writes=["rdenb"])
                        bb_ = C.bank()
                        S.op("pe", lambda e, bb_=bb_: e.matmul(C.PA[bb_][0:64, :], lhsT=C.ones_bf[64:65, 0:64], rhs=RDENB[64:65, :], start=True, stop=True),
                             reads=["rdenb", "consts"], writes=[("pa", bb_)])
                        S.op("act", lambda e, bb_=bb_: e.activation(out=BCS[0:64, :], in_=C.PA[bb_][0:64, :], func=AF.Copy), reads=[("pa", bb_)], writes=["bcs"])
                        S.op("dve", lambda e, oacc=oacc: e.tensor_tensor(out=OTMP[0:64, :], in0=oacc[0:64, :], in1=BCS[0:64, :], op=ALU.mult),
                             reads=[okey, "bcs"], writes=["otmp"])
                        if par == 0:
                            S.op("pool", lambda e, hl=hl, hp=hp, t0=t0: e.tensor_tensor(out=C.OG[0:64, hp, t0:t0 + 512], in0=OTMP[0:64, :], in1=SZB[0:64, hl, :], op=ALU.mult),
                                 reads=["otmp", ("szb", hl)], writes=["og"])
                        else:
                            oi = C.ostg_i % 2
                            C.ostg_i += 1
                            S.op("pool", lambda e, hl=hl, oi=oi: e.tensor_tensor(out=OSTG[oi][0:64, :], in0=OTMP[0:64, :], in1=SZB[0:64, hl, :], op=ALU.mult),
                                 reads=["otmp", ("szb", hl)], writes=[("ostg", oi)])
                            S.op("sp", lambda e, oi=oi, hp=hp, t0=t0: e.dma_start(out=C.OG[64:128, hp, t0:t0 + 512], in_=OSTG[oi][0:64, :]),
                                 reads=[("ostg", oi)], writes=["og"], dma=True)
            if dbg is not None and "og" in dbg:
                S.op("sp", lambda e: e.dma_start(out=dbg["og"].rearrange("(h p) t -> p h t", p=128), in_=C.OG[:, :, 0:T]), reads=["og"], dma=True, force=True)
            S.finalize_and_emit(block)


def phase_C(nc, S, C, T, dbg=None):
    NST = T // 512
    with ExitStack() as es:
        def sb(name, shape, dt):
            return es.enter_context(nc.sbuf_tensor(name, shape, dt))
        WG = sb("WG", [128, NC_, 2048], BF16)
        WOA = sb("WOA", [128, 4, D], BF16)
        WOB = sb("WOB", [128, 4, D], BF16)
        WO = sb("WO", [128, NC_, D], BF16)
        C.XS = [sb("xsc%d" % i, [128, 512], F32) for i in range(3)]
        C.xs_i = 0
        C.SQ = [sb("sqc%d" % i, [128, 512], BF16) for i in range(2)]
        C.XG = sb("XGc", [128, NC_, 512], BF16)
        C.RSTD = sb("RSTDc", [128, 512], F32)
        BG = sb("BG", [128, 16], F32)
        GPB = sb("GPB", [128, D], F32)
        GA = [sb("GA%d" % i, [128, 512], F32) for i in range(2)]
        GB = [sb("GB%d" % i, [128, 512], F32) for i in range(2)]
        MT1 = [sb("MT1_%d" % i, [128, 512], F32) for i in range(2)]
        MT2 = [sb("MT2_%d" % i, [128, 512], F32) for i in range(2)]
        MER = sb("MER", [128, NC_, 512], BF16)
        XTOK = [sb("XTOK%d" % i, [128, D], F32) for i in range(2)]
        OTOK = [sb("OTOK%d" % i, [128, D], F32) for i in range(2)]
        SCR = sb("SCR", [128, 512], F32)
        ST2 = sb("ST2", [128, 8], F32)

        with nc.Block() as block:
            w_in_v = C.w_in.rearrange("(c p) n -> p c n", p=128)
            for a in range(0, 2048, 1024):
                S.op("pool", lambda e, a=a: e.dma_start(out=WG[:, :, a:a + 1024], in_=w_in_v[:, :, C_GATE + a:C_GATE + a + 1024]), writes=["w"], dma=True)
            S.op("pool", lambda e: e.dma_start(out=WOA[:], in_=C.w_out_a.rearrange("(c p) n -> p c n", p=128)), writes=["w"], dma=True)
            S.op("pool", lambda e: e.dma_start(out=WOB[:], in_=C.w_out_b.rearrange("(c p) n -> p c n", p=128)), writes=["w"], dma=True)
            S.op("pool", lambda e: e.dma_start(out=WO[:], in_=C.w_o.rearrange("(c p) n -> p c n", p=128)), writes=["w"], dma=True)
            _col_load(S, nc, BG[:, :], C.b_gate, 16, "vecs")
            S.op("sp", lambda e: e.dma_start(out=GPB[:], in_=C.g_post.partition_broadcast(128)), writes=["vecs"], dma=True)
            for st in range(NST):
                t0 = st * 512
                make_u(S, C, st)
                for ft in range(8):
                    i2 = ft % 2
                    bga = proj(S, C, WG, ft * 128)
                    S.op("dve", lambda e, bga=bga, i2=i2: e.tensor_tensor(out=GA[i2][:], in0=C.PA[bga][:, :], in1=C.RSTD[:], op=ALU.mult),
                         reads=[("pa", bga), "rstd"], writes=[("ga", i2)])
                    S.op("act", lambda e, i2=i2, ft=ft: e.activation(out=GA[i2][:], in_=GA[i2][:], func=AF.Sigmoid, bias=BG[:, ft:ft + 1]),
                         reads=[("ga", i2), "vecs"], writes=[("ga", i2)])
                    bgb = proj(S, C, WG, 1024 + ft * 128)
                    S.op("dve", lambda e, bgb=bgb, i2=i2: e.tensor_tensor(out=GB[i2][:], in0=C.PA[bgb][:, :], in1=C.RSTD[:], op=ALU.mult),
                         reads=[("pa", bgb), "rstd"], writes=[("gb", i2)])
                    S.op("act", lambda e, i2=i2, ft=ft: e.activation(out=GB[i2][:], in_=GB[i2][:], func=AF.Sigmoid, bias=BG[:, 8 + ft:9 + ft]),
                         reads=[("gb", i2), "vecs"], writes=[("gb", i2)])
                    bya = C.bank()
                    for kc in range(4):
                        S.op("pe", lambda e, kc=kc, ft=ft, bya=bya, t0=t0: e.matmul(C.PA[bya][:, :], lhsT=WOA[:, kc, ft * 128:(ft + 1) * 128],
                                                                                rhs=C.YZ[:, kc, t0:t0 + 512], start=(kc == 0), stop=(kc == 3)),
                             reads=["yz", "w"], writes=[("pa", bya)])
                    S.op("dve", lambda e, bya=bya, i2=i2: e.tensor_tensor(out=MT1[i2][:], in0=C.PA[bya][:, :], in1=GA[i2][:], op=ALU.mult),
                         reads=[("pa", bya), ("ga", i2)], writes=[("mt1", i2)])
                    byb = C.bank()
                    for kc in range(4):
                        S.op("pe", lambda e, kc=kc, ft=ft, byb=byb, t0=t0: e.matmul(C.PA[byb][:, :], lhsT=WOB[:, kc, ft * 128:(ft + 1) * 128],
                                                                                rhs=C.OG[:, kc, t0:t0 + 512], start=(kc == 0), stop=(kc == 3)),
                             reads=["og", "w"], writes=[("pa", byb)])
                    S.op("dve", lambda e, byb=byb, i2=i2: e.tensor_tensor(out=MT2[i2][:], in0=C.PA[byb][:, :], in1=GB[i2][:], op=ALU.mult),
                         reads=[("pa", byb), ("gb", i2)], writes=[("mt2", i2)])
                    S.op("pool", lambda e, i2=i2, ft=ft: e.tensor_tensor(out=MER[:, ft, :], in0=MT1[i2][:], in1=MT2[i2][:], op=ALU.add),
                         reads=[("mt1", i2), ("mt2", i2)], writes=[("mer", ft)])
                for j in range(4):
                    tb = j * 128
                    ta = t0 + tb
                    xi = (st * 4 + j) % 2
                    S.op("sp", lambda e, xi=xi, ta=ta: e.dma_start(out=XTOK[xi][:], in_=C.x[ta:ta + 128, :]), writes=[("xtok", xi)], dma=True)
                    banks = []
                    for half in range(2):
                        bo = C.PY if half == 0 else C.PS
                        bkey = "py" if half == 0 else "ps"
                        for ft in range(8):
                            S.op("pe", lambda e, ft=ft, half=half, bo=bo, tb=tb: e.matmul(bo[:, :], lhsT=MER[:, ft, tb:tb + 128], rhs=WO[:, ft, half * 512:(half + 1) * 512],
                                                                                     start=(ft == 0), stop=(ft == 7)),
                                 reads=[("mer", ft), "w"], writes=[bkey])
                        S.op("act", lambda e, half=half, bo=bo: e.activation(out=SCR[:], in_=bo[:, :], func=AF.Square, accum_out=ST2[:, half:half + 1]),
                             reads=[bkey], writes=["scr", ("st2", half)])
                    S.op("dve", lambda e: e.tensor_tensor(out=ST2[:, 2:3], in0=ST2[:, 0:1], in1=ST2[:, 1:2], op=ALU.add),
                         reads=[("st2", 0), ("st2", 1)], writes=[("st2", 2)])
                    S.op("act", lambda e: e.activation(out=ST2[:, 3:4], in_=ST2[:, 2:3], func=AF.Ln, scale=1.0 / D, bias=C.eps_norm[:, 0:1]),
                         reads=[("st2", 2), "consts"], writes=[("st2", 3)])
                    S.op("act", lambda e: e.activation(out=ST2[:, 4:5], in_=ST2[:, 3:4], func=AF.Exp, scale=-0.5), reads=[("st2", 3)], writes=[("st2", 4)])
                    for half in range(2):
                        bo = C.PY if half == 0 else C.PS
                        bkey = "py" if half == 0 else "ps"
                        hs = slice(half * 512, (half + 1) * 512)
                        S.op("dve", lambda e, bo=bo, hs=hs, xi=xi: e.scalar_tensor_tensor(out=OTOK[xi][:, hs], in0=bo[:, :], scalar=ST2[:, 4:5], in1=GPB[:, hs],
                                                                                     op0=ALU.mult, op1=ALU.mult),
                             reads=[bkey, ("st2", 4), "vecs"], writes=[("otok", xi, half)])
                        S.op("pool", lambda e, hs=hs, xi=xi: e.tensor_tensor(out=OTOK[xi][:, hs], in0=OTOK[xi][:, hs], in1=XTOK[xi][:, hs], op=ALU.add),
                             reads=[("otok", xi, half), ("xtok", xi)], writes=[("otok", xi, half)])
                    S.op("sp", lambda e, xi=xi, ta=ta: e.dma_start(out=C.out[ta:ta + 128, :], in_=OTOK[xi][:]),
                         reads=[("otok", xi, 0), ("otok", xi, 1)], writes=["outdram"], dma=True)
            S.finalize_and_emit(block)


def build(T=4096, phases="ABC", debug=False, max_ops=None):
    nc = bass.Bass("TRN2", target_bir_lowering=False)
    Sched.max_ops = max_ops
    C = Ctx()
    dt = lambda name, shape, dtp=F32: nc.dram_tensor(name, shape, dtp, kind="ExternalInput").ap()
    C.xT = dt("xT", [D, T])
    C.x = dt("x", [T, D])
    C.positions = dt("positions", [T], I32)
    C.g_pre = dt("g_pre", [D])
    C.w_in = dt("w_in", [D, IN_COLS])
    C.b_gate = dt("b_gate", [2 * D])
    C.mu_shift = dt("mu_shift", [1664])
    C.w0 = dt("w0", [512])
    C.w_decay_up = dt("w_decay_up", [64, 512])
    C.a0 = dt("a0", [512])
    C.w_iclr_up = dt("w_iclr_up", [64, 512])
    C.k_k = dt("k_k", [512])
    C.k_a = dt("k_a", [512])
    C.r_k = dt("r_k", [8, 64])
    C.gn_gain = dt("gn_gain", [512])
    C.gn_bias = dt("gn_bias", [512])
    C.w_out_a = dt("w_out_a", [512, D])
    C.g_q = dt("g_q", [256])
    C.w_uq = dt("w_uq", [256, 768])
    C.g_kv = dt("g_kv", [128])
    C.w_ukv = dt("w_ukv", [128, 1024])
    C.w_out_b = dt("w_out_b", [512, D])
    C.w_o = dt("w_o", [D, D])
    C.g_post = dt("g_post", [D])
    C.rope_invf = dt("rope_invf", [32])
    C.out = nc.dram_tensor("out", [T, D], F32, kind="ExternalOutput").ap()
    dbg = None
    if debug:
        dbg = {"yz": nc.dram_tensor("dbg_yz", [512, T], BF16, kind="ExternalOutput").ap(),
               "og": nc.dram_tensor("dbg_og", [512, T], BF16, kind="ExternalOutput").ap()}

    with ExitStack() as es:
        S = Sched(nc, es)
        sb = lambda name, shape, dtp: es.enter_context(nc.sbuf_tensor(name, shape, dtp))
        C.ident = sb("ident", [128, 128], BF16)
        C.identf = sb("identf", [128, 128], F32)
        C.bones = sb("bones", [128, 128], BF16)
        C.ones_bf = sb("ones_bf", [128, 128], BF16)
        C.eps_norm = sb("eps_norm", [128, 1], F32)
        C.eps_gn = sb("eps_gn", [128, 1], F32)
        C.gpre = sb("gpre", [128, NC_], F32)
        C.YZ = sb("YZ", [128, 4, T], BF16)
        C.ptb_i = 0
        C.ostg_i = 0
        C.PA = [es.enter_context(nc.psum_tensor("PA%d" % i, [128, 512], F32)) for i in range(5)]
        C.PT = es.enter_context(nc.psum_tensor("PT", [128, 1024], BF16))
        C.PY = es.enter_context(nc.psum_tensor("PY", [128, 512], F32))
        C.PS = es.enter_context(nc.psum_tensor("PS", [128, 512], F32))
        C.bank_i = 0

        def bank():
            b = C.bank_i % len(C.PA)
            C.bank_i += 1
            return b
        C.bank = bank

        with nc.allow_non_contiguous_dma(reason="small per-feature vectors"):
            with nc.Block() as block:
                S.op("pool", lambda e: e.memset(C.identf[:], 0.0), writes=["identf"])
                S.op("pool", lambda e: e.affine_select(out=C.identf[:], in_=C.identf[:], pattern=[[-1, 128]],
                                                       compare_op=ALU.not_equal, fill=1.0, base=0, channel_multiplier=1),
                     reads=["identf"], writes=["identf"])
                S.op("dve", lambda e: e.tensor_copy(out=C.ident[:], in_=C.identf[:]), reads=["identf"], writes=["consts"])
                S.op("pool", lambda e: e.memset(C.ones_bf[:], 1.0), writes=["consts"])
                S.op("pool", lambda e: e.memset(C.bones[:], 0.0), writes=["consts"])
                S.op("pool", lambda e: e.memset(C.bones[0:64, 0:64], 1.0), writes=["consts"])
                S.op("pool", lambda e: e.memset(C.bones[64:128, 64:128], 1.0), writes=["consts"])
                S.op("pool", lambda e: e.memset(C.eps_norm[:], NORM_EPS), writes=["consts"])
                S.op("pool", lambda e: e.memset(C.eps_gn[:], GN_EPS), writes=["consts"])
                _col_load(S, nc, C.gpre[:, :], C.g_pre, NC_, "vecs")
                S.finalize_and_emit(block)
            if "A" in phases:
                phase_A(nc, S, C, T, dbg)
            C.OG = sb("OG", [128, 4, T], BF16)
            if "B" in phases:
                phase_B(nc, S, C, T, dbg)
            if "C" in phases:
                phase_C(nc, S, C, T, dbg)
    C.S = S
    return nc


def rope_invf():
    inv = (np.float32(ROPE_THETA) ** (-np.arange(0, 32, 2, dtype=np.float32) / np.float32(32))).astype(np.float32)
    return np.concatenate([inv, inv]).astype(np.float32)


_CACHE = {}


def kernel(**inputs):
    x = np.ascontiguousarray(np.asarray(inputs["x"], dtype=np.float32))
    B, T, _ = x.shape
    if "nc" not in _CACHE:
        _CACHE["nc"] = build(T=T, phases="ABC", debug=False)
    nc = _CACHE["nc"]
    pos = np.ascontiguousarray(np.asarray(inputs["positions"]).astype(np.int32))
    wnames = ["g_pre", "w_in", "b_gate", "mu_shift", "w0", "w_decay_up", "a0", "w_iclr_up", "k_k", "k_a", "r_k",
              "gn_gain", "gn_bias", "w_out_a", "g_q", "w_uq", "g_kv", "w_ukv", "w_out_b", "w_o", "g_post"]
    shared = {k: np.ascontiguousarray(np.asarray(inputs[k], dtype=np.float32)) for k in wnames}
    shared["rope_invf"] = rope_invf()
    in_maps = []
    for b in range(B):
        m = dict(shared)
        m["x"] = x[b]
        m["xT"] = np.ascontiguousarray(x[b].T)
        m["positions"] = pos[b]
        in_maps.append(m)
    res = run_bass_kernel_spmd(nc, in_maps, core_ids=list(range(B)))
    return np.stack([np.asarray(r["out"], dtype=np.float32) for r in res.results], axis=0)
```

```python
import numpy as np
from contextlib import ExitStack
import concourse.bass as bass
import concourse.mybir as mybir
from concourse.bass_utils import run_bass_kernel_spmd

F32 = mybir.dt.float32
BF16 = mybir.dt.bfloat16
I32 = mybir.dt.int32
AF = mybir.ActivationFunctionType
ALU = mybir.AluOpType
AX = mybir.AxisListType

D = 1024
NC_ = 8
HEADS = 8
DECAY_SCALE = 0.6065306597
GN_EPS = 64e-5
NORM_EPS = 1e-6
ROPE_THETA = 10000.0
IN_COLS = 5152
C_R, C_K, C_V, C_WD, C_AD = 0, 512, 1024, 1536, 1600
C_ZA = 1664
C_CQ = 2176
C_CKV = 2432
C_KPE = 2560
C_ZB = 2592
C_GATE = 3104


class _Rec:
    def __getattr__(self, name):
        def f(*a, **k):
            self.__dict__["call"] = (name, a, k)
            return self
        return f


class Sched:
    ENG = ["pe", "dve", "act", "pool", "sp"]

    def __init__(self, nc, es, n_dma_sems=16):
        self.nc = nc
        self.sem = {e: es.enter_context(nc.semaphore("s_" + e)) for e in self.ENG}
        self.cnt = {e: 0 for e in self.ENG}
        self.dma_sems = [es.enter_context(nc.semaphore("s_dma%d" % i)) for i in range(n_dma_sems)]
        self.dma_cnt = [0] * n_dma_sems
        self.dma_rr = 0
        self.n_sw = 4
        self.dma_rr_sw = 0
        self.waited = {e: {} for e in self.ENG}
        self.ops = []
        self.writers = {}
        self.readers = {}
        self.nops = 0

    max_ops = None
    log = None
    defer = None
    deferred = None

    def op(self, eng, fn, reads=(), writes=(), dma=False, force=False):
        if self.max_ops is not None and len(self.ops) >= self.max_ops and not force:
            return None
        if self.defer is not None:
            rec = _Rec()
            fn(rec)
            self.defer.append((eng, rec.call, tuple(reads), tuple(writes), dma))
            return None
        return self._op(eng, fn, reads, writes, dma)

    def flush(self, n=None):
        lst = self.deferred
        k = len(lst) if n is None else min(n, len(lst))
        for _ in range(k):
            eng, call, reads, writes, dma = lst.pop(0)
            self._op(eng, None, reads, writes, dma, call=call)
        return len(lst)

    def _op(self, eng, fn, reads=(), writes=(), dma=False, call=None):
        idx = len(self.ops)
        deps = set()
        for k in reads:
            deps.update(self.writers.get(k, ()))
        for k in writes:
            deps.update(self.readers.get(k, ()))
            deps.update(self.writers.get(k, ()))
        for k in writes:
            if self.readers.get(k):
                self.writers[k] = [idx]
                self.readers[k] = []
            else:
                lst = self.writers.setdefault(k, [])
                lst.append(idx)
                if len(lst) > 40:
                    del lst[0]
        for k in reads:
            lst = self.readers.setdefault(k, [])
            lst.append(idx)
            if len(lst) > 40:
                del lst[0]
        if call is None:
            rec = _Rec()
            fn(rec)
            call = rec.call
        fn2 = lambda e_, call=call: getattr(e_, call[0])(*call[1], **call[2])
        self.ops.append(dict(eng=eng, fn=fn2, dma=dma, deps=deps, need_inc=False, idx=idx, name=call[0]))
        if self.log is not None:
            self.log.append((idx, eng, call[0], [k for k in writes]))
        return idx

    def finalize_and_emit(self, block, barrier=True):
        ops = self.ops
        self.nops += len(ops)
        for o in ops:
            pd = []
            for d in o["deps"]:
                p = ops[d]
                if p["eng"] == "pe" and o["eng"] == "pe" and not p["dma"] and not o["dma"]:
                    continue
                pd.append(p)
                p["need_inc"] = True
            o["pdeps"] = pd
        for o in ops:
            if o["dma"]:
                if o["eng"] == "pool":
                    s = self.dma_rr_sw % self.n_sw
                    self.dma_rr_sw += 1
                else:
                    s = self.n_sw + self.dma_rr % (len(self.dma_sems) - self.n_sw)
                    self.dma_rr += 1
                o["prev_val"] = self.dma_cnt[s]
                self.dma_cnt[s] += 16
                o["sem"] = self.dma_sems[s]
                o["sem_key"] = "dma%d" % s
                o["val"] = self.dma_cnt[s]
            else:
                if o["need_inc"]:
                    self.cnt[o["eng"]] += 1
                o["sem"] = self.sem[o["eng"]]
                o["sem_key"] = o["eng"]
                o["val"] = self.cnt[o["eng"]] if o["need_inc"] else None
        final = dict(self.cnt)
        final_dma = list(self.dma_cnt)
        by_eng = {e: [o for o in ops if o["eng"] == e] for e in self.ENG}

        def emit(e, eng):
            waited = self.waited[e]
            for o in by_eng[e]:
                need = {}
                for p in o["pdeps"]:
                    k = p["sem_key"]
                    if need.get(k, (None, 0))[1] < p["val"]:
                        need[k] = (p["sem"], p["val"])
                if o["dma"]:
                    k = o["sem_key"]
                    if o["prev_val"] > 0 and need.get(k, (None, 0))[1] < o["prev_val"]:
                        need[k] = (o["sem"], o["prev_val"])
                for k, (s, v) in need.items():
                    if waited.get(k, 0) < v:
                        eng.wait_ge(s, v)
                        waited[k] = v
                ins = o["fn"](eng)
                if o["dma"]:
                    ins.then_inc(o["sem"], 16)
                elif o["need_inc"]:
                    ins.then_inc(o["sem"], 1)
            if barrier:
                for k in self.ENG:
                    if k != e and final[k] > waited.get(k, 0):
                        eng.wait_ge(self.sem[k], final[k])
                        waited[k] = final[k]
                for i, v in enumerate(final_dma):
                    k = "dma%d" % i
                    if v > waited.get(k, 0):
                        eng.wait_ge(self.dma_sems[i], v)
                        waited[k] = v

        @block.tensor
        def _(eng):
            emit("pe", eng)

        @block.vector
        def _(eng):
            emit("dve", eng)

        @block.scalar
        def _(eng):
            emit("act", eng)

        @block.gpsimd
        def _(eng):
            emit("pool", eng)

        @block.sync
        def _(eng):
            emit("sp", eng)

        self.ops = []
        self.writers = {}
        self.readers = {}


class Ctx:
    pass


def _col_load(S, nc, dst, src_vec, ncols, key):
    S.op("sp", lambda e: e.dma_start(out=dst, in_=src_vec.rearrange("(c p) -> p c", p=128)), writes=[key], dma=True)


def make_u(S, C, st):
    t0 = st * 512
    bank = C.bank()
    for dc in range(NC_):
        xs = C.XS[C.xs_i % len(C.XS)]
        xk = ("xs", C.xs_i % len(C.XS))
        C.xs_i += 1
        S.op("sp", lambda e, xs=xs, dc=dc: e.dma_start(out=xs[:], in_=C.xT[dc * 128:(dc + 1) * 128, t0:t0 + 512]),
             writes=[xk], dma=True)
        sq = C.SQ[dc % 2]
        S.op("act", lambda e, xs=xs, sq=sq: e.activation(out=sq[:], in_=xs[:], func=AF.Square),
             reads=[xk], writes=[("sq", dc % 2)])
        S.op("pe", lambda e, sq=sq, dc=dc, bank=bank: e.matmul(C.PA[bank][:, :], lhsT=C.ones_bf[:, :], rhs=sq[:],
                                                             start=(dc == 0), stop=(dc == NC_ - 1)),
             reads=[("sq", dc % 2), "consts"], writes=[("pa", bank)])
        S.op("act", lambda e, xs=xs, dc=dc: e.activation(out=C.XG[:, dc, :], in_=xs[:], func=AF.Copy,
                                                        scale=C.gpre[:, dc:dc + 1]),
             reads=[xk, "vecs"], writes=[("xg", dc)])
    S.op("act", lambda e, bank=bank: e.activation(out=C.RSTD[:], in_=C.PA[bank][:, :], func=AF.Ln,
                                                  scale=1.0 / D, bias=C.eps_norm[:, 0:1]),
         reads=[("pa", bank), "consts"], writes=["rstd"])
    S.op("act", lambda e: e.activation(out=C.RSTD[:], in_=C.RSTD[:], func=AF.Exp, scale=-0.5),
         reads=["rstd"], writes=["rstd"])


def proj(S, C, W, col0, ncols=128):
    bank = C.bank()
    for dc in range(NC_):
        S.op("pe", lambda e, dc=dc, bank=bank: e.matmul(C.PA[bank][0:ncols, :], lhsT=W[:, dc, col0:col0 + ncols],
                                                       rhs=C.XG[:, dc, :], start=(dc == 0), stop=(dc == NC_ - 1)),
             reads=[("xg", dc), "w"], writes=[("pa", bank)])
    return bank


def phase_A(nc, S, C, T, dbg=None):
    NST = T // 512
    with ExitStack() as es:
        def sb(name, shape, dt):
            return es.enter_context(nc.sbuf_tensor(name, shape, dt))

        WA = sb("WA", [128, NC_, 2176], BF16)
        LORA = sb("LORA", [128, 512], BF16)
        C.XS = [sb("xs%d" % i, [128, 512], F32) for i in range(2)]
        C.xs_i = 0
        C.SQ = [sb("sq%d" % i, [128, 512], BF16) for i in range(2)]
        C.XG = sb("XG", [128, NC_, 512], BF16)
        C.RSTD = sb("RSTD", [128, 512], F32)
        MU = sb("MU", [128, 13], F32)
        W0 = sb("W0", [128, 4], F32)
        A0 = sb("A0", [128, 4], F32)
        KK_ = sb("KKv", [128, 4], F32)
        KA = sb("KA", [128, 4], F32)
        OMKA = sb("OMKA", [128, 4], F32)
        RK = sb("RK", [128, 4], F32)
        GNG = sb("GNG", [128, 512], F32)
        GNB = sb("GNB", [128, 512], F32)
        CARRY = sb("CARRY", [128, 13], F32)
        MX = sb("MX", [128, 4, 128], BF16)
        MZ = sb("MZ", [128, 8, 64], BF16)
        I2 = sb("I2", [128, 64], F32)
        HSEL = sb("HSEL", [128, 4, 8], BF16)
        SMASK = sb("SMASK", [128, 512], F32)
        TMPM = sb("TMPM", [128, 2, 64], F32)
        TMPM2 = sb("TMPM2", [128, 2, 64], F32)
        R3 = [sb("R%d" % i, [128, 516], F32) for i in range(2)]
        DD = [sb("DD%d" % i, [128, 512], F32) for i in range(1)]
        FT = [sb("FT%d" % i, [128, 512], F32) for i in range(11)]
        TA = sb("TA", [128, 512], BF16)
        SQK = sb("SQK", [128, 512], BF16)
        AR = sb("AR", [128, 4, 2, 512], BF16)
        BT = sb("BT", [128, 4, 512], BF16)
        KT = sb("KT", [128, 4, 512], BF16)
        VV = sb("VV", [128, 4, 512], BF16)
        RKR = sb("RKR", [128, 4, 512], BF16)
        SZ = sb("SZ", [128, 4, 512], BF16)
        WCS = sb("WCS", [128, 4, 8], F32)
        NB = [sb("NB%d" % i, [128, 8, 192], BF16) for i in range(2)]
        KB = [sb("KB%d" % i, [128, 8, 192], BF16) for i in range(2)]
        N1T = [sb("N1T%d" % i, [128, 8, 64], BF16) for i in range(2)]
        NP = [sb("NP%d" % i, [128, 8, 64], BF16) for i in range(4)]
        NPT = [sb("NPT%d" % i, [128, 8, 64], BF16) for i in range(4)]
        XT = [sb("XT%d" % i, [128, 8, 128], BF16) for i in range(2)]
        VMT = [sb("VMT%d" % i, [128, 8, 64], BF16) for i in range(2)]
        GY = [sb("GY%d" % i, [128, 4, 2, 64], BF16) for i in range(2)]
        GS = [sb("GS%d" % i, [128, 4, 2, 128], BF16) for i in range(2)]
        SLBD = [sb("SLBD%d" % i, [128, 4, 2, 128], F32) for i in range(2)]
        STT = [sb("ST%d" % i, [128, 4, 128], BF16) for i in range(2)]
        STMP = sb("STMP", [128, 4, 128], F32)
        YLOCS = [sb("YLOC%d" % i, [128, 512], F32) for i in range(2)]
        YSB = sb("YSB", [128, 512], F32)
        YN = sb("YN", [128, 512], F32)
        YBF = sb("YBF", [128, 512], BF16)
        STAT = sb("STAT", [128, 48], F32)
        BON = sb("BON", [128, 8], F32)

        C.sbuf_left_A = nc.sbuf_bytes_remaining
        with nc.Block() as block:
            for i, (a, b) in enumerate([(0, 1088), (1088, 2176)]):
                S.op("pool", lambda e, a=a, b=b: e.dma_start(
                    out=WA[:, :, a:b], in_=C.w_in.rearrange("(c p) n -> p c n", p=128)[:, :, a:b]),
                    writes=["w"], dma=True)
            S.op("pool", lambda e: e.dma_start(out=LORA[0:64, :], in_=C.w_decay_up), writes=["w"], dma=True)
            S.op("pool", lambda e: e.dma_start(out=LORA[64:128, :], in_=C.w_iclr_up), writes=["w"], dma=True)
            _col_load(S, nc, MU[:, :], C.mu_shift, 13, "vecs")
            _col_load(S, nc, W0[:, :], C.w0, 4, "vecs")
            _col_load(S, nc, A0[:, :], C.a0, 4, "vecs")
            _col_load(S, nc, KK_[:, :], C.k_k, 4, "vecs")
            _col_load(S, nc, KA[:, :], C.k_a, 4, "vecs")
            _col_load(S, nc, RK[:, :], C.r_k.rearrange("h n -> (h n)"), 4, "vecs")
            S.op("sp", lambda e: e.dma_start(out=GNG[:], in_=C.gn_gain.partition_broadcast(128)), writes=["vecs"], dma=True)
            S.op("sp", lambda e: e.dma_start(out=GNB[:], in_=C.gn_bias.partition_broadcast(128)), writes=["vecs"], dma=True)
            S.op("dve", lambda e: e.tensor_scalar(out=OMKA[:], in0=KA[:], scalar1=-1.0, scalar2=1.0, op0=ALU.mult, op1=ALU.add),
                 reads=["vecs"], writes=["vecs2"])
            S.op("pool", lambda e: e.memset(CARRY[:], 0.0), writes=["carry"])
            for i in range(2):
                S.op("pool", lambda e, i=i: e.memset(R3[i][:], 0.0), writes=[("R", i)])
            S.op("pool", lambda e: e.memset(STT[0][:], 0.0), writes=[("st", 0)])
            for i in range(2):
                S.op("pool", lambda e, i=i: e.memset(GS[i][:], 0.0), writes=[("gs", i)])
                S.op("pool", lambda e, i=i: e.memset(SLBD[i][:], 0.0), writes=[("slbd", i)])
            S.op("pool", lambda e: e.memset(SMASK[:], 1.0), writes=["masks"])
            S.op("pool", lambda e: e.memset(SMASK[:].rearrange("p (c t) -> p c t", t=64)[:, :, 0:1], 0.0), writes=["masks"])
            S.op("pool", lambda e: e.memset(TMPM2[:], 1.0), writes=["tmpm2"])

            def sel(cmp_, sign):
                return lambda e: e.affine_select(out=TMPM[:], in_=TMPM2[:], pattern=[[64 * sign, 2], [sign, 64]],
                                                 compare_op=cmp_, fill=0.0, base=0, channel_multiplier=-sign)

            def halves(dst_fn, bshape):
                for half in range(2):
                    ps_ = slice(half * 64, half * 64 + 64)
                    src = TMPM[ps_, half, :]
                    if bshape is not None:
                        src = src.unsqueeze(1).to_broadcast([64, bshape, 64])
                    S.op("dve", lambda e, ps_=ps_, src=src: e.tensor_copy(out=dst_fn(ps_), in_=src), reads=["tmpm"], writes=["masks"])
            S.op("pool", sel(ALU.is_gt, 1), reads=["tmpm2"], writes=["tmpm"])
            halves(lambda ps_: MX[ps_, :, 0:64], 4)
            S.op("pool", sel(ALU.is_ge, 1), reads=["tmpm2", "masks"], writes=["tmpm"])
            halves(lambda ps_: MX[ps_, :, 64:128], 4)
            S.op("pool", sel(ALU.is_gt, -1), reads=["tmpm2", "masks"], writes=["tmpm"])
            halves(lambda ps_: MZ[ps_, :, :], 8)
            S.op("pool", sel(ALU.is_equal, 1), reads=["tmpm2", "masks"], writes=["tmpm"])
            halves(lambda ps_: I2[ps_, :], None)
            S.op("pool", lambda e: e.memset(HSEL[:], 0.0), writes=["masks"])
            for hp in range(4):
                for half in range(2):
                    S.op("pool", lambda e, hp=hp, half=half: e.memset(HSEL[half * 64:half * 64 + 64, hp, 2 * hp + half:2 * hp + half + 1], 1.0),
                         writes=["masks"])

            st_slot = 0
            for st in range(NST):
                make_u(S, C, st)
                def shifted(ft, col0, out_ap, okey):
                    bank = proj(S, C, WA, col0)
                    ri = ft % 2
                    Rb = R3[ri]
                    S.op("dve", lambda e: e.tensor_tensor(out=Rb[:, 1:513], in0=C.PA[bank][:, :], in1=C.RSTD[:], op=ALU.mult),
                         reads=[("pa", bank), "rstd"], writes=[("R", ri)])
                    S.op("pool", lambda e: e.tensor_copy(out=Rb[:, 0:1], in_=CARRY[:, ft:ft + 1]), reads=["carry"], writes=[("R", ri)])
                    S.op("pool", lambda e: e.tensor_copy(out=CARRY[:, ft:ft + 1], in_=Rb[:, 512:513]), reads=[("R", ri)], writes=["carry"])
                    di = 0
                    S.op("pool", lambda e: e.tensor_tensor(out=DD[di][:], in0=Rb[:, 0:512], in1=Rb[:, 1:513], op=ALU.subtract),
                         reads=[("R", ri)], writes=[("dd", di)])
                    S.op("dve", lambda e: e.scalar_tensor_tensor(out=out_ap, in0=DD[di][:], scalar=MU[:, ft:ft + 1], in1=Rb[:, 1:513],
                                                                op0=ALU.mult, op1=ALU.add),
                         reads=[("dd", di), ("R", ri), "vecs"], writes=[okey])

                shifted(12, C_WD, FT[0][:], ("ft", 0))
                S.op("act", lambda e: e.activation(out=TA[0:64, :], in_=FT[0][0:64, :], func=AF.Tanh), reads=[("ft", 0)], writes=["ta"])
                S.op("act", lambda e: e.activation(out=TA[64:128, :], in_=FT[0][64:128, :], func=AF.Copy), reads=[("ft", 0)], writes=["ta"])
                for hp in range(4):
                    fs = slice(hp * 128, hp * 128 + 128)
                    bw = C.bank()
                    S.op("pe", lambda e, bw=bw, fs=fs: e.matmul(C.PA[bw][:, :], lhsT=LORA[0:64, fs], rhs=TA[0:64, :], start=True, stop=True),
                         reads=["ta", "w"], writes=[("pa", bw)])
                    ba = C.bank()
                    S.op("pe", lambda e, ba=ba, fs=fs: e.matmul(C.PA[ba][:, :], lhsT=LORA[64:128, fs], rhs=TA[64:128, :], start=True, stop=True),
                         reads=["ta", "w"], writes=[("pa", ba)])
                    WS, AS, CUM, CX, WT, WINV, WEX, RR, KKK, KN = [FT[i] for i in range(1, 11)]
                    T1, T2, T3 = WS, CX, CUM
                    kf = lambda i: ("ft", {11: 1, 12: 4, 13: 3}.get(i, i))
                    S.op("act", lambda e, bw=bw, hp=hp: e.activation(out=WS[:], in_=C.PA[bw][:, :], func=AF.Sigmoid, bias=W0[:, hp:hp + 1]),
                         reads=[("pa", bw), "vecs"], writes=[kf(1)])
                    S.op("act", lambda e, ba=ba, hp=hp: e.activation(out=AS[:], in_=C.PA[ba][:, :], func=AF.Sigmoid, bias=A0[:, hp:hp + 1]),
                         reads=[("pa", ba), "vecs"], writes=[kf(2)])
                    S.op("dve", lambda e: e.tensor_tensor_scan(out=CUM[:], data0=SMASK[:], data1=WS[:], initial=0.0, op0=ALU.mult, op1=ALU.add),
                         reads=[kf(1), "masks"], writes=[kf(3)])
                    S.op("pool", lambda e: e.tensor_tensor(out=CX[:], in0=CUM[:], in1=WS[:], op=ALU.subtract), reads=[kf(3), kf(1)], writes=[kf(4)])
                    S.op("act", lambda e: e.activation(out=WT[:], in_=CUM[:], func=AF.Exp, scale=-DECAY_SCALE), reads=[kf(3)], writes=[kf(5)])
                    S.op("act", lambda e: e.activation(out=WINV[:], in_=CUM[:], func=AF.Exp, scale=DECAY_SCALE), reads=[kf(3)], writes=[kf(6)])
                    S.op("act", lambda e: e.activation(out=WEX[:], in_=CX[:], func=AF.Exp, scale=-DECAY_SCALE), reads=[kf(4)], writes=[kf(7)])
                    S.op("pool", lambda e, hp=hp: e.tensor_copy(out=WCS[:, hp, :], in_=WT[:].rearrange("p (c t) -> p c t", t=64)[:, :, 63]),
                         reads=[kf(5)], writes=["wcs"])
                    shifted(hp, C_R + hp * 128, RR[:], kf(8))
                    shifted(4 + hp, C_K + hp * 128, KKK[:], kf(9))
                    shifted(8 + hp, C_V + hp * 128, VV[:, hp, :], ("vv", hp))
                    S.op("act", lambda e, hp=hp: e.activation(out=SQK[:], in_=KKK[:], func=AF.Square, scale=KK_[:, hp:hp + 1]),
                         reads=[kf(9), "vecs"], writes=["sqk"])
                    bs = C.bank()
                    S.op("pe", lambda e, bs=bs: e.matmul(C.PA[bs][:, :], lhsT=C.bones[:, :], rhs=SQK[:], start=True, stop=True),
                         reads=["sqk", "consts"], writes=[("pa", bs)])
                    S.op("dve", lambda e, bs=bs: e.tensor_scalar(out=T1[:], in0=C.PA[bs][:, :], scalar1=1e-24, scalar2=None, op0=ALU.max),
                         reads=[("pa", bs)], writes=[kf(11)])
                    S.op("act", lambda e: e.activation(out=T1[:], in_=T1[:], func=AF.Ln), reads=[kf(11)], writes=[kf(11)])
                    S.op("act", lambda e: e.activation(out=T1[:], in_=T1[:], func=AF.Exp, scale=-0.5), reads=[kf(11)], writes=[kf(11)])
                    S.op("dve", lambda e, hp=hp: e.scalar_tensor_tensor(out=KN[:], in0=KKK[:], scalar=KK_[:, hp:hp + 1], in1=T1[:],
                                                                       op0=ALU.mult, op1=ALU.mult),
                         reads=[kf(9), kf(11), "vecs"], writes=[kf(10)])
                    S.op("dve", lambda e, hp=hp: e.scalar_tensor_tensor(out=AR[:, hp, 0, :], in0=KN[:], scalar=-1.0, in1=WEX[:],
                                                                       op0=ALU.mult, op1=ALU.mult),
                         reads=[kf(10), kf(7)], writes=[("ar", hp)])
                    S.op("pool", lambda e: e.tensor_tensor(out=T2[:], in0=KN[:], in1=AS[:], op=ALU.mult), reads=[kf(10), kf(2)], writes=[kf(12)])
                    S.op("pool", lambda e, hp=hp: e.tensor_tensor(out=BT[:, hp, :], in0=T2[:], in1=WINV[:], op=ALU.mult),
                         reads=[kf(12), kf(6)], writes=[("bt", hp)])
                    S.op("pool", lambda e, hp=hp: e.tensor_scalar(out=T3[:], in0=AS[:], scalar1=KA[:, hp:hp + 1], scalar2=OMKA[:, hp:hp + 1],
                                                                 op0=ALU.mult, op1=ALU.add),
                         reads=[kf(2), "vecs", "vecs2"], writes=[kf(13)])
                    S.op("pool", lambda e: e.tensor_tensor(out=T3[:], in0=T3[:], in1=KKK[:], op=ALU.mult), reads=[kf(13), kf(9)], writes=[kf(13)])
                    S.op("pool", lambda e, hp=hp: e.tensor_tensor(out=KT[:, hp, :], in0=T3[:], in1=WINV[:], op=ALU.mult),
                         reads=[kf(13), kf(6)], writes=[("kt", hp)])
                    S.op("dve", lambda e, hp=hp: e.scalar_tensor_tensor(out=RKR[:, hp, :], in0=T3[:], scalar=RK[:, hp:hp + 1], in1=RR[:],
                                                                       op0=ALU.mult, op1=ALU.mult),
                         reads=[kf(13), kf(8), "vecs"], writes=[("rkr", hp)])
                    S.op("pool", lambda e, hp=hp: e.tensor_tensor(out=AR[:, hp, 1, :], in0=RR[:], in1=WT[:], op=ALU.mult),
                         reads=[kf(8), kf(5)], writes=[("ar", hp)])
                    bz = proj(S, C, WA, C_ZA + hp * 128)
                    S.op("dve", lambda e, bz=bz: e.tensor_tensor(out=T2[:], in0=C.PA[bz][:, :], in1=C.RSTD[:], op=ALU.mult),
                         reads=[("pa", bz), "rstd"], writes=[kf(12)])
                    S.op("act", lambda e, hp=hp: e.activation(out=SZ[:, hp, :], in_=T2[:], func=AF.Silu), reads=[kf(12)], writes=[("sz", hp)])

                def tile_gen(j):
                    nonlocal st_slot
                    tb = j * 128
                    tt = st * 4 + j
                    sl_ = tt % 2
                    nb, kb, xt, vmt, gy, gs, slbd = NB[sl_], KB[sl_], XT[sl_], VMT[sl_], GY[sl_], GS[sl_], SLBD[sl_]
                    kn = lambda name: (name, sl_)
                    for rnd, pair in enumerate([((AR, 0), (BT, None)), ((KT, None), (VV, None))]):
                        for pi, (src, sub) in enumerate(pair):
                            for hp in range(4):
                                in_ap = src[:, hp, 0, tb:tb + 128] if sub is not None else src[:, hp, tb:tb + 128]
                                rk = {id(AR): ("ar", hp), id(BT): ("bt", hp), id(KT): ("kt", hp), id(VV): ("vv", hp)}[id(src)]
                                S.op("pe", lambda e, in_ap=in_ap, pi=pi, hp=hp: e.transpose(
                                    out=C.PT[:, pi * 512 + hp * 128: pi * 512 + hp * 128 + 128], in_=in_ap, identity=C.ident[:, :]),
                                    reads=[rk, "consts"], writes=["pt"])
                        if rnd == 0:
                            S.op("act", lambda e, xt=xt: e.activation(out=xt[:, :, 0:64], in_=C.PT[:, 0:512].rearrange("p (h k) -> p h k", k=64), func=AF.Copy),
                                 reads=["pt"], writes=[kn("xt")])
                            S.op("act", lambda e, nb=nb: e.activation(out=nb[:, :, 128:192], in_=C.PT[:, 512:1024].rearrange("p (h k) -> p h k", k=64), func=AF.Copy),
                                 reads=["pt"], writes=[kn("nb")])
                        else:
                            S.op("act", lambda e, kb=kb: e.activation(out=kb[:, :, 128:192], in_=C.PT[:, 0:512].rearrange("p (h k) -> p h k", k=64), func=AF.Copy),
                                 reads=["pt"], writes=[kn("kb")])
                            S.op("act", lambda e, vmt=vmt: e.activation(out=vmt[:, :, :], in_=C.PT[:, 512:1024].rearrange("p (h k) -> p h k", k=64), func=AF.Copy),
                                 reads=["pt"], writes=[kn("vmt")])
                    yield
                    def rows(h):
                        return slice((h % 2) * 64, (h % 2) * 64 + 64)
                    for par in range(2):
                        for (lsrc, dst, dkey) in ((BT, nb, "nb"), (KT, kb, "kb")):
                            bank = C.bank()
                            pv = C.PA[bank][:, :].rearrange("p (h c) -> p h c", c=128)
                            for hp in range(4):
                                h = 2 * hp + par
                                for e_ in range(2):
                                    c0 = tb + e_ * 64
                                    S.op("pe", lambda e, lsrc=lsrc, h=h, hp=hp, e_=e_, c0=c0, pv=pv: e.matmul(
                                        pv[e_ * 64:e_ * 64 + 64, hp, :], lhsT=lsrc[rows(h), hp, c0:c0 + 64],
                                        rhs=AR[rows(h), hp, :, c0:c0 + 64], start=True, stop=True),
                                        reads=[("ar", hp), ("bt", hp) if lsrc is BT else ("kt", hp)], writes=[("pa", bank)])
                            dview = dst[:].rearrange("p (hp two) c -> p hp two c", two=2)[:, :, par, 0:128]
                            S.op("dve", lambda e, dview=dview, bank=bank: e.tensor_tensor(
                                out=dview, in0=C.PA[bank][:, :].rearrange("p (h c) -> p h c", c=128), in1=MX[:], op=ALU.mult),
                                reads=[("pa", bank), "masks"], writes=[kn(dkey)])
                    n1t = N1T[sl_]
                    for par in range(2):
                        bankz = C.bank()
                        pz = C.PA[bankz][:, 0:256].rearrange("p (h c) -> p h c", c=64)
                        for hp in range(4):
                            h = 2 * hp + par
                            for e_ in range(2):
                                c0 = tb + e_ * 64
                                S.op("pe", lambda e, h=h, hp=hp, e_=e_, c0=c0, pz=pz: e.matmul(
                                    pz[e_ * 64:e_ * 64 + 64, hp, :], lhsT=AR[rows(h), hp, 0, c0:c0 + 64], rhs=BT[rows(h), hp, c0:c0 + 64],
                                    start=True, stop=True), reads=[("ar", hp), ("bt", hp)], writes=[("pa", bankz)])
                        dview = n1t[:].rearrange("p (hp two) c -> p hp two c", two=2)[:, :, par, :]
                        S.op("dve", lambda e, dview=dview, pz=pz: e.tensor_tensor(out=dview, in0=pz, in1=MZ[:, 0:4, :], op=ALU.mult),
                             reads=[("pa", bankz), "masks"], writes=[kn("n1t")])
                    yield
                    bankp = C.bank()
                    pp = C.PA[bankp][:, :].rearrange("p (h c) -> p h c", c=64)
                    for h in range(8):
                        for e_ in range(2):
                            er = slice(e_ * 64, e_ * 64 + 64)
                            S.op("pe", lambda e, h=h, er=er: e.matmul(pp[er, h, :], lhsT=kb[er, h, 0:64], rhs=vmt[er, h, :], start=True, stop=True),
                                 reads=[kn("kb"), kn("vmt")], writes=[("pa", bankp)])
                    S.op("act", lambda e, xt=xt, bankp=bankp: e.activation(out=xt[:, :, 64:128], in_=C.PA[bankp][:, :].rearrange("p (h c) -> p h c", c=64), func=AF.Copy),
                         reads=[("pa", bankp)], writes=[kn("xt")])
                    yield
                    ncur, ntcur = (nb, slice(0, 64)), (n1t, slice(0, 64))
                    ncur_key, ntcur_key = kn("nb"), kn("n1t")
                    for lvl in range(6):
                        for g in range(2):
                            bank = C.bank()
                            pa_ = C.PA[bank][:, :].rearrange("p (h c) -> p h c", c=128)
                            for h4 in range(4):
                                h = g * 4 + h4
                                for e_ in range(2):
                                    er = slice(e_ * 64, e_ * 64 + 64)
                                    S.op("pe", lambda e, h=h, h4=h4, er=er, pa_=pa_, ncur=ncur: e.matmul(
                                        pa_[er, h4, :], lhsT=ncur[0][er, h, ncur[1]], rhs=xt[er, h, :], start=True, stop=True),
                                        reads=[ncur_key, kn("xt")], writes=[("pa", bank)])
                            S.op("dve", lambda e, g=g, bank=bank, xt=xt: e.tensor_tensor(
                                out=xt[:, g * 4:g * 4 + 4, :], in0=C.PA[bank][:, :].rearrange("p (h c) -> p h c", c=128),
                                in1=xt[:, g * 4:g * 4 + 4, :], op=ALU.add),
                                reads=[("pa", bank), kn("xt")], writes=[kn("xt")])
                        if lvl < 5:
                            nn, nnt = NP[sl_ * 2 + lvl % 2], NPT[sl_ * 2 + lvl % 2]
                            nnk, nntk = ("np", sl_ * 2 + lvl % 2), ("npt", sl_ * 2 + lvl % 2)
                            b1 = C.bank()
                            p1 = C.PA[b1][:, :].rearrange("p (h c) -> p h c", c=64)
                            for h in range(8):
                                for e_ in range(2):
                                    er = slice(e_ * 64, e_ * 64 + 64)
                                    S.op("pe", lambda e, h=h, er=er, p1=p1, ncur=ncur, ntcur=ntcur: e.matmul(
                                        p1[er, h, :], lhsT=ntcur[0][er, h, ntcur[1]], rhs=ncur[0][er, h, ncur[1]], start=True, stop=True),
                                        reads=[ncur_key, ntcur_key], writes=[("pa", b1)])
                            S.op("act", lambda e, nn=nn, b1=b1: e.activation(out=nn[:], in_=C.PA[b1][:, :].rearrange("p (h c) -> p h c", c=64), func=AF.Copy),
                                 reads=[("pa", b1)], writes=[nnk])
                            b2 = C.bank()
                            p2 = C.PA[b2][:, :].rearrange("p (h c) -> p h c", c=64)
                            for h in range(8):
                                for e_ in range(2):
                                    er = slice(e_ * 64, e_ * 64 + 64)
                                    S.op("pe", lambda e, h=h, er=er, p2=p2, ncur=ncur, ntcur=ntcur: e.matmul(
                                        p2[er, h, :], lhsT=ncur[0][er, h, ncur[1]], rhs=ntcur[0][er, h, ntcur[1]], start=True, stop=True),
                                        reads=[ncur_key, ntcur_key], writes=[("pa", b2)])
                            S.op("act", lambda e, nnt=nnt, b2=b2: e.activation(out=nnt[:], in_=C.PA[b2][:, :].rearrange("p (h c) -> p h c", c=64), func=AF.Copy),
                                 reads=[("pa", b2)], writes=[nntk])
                            ncur, ntcur = (nn, slice(0, 64)), (nnt, slice(0, 64))
                            ncur_key, ntcur_key = nnk, nntk
                        yield
                    yield
                    for e_ in range(2):
                        er = slice(e_ * 64, e_ * 64 + 64)
                        c0 = tb + e_ * 64
                        bank = C.bank()
                        pg = C.PA[bank][:, :].rearrange("p (h c) -> p h c", c=128)
                        for h in range(8):
                            hp = h // 2
                            S.op("pe", lambda e, h=h, hp=hp, er=er, pg=pg: e.matmul(
                                pg[rows(h), hp, :], lhsT=xt[er, h, 0:64], rhs=nb[er, h, 64:192], start=True, stop=True),
                                reads=[kn("xt"), kn("nb")], writes=[("pa", bank)])
                        S.op("dve", lambda e, e_=e_, c0=c0, pg=pg: e.tensor_tensor(
                            out=gy[:, :, e_, :], in0=pg[:, :, 0:64], in1=AR[:, :, 1, c0:c0 + 64], op=ALU.add),
                            reads=[("pa", bank)] + [("ar", hp) for hp in range(4)], writes=[kn("gy")])
                        for par in range(2):
                            pr = slice(par * 64, par * 64 + 64)
                            S.op("dve", lambda e, e_=e_, pg=pg, pr=pr, par=par: e.tensor_tensor(
                                out=gs[pr, :, e_, par * 64:par * 64 + 64], in0=pg[pr, :, 64:128],
                                in1=I2[pr, :].unsqueeze(1).to_broadcast([64, 4, 64]), op=ALU.add),
                                reads=[("pa", bank), "masks"], writes=[kn("gs")])
                        bl = C.bank()
                        psl = C.PA[bl][:, :].rearrange("p (h c) -> p h c", c=128)
                        for h in range(8):
                            hp = h // 2
                            par = h % 2
                            S.op("pe", lambda e, h=h, hp=hp, par=par, er=er, psl=psl: e.matmul(
                                psl[rows(h), hp, par * 64:par * 64 + 64], lhsT=nb[er, h, 128:192], rhs=xt[er, h, 64:128], start=True, stop=False),
                                reads=[kn("nb"), kn("xt")], writes=[("pa", bl)])
                            S.op("pe", lambda e, h=h, hp=hp, par=par, er=er, psl=psl: e.matmul(
                                psl[rows(h), hp, par * 64:par * 64 + 64], lhsT=kb[er, h, 128:192], rhs=vmt[er, h, :], start=False, stop=True),
                                reads=[kn("kb"), kn("vmt")], writes=[("pa", bl)])
                        for par in range(2):
                            pr = slice(par * 64, par * 64 + 64)
                            S.op("act", lambda e, e_=e_, psl=psl, pr=pr, par=par: e.activation(
                                out=slbd[pr, :, e_, par * 64:par * 64 + 64], in_=psl[pr, :, par * 64:par * 64 + 64], func=AF.Copy),
                                reads=[("pa", bl)], writes=[kn("slbd")])
                    byl = C.bank()
                    pyl = C.PA[byl][:, :].rearrange("p (h c) -> p h c", c=64)
                    for e_ in range(2):
                        er = slice(e_ * 64, e_ * 64 + 64)
                        for h in range(8):
                            S.op("pe", lambda e, h=h, er=er: e.matmul(
                                pyl[er, h, :], lhsT=nb[er, h, 64:128], rhs=xt[er, h, 64:128], start=True, stop=False),
                                reads=[kn("nb"), kn("xt")], writes=[("pa", byl)])
                            S.op("pe", lambda e, h=h, er=er: e.matmul(
                                pyl[er, h, :], lhsT=kb[er, h, 64:128], rhs=vmt[er, h, :], start=False, stop=True),
                                reads=[kn("kb"), kn("vmt")], writes=[("pa", byl)])
                    S.op("act", lambda e: e.activation(out=YLOCS[sl_][:], in_=C.PA[byl][:, :], func=AF.Copy), reads=[("pa", byl)], writes=[kn("yloc")])
                    yield
                    py = C.PY[:, :].rearrange("p (h c) -> p h c", c=128)
                    for e_ in range(2):
                        er = slice(e_ * 64, e_ * 64 + 64)
                        cl = j * 2 + e_
                        stc = STT[st_slot]
                        stn = STT[1 - st_slot]
                        psv = C.PS[:, :].rearrange("p (h c) -> p h c", c=128)
                        for hp in range(4):
                            S.op("pe", lambda e, hp=hp, er=er, stc=stc, e_=e_: e.matmul(
                                py[er, hp, :], lhsT=gy[:, hp, e_, :], rhs=stc[:, hp, :], start=True, stop=True),
                                reads=[kn("gy"), ("st", st_slot)], writes=["py"])
                        for hp in range(4):
                            S.op("pe", lambda e, hp=hp, stc=stc, e_=e_: e.matmul(
                                psv[:, hp, :], lhsT=gs[:, hp, e_, :], rhs=stc[:, hp, :], start=True, stop=True),
                                reads=[kn("gs"), ("st", st_slot)], writes=["ps"])
                        S.op("dve", lambda e, e_=e_: e.tensor_tensor(out=STMP[:], in0=psv, in1=slbd[:, :, e_, :], op=ALU.add),
                             reads=["ps", kn("slbd")], writes=["stmp"])
                        S.op("dve", lambda e, stn=stn, cl=cl: e.tensor_tensor(
                            out=stn[:], in0=STMP[:], in1=WCS[:, :, cl:cl + 1].to_broadcast([128, 4, 128]), op=ALU.mult),
                            reads=["stmp", "wcs"], writes=[("st", 1 - st_slot)])
                        st_slot = 1 - st_slot
                    S.op("dve", lambda e: e.tensor_tensor(out=YSB[:], in0=C.PY[:, :], in1=YLOCS[sl_][:], op=ALU.add), reads=["py", kn("yloc")], writes=["ysb"])
                    S.op("act", lambda e: e.activation(out=YN[:], in_=YSB[:], func=AF.Square), reads=["ysb"], writes=["yn"])
                    S.op("dve", lambda e: e.tensor_reduce(out=STAT[:, 0:8], in_=YSB[:].rearrange("p (h c) -> p h c", c=64), op=ALU.add, axis=AX.X),
                         reads=["ysb"], writes=["stat0"])
                    S.op("dve", lambda e: e.tensor_reduce(out=STAT[:, 8:16], in_=YN[:].rearrange("p (h c) -> p h c", c=64), op=ALU.add, axis=AX.X),
                         reads=["yn"], writes=["stat1"])
                    S.op("dve", lambda e: e.tensor_scalar(out=STAT[:, 16:24], in0=STAT[:, 0:8], scalar1=1.0 / 64, scalar2=None, op0=ALU.mult),
                         reads=["stat0"], writes=["stat2"])
                    S.op("dve", lambda e: e.tensor_tensor(out=STAT[:, 24:32], in0=STAT[:, 16:24], in1=STAT[:, 16:24], op=ALU.mult),
                         reads=["stat2"], writes=["stat3"])
                    S.op("dve", lambda e: e.scalar_tensor_tensor(out=STAT[:, 32:40], in0=STAT[:, 8:16], scalar=1.0 / 64, in1=STAT[:, 24:32],
                                                                op0=ALU.mult, op1=ALU.subtract),
                         reads=["stat1", "stat3"], writes=["stat4"])
                    S.op("act", lambda e: e.activation(out=STAT[:, 32:40], in_=STAT[:, 32:40], func=AF.Ln, bias=C.eps_gn[:, 0:1]),
                         reads=["stat4", "consts"], writes=["stat4"])
                    S.op("act", lambda e: e.activation(out=STAT[:, 32:40], in_=STAT[:, 32:40], func=AF.Exp, scale=-0.5),
                         reads=["stat4"], writes=["stat4"])
                    S.op("dve", lambda e: e.scalar_tensor_tensor(out=STAT[:, 40:48], in0=STAT[:, 16:24], scalar=-1.0, in1=STAT[:, 32:40],
                                                                op0=ALU.mult, op1=ALU.mult),
                         reads=["stat2", "stat4"], writes=["stat5"])
                    v3 = lambda t: t[:].rearrange("p (h c) -> p h c", c=64)
                    S.op("dve", lambda e: e.tensor_tensor(out=v3(YN), in0=v3(YSB), in1=STAT[:, 32:40].unsqueeze(2).to_broadcast([128, 8, 64]), op=ALU.mult),
                         reads=["ysb", "stat4"], writes=["yn"])
                    S.op("pool", lambda e: e.tensor_tensor(out=v3(YN), in0=v3(YN), in1=STAT[:, 40:48].unsqueeze(2).to_broadcast([128, 8, 64]), op=ALU.add),
                         reads=["yn", "stat5"], writes=["yn"])
                    S.op("pool", lambda e: e.tensor_tensor(out=YN[:], in0=YN[:], in1=GNG[:], op=ALU.mult), reads=["yn", "vecs"], writes=["yn"])
                    S.op("pool", lambda e: e.tensor_tensor(out=YN[:], in0=YN[:], in1=GNB[:], op=ALU.add), reads=["yn", "vecs"], writes=["yn"])
                    bb_ = C.bank()
                    for hp in range(4):
                        S.op("pe", lambda e, hp=hp, bb_=bb_: e.matmul(C.PA[bb_][:, 0:8], lhsT=RKR[:, hp, tb:tb + 128], rhs=HSEL[:, hp, :],
                                                                     start=(hp == 0), stop=(hp == 3)),
                             reads=[("rkr", hp), "masks"], writes=[("pa", bb_)])
                    S.op("act", lambda e, bb_=bb_: e.activation(out=BON[:], in_=C.PA[bb_][:, 0:8], func=AF.Copy), reads=[("pa", bb_)], writes=["bon"])
                    S.op("dve", lambda e, vmt=vmt: e.tensor_tensor(out=v3(YLOCS[sl_]), in0=vmt[:], in1=BON[:].unsqueeze(2).to_broadcast([128, 8, 64]), op=ALU.mult),
                         reads=[kn("vmt"), "bon"], writes=[kn("yloc")])
                    S.op("pool", lambda e: e.tensor_tensor(out=YBF[:], in0=YN[:], in1=YLOCS[sl_][:], op=ALU.add), reads=["yn", kn("yloc")], writes=["ybf"])
                    for hp in range(4):
                        S.op("pe", lambda e, hp=hp: e.transpose(out=C.PT[:, hp * 128:hp * 128 + 128], in_=YBF[:, hp * 128:hp * 128 + 128], identity=C.ident[:, :]),
                             reads=["ybf", "consts"], writes=["pt"])
                    t_abs = st * 512 + tb
                    S.op("dve", lambda e, t_abs=t_abs, tb=tb: e.tensor_tensor(
                        out=C.YZ[:, :, t_abs:t_abs + 128], in0=C.PT[:, 0:512].rearrange("p (h c) -> p h c", c=128),
                        in1=SZ[:, :, tb:tb + 128], op=ALU.mult),
                        reads=["pt"] + [("sz", hp) for hp in range(4)], writes=["yz"])
                for pair in ((0, 1), (2, 3)):
                    gens = [tile_gen(j) for j in pair]
                    while gens:
                        for g in list(gens):
                            try:
                                next(g)
                            except StopIteration:
                                gens.remove(g)
            if dbg is not None and "yz" in dbg:
                S.op("sp", lambda e: e.dma_start(out=dbg["yz"].rearrange("(h p) t -> p h t", p=128), in_=C.YZ[:, :, 0:T]), reads=["yz"], dma=True, force=True)
            S.finalize_and_emit(block)


def phase_B(nc, S, C, T, dbg=None):
    NST = T // 512
    NKT = T // 128
    SCALE = 1.0 / float(np.sqrt(96.0))
    TWO_PI = float(2 * np.pi)
    with ExitStack() as es:
        def sb(name, shape, dt):
            return es.enter_context(nc.sbuf_tensor(name, shape, dt))
        WB = sb("WB", [128, NC_, 960], BF16)
        WUQ = sb("WUQ", [128, 2, 768], BF16)
        WUQS = sb("WUQS", [128, 2, 8, 32], BF16)
        WUKV = sb("WUKV", [128, 8, 128], BF16)
        C.XS = [sb("xsb%d" % i, [128, 512], F32) for i in range(2)]
        C.xs_i = 0
        C.SQ = [sb("sqb%d" % i, [128, 512], BF16) for i in range(2)]
        C.XG = sb("XGb", [128, NC_, 512], BF16)
        C.RSTD = sb("RSTDb", [128, 512], F32)
        GQ = sb("GQ", [128, 2], F32)
        GKV = sb("GKV", [128, 1], F32)
        INVF = sb("INVF", [128, 1], F32)
        KT = sb("KTb", [128, 4, T], BF16)
        VT = sb("VTb", [128, NKT, 4, 65], BF16)
        QTS = [sb("QTb%d" % i, [128, 4, 512], BF16) for i in range(2)]
        SZBS = [sb("SZB%d" % i, [128, 4, 512], BF16) for i in range(2)]
        CQ = [sb("CQ%d" % i, [128, 512], F32) for i in range(2)]
        CKV = CQ[0]
        CQN = sb("CQN", [128, 2, 512], BF16)
        CKVN = sb("CKVN", [128, 512], BF16)
        RQ = sb("RQ", [128, 512], F32)
        ANG = sb("ANG", [128, 512], F32)
        TR1 = sb("TR1", [128, 512], F32)
        TR2 = sb("TR2", [128, 512], F32)
        TRI = sb("TRI", [128, 512], I32)
        POSI = TRI
        COS = sb("COS", [128, 512], F32)
        SIN = sb("SIN", [128, 512], F32)
        CR = sb("CR", [128, 512], F32)
        SR = sb("SR", [128, 512], F32)
        KR = sb("KR", [128, 512], BF16)
        PTB = [sb("PTB%d" % i, [128, 512], BF16) for i in range(5)]
        RDEN = sb("RDEN", [128, 512], F32)
        RDENB = sb("RDENB", [128, 512], BF16)
        BCS = sb("BCS", [128, 512], BF16)
        OTMP = sb("OTMP", [128, 512], F32)
        OSTG = [sb("OSTG%d" % i, [128, 512], BF16) for i in range(2)]

        C.sbuf_left_B = nc.sbuf_bytes_remaining
        with nc.Block() as block:
            w_in_v = C.w_in.rearrange("(c p) n -> p c n", p=128)
            S.op("pool", lambda e: e.dma_start(out=WB[:, :, 0:928], in_=w_in_v[:, :, C_CQ:C_CQ + 928]), writes=["w"], dma=True)
            S.op("pool", lambda e: e.dma_start(out=WB[:, :, 928:944], in_=w_in_v[:, :, C_KPE + 16:C_KPE + 32]), writes=["w"], dma=True)
            S.op("pool", lambda e: e.dma_start(out=WB[:, :, 944:960], in_=w_in_v[:, :, C_KPE:C_KPE + 16]), writes=["w"], dma=True)
            S.op("dve", lambda e: e.tensor_scalar(out=WB[:, :, 928:944], in0=WB[:, :, 928:944], scalar1=-1.0, scalar2=None, op0=ALU.mult),
                 reads=["w"], writes=["w"])
            S.op("pool", lambda e: e.dma_start(out=WUQ[:], in_=C.w_uq.rearrange("(c p) n -> p c n", p=128)), writes=["w"], dma=True)
            S.op("pool", lambda e: e.dma_start(out=WUKV[:], in_=C.w_ukv.rearrange("p (h c) -> p h c", c=128)), writes=["w"], dma=True)
            wq4 = WUQ[:].rearrange("p c (h d) -> p c h d", d=96)
            S.op("dve", lambda e: e.tensor_scalar(out=WUQS[:, :, :, 0:16], in0=wq4[:, :, :, 80:96], scalar1=-1.0, scalar2=None, op0=ALU.mult),
                 reads=["w"], writes=["w2"])
            S.op("dve", lambda e: e.tensor_copy(out=WUQS[:, :, :, 16:32], in_=wq4[:, :, :, 64:80]), reads=["w"], writes=["w2"])
            _col_load(S, nc, GQ[:, :], C.g_q, 2, "vecs")
            _col_load(S, nc, GKV[:, :], C.g_kv, 1, "vecs")
            S.op("sp", lambda e: e.dma_start(out=INVF[64:96, :], in_=C.rope_invf.rearrange("(p o) -> p o", o=1)), writes=["vecs"], dma=True)
            S.op("pool", lambda e: e.memset(VT[:, :, :, 64:65], 1.0), writes=["vtones"])
            rr = slice(64, 96)

            def prologue_gen(hg, st):
                t0 = st * 512
                make_u(S, C, st)
                yield
                S.op("sp", lambda e, t0=t0: e.dma_start(out=POSI[rr, :], in_=C.positions[t0:t0 + 512].partition_broadcast(32)),
                     writes=["tri"], dma=True)
                S.op("dve", lambda e: e.tensor_copy(out=ANG[rr, :], in_=POSI[rr, :]), reads=["tri"], writes=["ang"])
                S.op("dve", lambda e: e.tensor_scalar(out=ANG[rr, :], in0=ANG[rr, :], scalar1=INVF[rr, 0:1], scalar2=None, op0=ALU.mult),
                     reads=["ang", "vecs"], writes=["ang"])
                for (dst, off, key) in ((SIN, 0.0, "sin"), (COS, 0.25, "cos")):
                    S.op("dve", lambda e, off=off: e.tensor_scalar(out=TR1[rr, :], in0=ANG[rr, :], scalar1=1.0 / TWO_PI, scalar2=off,
                                                                    op0=ALU.mult, op1=ALU.add), reads=["ang"], writes=["tr1"])
                    S.op("dve", lambda e: e.tensor_copy(out=TRI[rr, :], in_=TR1[rr, :]), reads=["tr1"], writes=["tri"])
                    S.op("dve", lambda e: e.tensor_copy(out=TR2[rr, :], in_=TRI[rr, :]), reads=["tri"], writes=["tr2"])
                    S.op("dve", lambda e: e.tensor_tensor(out=TR1[rr, :], in0=TR1[rr, :], in1=TR2[rr, :], op=ALU.subtract),
                         reads=["tr1", "tr2"], writes=["tr1"])
                    S.op("dve", lambda e: e.tensor_scalar(out=TR2[rr, :], in0=TR1[rr, :], scalar1=0.5, scalar2=None, op0=ALU.is_gt),
                         reads=["tr1"], writes=["tr2"])
                    S.op("dve", lambda e: e.tensor_tensor(out=TR1[rr, :], in0=TR1[rr, :], in1=TR2[rr, :], op=ALU.subtract),
                         reads=["tr1", "tr2"], writes=["tr1"])
                    S.op("dve", lambda e: e.tensor_scalar(out=TR2[rr, :], in0=TR1[rr, :], scalar1=-0.5, scalar2=None, op0=ALU.is_lt),
                         reads=["tr1"], writes=["tr2"])
                    S.op("dve", lambda e: e.tensor_tensor(out=TR1[rr, :], in0=TR1[rr, :], in1=TR2[rr, :], op=ALU.add),
                         reads=["tr1", "tr2"], writes=["tr1"])
                    S.op("act", lambda e, dst=dst: e.activation(out=dst[rr, :], in_=TR1[rr, :], func=AF.Sin, scale=TWO_PI),
                         reads=["tr1"], writes=[key])
                yield
                for i in range(2):
                    b = proj(S, C, WB, i * 128)
                    S.op("dve", lambda e, b=b, i=i: e.tensor_tensor(out=CQ[i][:], in0=C.PA[b][:, :], in1=C.RSTD[:], op=ALU.mult),
                         reads=[("pa", b), "rstd"], writes=[("cq", i)])
                bss = C.bank()
                for i in range(2):
                    S.op("act", lambda e, i=i: e.activation(out=C.SQ[i][:], in_=CQ[i][:], func=AF.Square), reads=[("cq", i)], writes=[("sq", i)])
                    S.op("pe", lambda e, i=i, bss=bss: e.matmul(C.PA[bss][:, :], lhsT=C.ones_bf[:, :], rhs=C.SQ[i][:], start=(i == 0), stop=(i == 1)),
                         reads=[("sq", i), "consts"], writes=[("pa", bss)])
                S.op("act", lambda e, bss=bss: e.activation(out=RQ[:], in_=C.PA[bss][:, :], func=AF.Ln, scale=1.0 / 256, bias=C.eps_norm[:, 0:1]),
                     reads=[("pa", bss), "consts"], writes=["rq"])
                S.op("act", lambda e: e.activation(out=RQ[:], in_=RQ[:], func=AF.Exp, scale=-0.5), reads=["rq"], writes=["rq"])
                for i in range(2):
                    S.op("dve", lambda e, i=i: e.scalar_tensor_tensor(out=CQN[:, i, :], in0=CQ[i][:], scalar=GQ[:, i:i + 1], in1=RQ[:],
                                                                     op0=ALU.mult, op1=ALU.mult),
                         reads=[("cq", i), "rq", "vecs"], writes=[("cqn", i)])
                b = proj(S, C, WB, 256)
                S.op("dve", lambda e, b=b: e.tensor_tensor(out=CKV[:], in0=C.PA[b][:, :], in1=C.RSTD[:], op=ALU.mult),
                     reads=[("pa", b), "rstd"], writes=[("cq", 0)])
                bss = C.bank()
                S.op("act", lambda e: e.activation(out=C.SQ[0][:], in_=CKV[:], func=AF.Square), reads=[("cq", 0)], writes=[("sq", 0)])
                S.op("pe", lambda e, bss=bss: e.matmul(C.PA[bss][:, :], lhsT=C.ones_bf[:, :], rhs=C.SQ[0][:], start=True, stop=True),
                     reads=[("sq", 0), "consts"], writes=[("pa", bss)])
                S.op("act", lambda e, bss=bss: e.activation(out=RQ[:], in_=C.PA[bss][:, :], func=AF.Ln, scale=1.0 / 128, bias=C.eps_norm[:, 0:1]),
                     reads=[("pa", bss), "consts"], writes=["rq"])
                S.op("act", lambda e: e.activation(out=RQ[:], in_=RQ[:], func=AF.Exp, scale=-0.5), reads=["rq"], writes=["rq"])
                S.op("dve", lambda e: e.scalar_tensor_tensor(out=CKVN[:], in0=CKV[:], scalar=GKV[:, 0:1], in1=RQ[:], op0=ALU.mult, op1=ALU.mult),
                     reads=[("cq", 0), "rq", "vecs"], writes=["ckvn"])
                yield
                S.op("dve", lambda e: e.tensor_tensor(out=CR[rr, :], in0=COS[rr, :], in1=C.RSTD[rr, :], op=ALU.mult), reads=["cos", "rstd"], writes=["cr"])
                S.op("dve", lambda e: e.tensor_tensor(out=SR[rr, :], in0=SIN[rr, :], in1=C.RSTD[rr, :], op=ALU.mult), reads=["sin", "rstd"], writes=["sr"])
                bk = C.bank()
                bks = C.bank()
                for dc in range(NC_):
                    S.op("pe", lambda e, dc=dc, bk=bk: e.matmul(C.PA[bk][64:96, :], lhsT=WB[:, dc, 384:416], rhs=C.XG[:, dc, :],
                                                               start=(dc == 0), stop=(dc == NC_ - 1)), reads=[("xg", dc), "w"], writes=[("pa", bk)])
                for dc in range(NC_):
                    S.op("pe", lambda e, dc=dc, bks=bks: e.matmul(C.PA[bks][64:96, :], lhsT=WB[:, dc, 928:960], rhs=C.XG[:, dc, :],
                                                                 start=(dc == 0), stop=(dc == NC_ - 1)), reads=[("xg", dc), "w"], writes=[("pa", bks)])
                S.op("dve", lambda e, bk=bk: e.tensor_tensor(out=TR1[rr, :], in0=C.PA[bk][rr, :], in1=CR[rr, :], op=ALU.mult),
                     reads=[("pa", bk), "cr"], writes=["tr1"])
                S.op("dve", lambda e, bks=bks: e.tensor_tensor(out=TR2[rr, :], in0=C.PA[bks][rr, :], in1=SR[rr, :], op=ALU.mult),
                     reads=[("pa", bks), "sr"], writes=["tr2"])
                S.op("dve", lambda e: e.tensor_tensor(out=KR[rr, :], in0=TR1[rr, :], in1=TR2[rr, :], op=ALU.add), reads=["tr1", "tr2"], writes=["kr"])
                S.op("dve", lambda e: e.tensor_scalar(out=CR[rr, :], in0=COS[rr, :], scalar1=SCALE, scalar2=None, op0=ALU.mult), reads=["cos"], writes=["cr"])
                S.op("dve", lambda e: e.tensor_scalar(out=SR[rr, :], in0=SIN[rr, :], scalar1=SCALE, scalar2=None, op0=ALU.mult), reads=["sin"], writes=["sr"])
                yield
                for hl in range(4):
                    h = hg * 4 + hl
                    bkn = C.bank()
                    S.op("pe", lambda e, h=h, bkn=bkn: e.matmul(C.PA[bkn][0:64, :], lhsT=WUKV[:, h, 0:64], rhs=CKVN[:], start=True, stop=True),
                         reads=["ckvn", "w"], writes=[("pa", bkn)])
                    S.op("dve", lambda e, hl=hl, bkn=bkn, t0=t0: e.tensor_copy(out=KT[0:64, hl, t0:t0 + 512], in_=C.PA[bkn][0:64, :]),
                         reads=[("pa", bkn)], writes=[("kt", hl, st)])
                    S.op("dve", lambda e, hl=hl, t0=t0: e.tensor_copy(out=KT[rr, hl, t0:t0 + 512], in_=KR[rr, :]), reads=["kr"], writes=[("kt", hl, st)])
                    bq = C.bank()
                    for kc in range(2):
                        S.op("pe", lambda e, kc=kc, h=h, bq=bq: e.matmul(C.PA[bq][0:96, :], lhsT=WUQ[:, kc, h * 96:h * 96 + 96], rhs=CQN[:, kc, :],
                                                                        start=(kc == 0), stop=(kc == 1)),
                             reads=[("cqn", kc), "w"], writes=[("pa", bq)])
                    bqs = C.bank()
                    for kc in range(2):
                        S.op("pe", lambda e, kc=kc, h=h, bqs=bqs: e.matmul(C.PA[bqs][64:96, :], lhsT=WUQS[:, kc, h, :], rhs=CQN[:, kc, :],
                                                                          start=(kc == 0), stop=(kc == 1)),
                             reads=[("cqn", kc), "w2"], writes=[("pa", bqs)])
                    S.op("dve", lambda e, hl=hl, bq=bq: e.tensor_scalar(out=QTS[st % 2][0:64, hl, :], in0=C.PA[bq][0:64, :], scalar1=SCALE, scalar2=None, op0=ALU.mult),
                         reads=[("pa", bq)], writes=[("qt", st % 2, hl)])
                    S.op("dve", lambda e, bq=bq: e.tensor_tensor(out=TR1[rr, :], in0=C.PA[bq][rr, :], in1=CR[rr, :], op=ALU.mult),
                         reads=[("pa", bq), "cr"], writes=["tr1"])
                    S.op("dve", lambda e, bqs=bqs: e.tensor_tensor(out=TR2[rr, :], in0=C.PA[bqs][rr, :], in1=SR[rr, :], op=ALU.mult),
                         reads=[("pa", bqs), "sr"], writes=["tr2"])
                    S.op("dve", lambda e, hl=hl: e.tensor_tensor(out=QTS[st % 2][rr, hl, :], in0=TR1[rr, :], in1=TR2[rr, :], op=ALU.add),
                         reads=["tr1", "tr2"], writes=[("qt", st % 2, hl)])
                    bz = proj(S, C, WB, 416 + h * 64, ncols=64)
                    S.op("dve", lambda e, bz=bz: e.tensor_tensor(out=CQ[1][0:64, :], in0=C.PA[bz][0:64, :], in1=C.RSTD[0:64, :], op=ALU.mult),
                         reads=[("pa", bz), "rstd"], writes=[("cq", 1)])
                    S.op("act", lambda e, hl=hl: e.activation(out=SZBS[st % 2][0:64, hl, :], in_=CQ[1][0:64, :], func=AF.Silu), reads=[("cq", 1)], writes=[("szb", st % 2, hl)])
                    yield
                yield
                for j in range(4):
                    kt = st * 4 + j
                    bv = C.bank()
                    S.op("pe", lambda e, j=j, bv=bv: e.matmul(C.PA[bv][:, 0:256].rearrange("p (h c) -> p h c", c=64),
                                                           lhsT=CKVN[:, j * 128:(j + 1) * 128], rhs=WUKV[:, hg * 4:hg * 4 + 4, 64:128], start=True, stop=True),
                         reads=["ckvn", "w"], writes=[("pa", bv)])
                    S.op("dve", lambda e, kt=kt, bv=bv: e.tensor_copy(out=VT[:, kt, :, 0:64], in_=C.PA[bv][:, 0:256].rearrange("p (h c) -> p h c", c=64)),
                         reads=[("pa", bv)], writes=[("vt", st)])
                yield
            def attention_gen(hg, st):
                t0 = st * 512
                blocks = [(hl, kt) for hl in range(4) for kt in range(st * 4 + 4)]
                nkt = st * 4 + 4
                LOOK = 3
                binfo = {}
                tails1, tails2 = {}, {}

                def emit_qk(i):
                    hl, kt = blocks[i]
                    bs_ = C.bank()
                    S.op("pe", lambda e: e.matmul(C.PA[bs_][:, :], lhsT=KT[0:96, hl, kt * 128:(kt + 1) * 128], rhs=QTS[st % 2][0:96, hl, :],
                                                  start=True, stop=True),
                         reads=[("kt", hl, kt // 4), ("qt", st % 2, hl)], writes=[("pa", bs_)])
                    pi = C.ptb_i % len(PTB)
                    C.ptb_i += 1
                    ptb = PTB[pi]
                    S.op("act", lambda e: e.activation(out=ptb[:], in_=C.PA[bs_][:, :], func=AF.Exp),
                         reads=[("pa", bs_)], writes=[("ptb", pi)])
                    d = kt - st * 4
                    if d >= 0:
                        S.op("pool", lambda e: e.affine_select(out=ptb[:], in_=ptb[:], pattern=[[1, 512]], compare_op=ALU.is_ge,
                                                               fill=0.0, base=-128 * d, channel_multiplier=-1),
                             reads=[("ptb", pi)], writes=[("ptb", pi)])
                    binfo[i] = (ptb, pi)

                def acc_of(hl):
                    return (C.PY, "py") if hl % 2 == 0 else (C.PS, "ps")

                def emit_pv(i):
                    hl, kt = blocks[i]
                    ptb, pi = binfo.pop(i)
                    oacc, okey = acc_of(hl)
                    S.op("pe", lambda e: e.matmul(oacc[0:65, :], lhsT=VT[:, kt, hl, :], rhs=ptb[:], start=(kt == 0), stop=(kt == nkt - 1)),
                         reads=[("vt", kt // 4), "vtones", ("ptb", pi)], writes=[okey])

                def emit_tail1(hl):
                    oacc, okey = acc_of(hl)
                    S.op("act", lambda e: e.activation(out=RDEN[64:65, :], in_=oacc[64:65, :], func=AF.Ln), reads=[okey], writes=["rden"])
                    S.op("act", lambda e: e.activation(out=RDENB[64:65, :], in_=RDEN[64:65, :], func=AF.Exp, scale=-1.0), reads=["rden"], writes=["rdenb"])

                def emit_tail2(hl):
                    h = hg * 4 + hl
                    hp, par = h // 2, h % 2
                    oacc, okey = acc_of(hl)
                    bb_ = C.bank()
                    S.op("pe", lambda e: e.matmul(C.PA[bb_][0:64, :], lhsT=C.ones_bf[64:65, 0:64], rhs=RDENB[64:65, :], start=True, stop=True),
                         reads=["rdenb", "consts"], writes=[("pa", bb_)])
                    S.op("dve", lambda e: e.tensor_copy(out=BCS[0:64, :], in_=C.PA[bb_][0:64, :]), reads=[("pa", bb_)], writes=["bcs"])
                    S.op("dve", lambda e: e.tensor_tensor(out=OTMP[0:64, :], in0=oacc[0:64, :], in1=BCS[0:64, :], op=ALU.mult),
                         reads=[okey, "bcs"], writes=["otmp"])
                    if par == 0:
                        S.op("dve", lambda e: e.tensor_tensor(out=C.OG[0:64, hp, t0:t0 + 512], in0=OTMP[0:64, :], in1=SZBS[st % 2][0:64, hl, :], op=ALU.mult),
                             reads=["otmp", ("szb", st % 2, hl)], writes=["og"])
                    else:
                        oi = C.ostg_i % 2
                        C.ostg_i += 1
                        S.op("dve", lambda e: e.tensor_tensor(out=OSTG[oi][0:64, :], in0=OTMP[0:64, :], in1=SZBS[st % 2][0:64, hl, :], op=ALU.mult),
                             reads=["otmp", ("szb", st % 2, hl)], writes=[("ostg", oi)])
                        S.op("sp", lambda e: e.dma_start(out=C.OG[64:128, hp, t0:t0 + 512], in_=OSTG[oi][0:64, :]),
                             reads=[("ostg", oi)], writes=["og"], dma=True)

                nb_ = len(blocks)
                for i in range(nb_ + LOOK + 6):
                    if i < nb_:
                        emit_qk(i)
                    if 0 <= i - LOOK < nb_:
                        emit_pv(i - LOOK)
                        hl_, kt_ = blocks[i - LOOK]
                        if kt_ == nkt - 1:
                            tails1[i + 1] = hl_
                            tails2[i + 3] = hl_
                    if i in tails1:
                        emit_tail1(tails1[i])
                    if i in tails2:
                        emit_tail2(tails2[i])
                    yield
            def drain(g):
                if g is not None:
                    for _ in g:
                        pass
            prev = None
            for hg in range(2):
                for st in range(NST):
                    if st == 0:
                        drain(prev)
                        prev = None
                    S.deferred = []
                    S.defer = S.deferred
                    C.pool = [0, 1]
                    for _ in prologue_gen(hg, st):
                        pass
                    S.defer = None
                    C.pool = [2, 3, 4]
                    if prev is not None:
                        nblk = (st - 1) * 4 * 4 + 16
                        per = max(2, -(-len(S.deferred) // max(1, nblk - 8)))
                        for _ in prev:
                            S.flush(per)
                    S.flush(None)
                    prev = attention_gen(hg, st)
            drain(prev)
            C.pool = list(range(len(C.PA)))
            if dbg is not None and "og" in dbg:
                S.op("sp", lambda e: e.dma_start(out=dbg["og"].rearrange("(h p) t -> p h t", p=128), in_=C.OG[:, :, 0:T]), reads=["og"], dma=True, force=True)
            S.finalize_and_emit(block)


def phase_C(nc, S, C, T, dbg=None):
    NST = T // 512
    with ExitStack() as es:
        def sb(name, shape, dt):
            return es.enter_context(nc.sbuf_tensor(name, shape, dt))
        WG = sb("WG", [128, NC_, 2048], BF16)
        WOA = sb("WOA", [128, 4, D], BF16)
        WOB = sb("WOB", [128, 4, D], BF16)
        WO = sb("WO", [128, NC_, D], BF16)
        C.XS = [sb("xsc%d" % i, [128, 512], F32) for i in range(3)]
        C.xs_i = 0
        C.SQ = [sb("sqc%d" % i, [128, 512], BF16) for i in range(2)]
        C.XG = sb("XGc", [128, NC_, 512], BF16)
        C.RSTD = sb("RSTDc", [128, 512], F32)
        BG = sb("BG", [128, 16], F32)
        GPB = sb("GPB", [128, D], F32)
        GA = [sb("GA%d" % i, [128, 512], F32) for i in range(2)]
        GB = [sb("GB%d" % i, [128, 512], F32) for i in range(2)]
        MT1 = [sb("MT1_%d" % i, [128, 512], F32) for i in range(2)]
        MT2 = [sb("MT2_%d" % i, [128, 512], F32) for i in range(2)]
        MER = sb("MER", [128, NC_, 512], BF16)
        XTOK = [sb("XTOK%d" % i, [128, D], F32) for i in range(2)]
        OTOK = [sb("OTOK%d" % i, [128, D], F32) for i in range(2)]
        SCR = sb("SCR", [128, 512], F32)
        ST2 = sb("ST2", [128, 8], F32)

        C.sbuf_left_C = nc.sbuf_bytes_remaining
        with nc.Block() as block:
            w_in_v = C.w_in.rearrange("(c p) n -> p c n", p=128)
            for a in range(0, 2048, 1024):
                S.op("pool", lambda e, a=a: e.dma_start(out=WG[:, :, a:a + 1024], in_=w_in_v[:, :, C_GATE + a:C_GATE + a + 1024]), writes=["w"], dma=True)
            S.op("pool", lambda e: e.dma_start(out=WOA[:], in_=C.w_out_a.rearrange("(c p) n -> p c n", p=128)), writes=["w"], dma=True)
            S.op("pool", lambda e: e.dma_start(out=WOB[:], in_=C.w_out_b.rearrange("(c p) n -> p c n", p=128)), writes=["w"], dma=True)
            S.op("pool", lambda e: e.dma_start(out=WO[:], in_=C.w_o.rearrange("(c p) n -> p c n", p=128)), writes=["w"], dma=True)
            _col_load(S, nc, BG[:, :], C.b_gate, 16, "vecs")
            S.op("sp", lambda e: e.dma_start(out=GPB[:], in_=C.g_post.partition_broadcast(128)), writes=["vecs"], dma=True)
            for st in range(NST):
                t0 = st * 512
                make_u(S, C, st)
                for ft in range(8):
                    i2 = ft % 2
                    bga = proj(S, C, WG, ft * 128)
                    S.op("dve", lambda e, bga=bga, i2=i2: e.tensor_tensor(out=GA[i2][:], in0=C.PA[bga][:, :], in1=C.RSTD[:], op=ALU.mult),
                         reads=[("pa", bga), "rstd"], writes=[("ga", i2)])
                    S.op("act", lambda e, i2=i2, ft=ft: e.activation(out=GA[i2][:], in_=GA[i2][:], func=AF.Sigmoid, bias=BG[:, ft:ft + 1]),
                         reads=[("ga", i2), "vecs"], writes=[("ga", i2)])
                    bgb = proj(S, C, WG, 1024 + ft * 128)
                    S.op("dve", lambda e, bgb=bgb, i2=i2: e.tensor_tensor(out=GB[i2][:], in0=C.PA[bgb][:, :], in1=C.RSTD[:], op=ALU.mult),
                         reads=[("pa", bgb), "rstd"], writes=[("gb", i2)])
                    S.op("act", lambda e, i2=i2, ft=ft: e.activation(out=GB[i2][:], in_=GB[i2][:], func=AF.Sigmoid, bias=BG[:, 8 + ft:9 + ft]),
                         reads=[("gb", i2), "vecs"], writes=[("gb", i2)])
                    bya = C.bank()
                    for kc in range(4):
                        S.op("pe", lambda e, kc=kc, ft=ft, bya=bya, t0=t0: e.matmul(C.PA[bya][:, :], lhsT=WOA[:, kc, ft * 128:(ft + 1) * 128],
                                                                                rhs=C.YZ[:, kc, t0:t0 + 512], start=(kc == 0), stop=(kc == 3)),
                             reads=["yz", "w"], writes=[("pa", bya)])
                    S.op("dve", lambda e, bya=bya, i2=i2: e.tensor_tensor(out=MT1[i2][:], in0=C.PA[bya][:, :], in1=GA[i2][:], op=ALU.mult),
                         reads=[("pa", bya), ("ga", i2)], writes=[("mt1", i2)])
                    byb = C.bank()
                    for kc in range(4):
                        S.op("pe", lambda e, kc=kc, ft=ft, byb=byb, t0=t0: e.matmul(C.PA[byb][:, :], lhsT=WOB[:, kc, ft * 128:(ft + 1) * 128],
                                                                                rhs=C.OG[:, kc, t0:t0 + 512], start=(kc == 0), stop=(kc == 3)),
                             reads=["og", "w"], writes=[("pa", byb)])
                    S.op("dve", lambda e, byb=byb, i2=i2: e.tensor_tensor(out=MT2[i2][:], in0=C.PA[byb][:, :], in1=GB[i2][:], op=ALU.mult),
                         reads=[("pa", byb), ("gb", i2)], writes=[("mt2", i2)])
                    S.op("pool", lambda e, i2=i2, ft=ft: e.tensor_tensor(out=MER[:, ft, :], in0=MT1[i2][:], in1=MT2[i2][:], op=ALU.add),
                         reads=[("mt1", i2), ("mt2", i2)], writes=[("mer", ft)])
                for j in range(4):
                    tb = j * 128
                    ta = t0 + tb
                    xi = (st * 4 + j) % 2
                    S.op("sp", lambda e, xi=xi, ta=ta: e.dma_start(out=XTOK[xi][:], in_=C.x[ta:ta + 128, :]), writes=[("xtok", xi)], dma=True)
                    banks = []
                    for half in range(2):
                        bi_ = C.bank()
                        banks.append(bi_)
                        bo = C.PA[bi_]
                        bkey = ("pa", bi_)
                        for ft in range(8):
                            S.op("pe", lambda e, ft=ft, half=half, bo=bo, tb=tb: e.matmul(bo[:, :], lhsT=MER[:, ft, tb:tb + 128], rhs=WO[:, ft, half * 512:(half + 1) * 512],
                                                                                     start=(ft == 0), stop=(ft == 7)),
                                 reads=[("mer", ft), "w"], writes=[bkey])
                        S.op("act", lambda e, half=half, bo=bo: e.activation(out=SCR[:], in_=bo[:, :], func=AF.Square, accum_out=ST2[:, half:half + 1]),
                             reads=[bkey], writes=["scr", ("st2", half)])
                    S.op("dve", lambda e: e.tensor_tensor(out=ST2[:, 2:3], in0=ST2[:, 0:1], in1=ST2[:, 1:2], op=ALU.add),
                         reads=[("st2", 0), ("st2", 1)], writes=[("st2", 2)])
                    S.op("act", lambda e: e.activation(out=ST2[:, 3:4], in_=ST2[:, 2:3], func=AF.Ln, scale=1.0 / D, bias=C.eps_norm[:, 0:1]),
                         reads=[("st2", 2), "consts"], writes=[("st2", 3)])
                    S.op("act", lambda e: e.activation(out=ST2[:, 4:5], in_=ST2[:, 3:4], func=AF.Exp, scale=-0.5), reads=[("st2", 3)], writes=[("st2", 4)])
                    for half in range(2):
                        bo = C.PA[banks[half]]
                        bkey = ("pa", banks[half])
                        hs = slice(half * 512, (half + 1) * 512)
                        S.op("dve", lambda e, bo=bo, hs=hs, xi=xi: e.scalar_tensor_tensor(out=OTOK[xi][:, hs], in0=bo[:, :], scalar=ST2[:, 4:5], in1=GPB[:, hs],
                                                                                     op0=ALU.mult, op1=ALU.mult),
                             reads=[bkey, ("st2", 4), "vecs"], writes=[("otok", xi, half)])
                        S.op("pool", lambda e, hs=hs, xi=xi: e.tensor_tensor(out=OTOK[xi][:, hs], in0=OTOK[xi][:, hs], in1=XTOK[xi][:, hs], op=ALU.add),
                             reads=[("otok", xi, half), ("xtok", xi)], writes=[("otok", xi, half)])
                    S.op("sp", lambda e, xi=xi, ta=ta: e.dma_start(out=C.out[ta:ta + 128, :], in_=OTOK[xi][:]),
                         reads=[("otok", xi, 0), ("otok", xi, 1)], writes=["outdram"], dma=True)
            S.finalize_and_emit(block)


def build(T=4096, phases="ABC", debug=False, max_ops=None):
    nc = bass.Bass("TRN2", target_bir_lowering=False)
    Sched.max_ops = max_ops
    C = Ctx()
    dt = lambda name, shape, dtp=F32: nc.dram_tensor(name, shape, dtp, kind="ExternalInput").ap()
    C.xT = dt("xT", [D, T])
    C.x = dt("x", [T, D])
    C.positions = dt("positions", [T], I32)
    C.g_pre = dt("g_pre", [D])
    C.w_in = dt("w_in", [D, IN_COLS])
    C.b_gate = dt("b_gate", [2 * D])
    C.mu_shift = dt("mu_shift", [1664])
    C.w0 = dt("w0", [512])
    C.w_decay_up = dt("w_decay_up", [64, 512])
    C.a0 = dt("a0", [512])
    C.w_iclr_up = dt("w_iclr_up", [64, 512])
    C.k_k = dt("k_k", [512])
    C.k_a = dt("k_a", [512])
    C.r_k = dt("r_k", [8, 64])
    C.gn_gain = dt("gn_gain", [512])
    C.gn_bias = dt("gn_bias", [512])
    C.w_out_a = dt("w_out_a", [512, D])
    C.g_q = dt("g_q", [256])
    C.w_uq = dt("w_uq", [256, 768])
    C.g_kv = dt("g_kv", [128])
    C.w_ukv = dt("w_ukv", [128, 1024])
    C.w_out_b = dt("w_out_b", [512, D])
    C.w_o = dt("w_o", [D, D])
    C.g_post = dt("g_post", [D])
    C.rope_invf = dt("rope_invf", [32])
    C.out = nc.dram_tensor("out", [T, D], F32, kind="ExternalOutput").ap()
    dbg = None
    if debug:
        dbg = {"yz": nc.dram_tensor("dbg_yz", [512, T], BF16, kind="ExternalOutput").ap(),
               "og": nc.dram_tensor("dbg_og", [512, T], BF16, kind="ExternalOutput").ap()}

    with ExitStack() as es:
        S = Sched(nc, es)
        sb = lambda name, shape, dtp: es.enter_context(nc.sbuf_tensor(name, shape, dtp))
        C.ident = sb("ident", [128, 128], BF16)
        C.identf = sb("identf", [128, 128], F32)
        C.bones = sb("bones", [128, 128], BF16)
        C.ones_bf = sb("ones_bf", [128, 128], BF16)
        C.eps_norm = sb("eps_norm", [128, 1], F32)
        C.eps_gn = sb("eps_gn", [128, 1], F32)
        C.gpre = sb("gpre", [128, NC_], F32)
        C.YZ = sb("YZ", [128, 4, T], BF16)
        C.ptb_i = 0
        C.ostg_i = 0
        C.PA = [es.enter_context(nc.psum_tensor("PA%d" % i, [128, 512], F32)) for i in range(5)]
        C.PT = es.enter_context(nc.psum_tensor("PT", [128, 1024], BF16))
        C.PY = es.enter_context(nc.psum_tensor("PY", [128, 512], F32))
        C.PS = es.enter_context(nc.psum_tensor("PS", [128, 512], F32))
        C.bank_i = 0

        C.pool = list(range(len(C.PA)))
        C.pool_ctr = {}

        def bank():
            key = tuple(C.pool)
            i = C.pool_ctr.get(key, 0)
            C.pool_ctr[key] = i + 1
            return C.pool[i % len(C.pool)]
        C.bank = bank

        with nc.allow_non_contiguous_dma(reason="small per-feature vectors"):
            with nc.Block() as block:
                S.op("pool", lambda e: e.memset(C.identf[:], 0.0), writes=["identf"])
                S.op("pool", lambda e: e.affine_select(out=C.identf[:], in_=C.identf[:], pattern=[[-1, 128]],
                                                       compare_op=ALU.not_equal, fill=1.0, base=0, channel_multiplier=1),
                     reads=["identf"], writes=["identf"])
                S.op("dve", lambda e: e.tensor_copy(out=C.ident[:], in_=C.identf[:]), reads=["identf"], writes=["consts"])
                S.op("pool", lambda e: e.memset(C.ones_bf[:], 1.0), writes=["consts"])
                S.op("pool", lambda e: e.memset(C.bones[:], 0.0), writes=["consts"])
                S.op("pool", lambda e: e.memset(C.bones[0:64, 0:64], 1.0), writes=["consts"])
                S.op("pool", lambda e: e.memset(C.bones[64:128, 64:128], 1.0), writes=["consts"])
                S.op("pool", lambda e: e.memset(C.eps_norm[:], NORM_EPS), writes=["consts"])
                S.op("pool", lambda e: e.memset(C.eps_gn[:], GN_EPS), writes=["consts"])
                _col_load(S, nc, C.gpre[:, :], C.g_pre, NC_, "vecs")
                S.finalize_and_emit(block)
            if "A" in phases:
                phase_A(nc, S, C, T, dbg)
            C.OG = sb("OG", [128, 4, T], BF16)
            if "B" in phases:
                phase_B(nc, S, C, T, dbg)
            if "C" in phases:
                phase_C(nc, S, C, T, dbg)
    C.S = S
    nc._ctx = C
    return nc


def rope_invf():
    inv = (np.float32(ROPE_THETA) ** (-np.arange(0, 32, 2, dtype=np.float32) / np.float32(32))).astype(np.float32)
    return np.concatenate([inv, inv]).astype(np.float32)


_CACHE = {}


def kernel(**inputs):
    x = np.ascontiguousarray(np.asarray(inputs["x"], dtype=np.float32))
    B, T, _ = x.shape
    if "nc" not in _CACHE:
        _CACHE["nc"] = build(T=T, phases="ABC", debug=False)
    nc = _CACHE["nc"]
    pos = np.ascontiguousarray(np.asarray(inputs["positions"]).astype(np.int32))
    wnames = ["g_pre", "w_in", "b_gate", "mu_shift", "w0", "w_decay_up", "a0", "w_iclr_up", "k_k", "k_a", "r_k",
              "gn_gain", "gn_bias", "w_out_a", "g_q", "w_uq", "g_kv", "w_ukv", "w_out_b", "w_o", "g_post"]
    shared = {k: np.ascontiguousarray(np.asarray(inputs[k], dtype=np.float32)) for k in wnames}
    shared["rope_invf"] = rope_invf()
    in_maps = []
    for b in range(B):
        m = dict(shared)
        m["x"] = x[b]
        m["xT"] = np.ascontiguousarray(x[b].T)
        m["positions"] = pos[b]
        in_maps.append(m)
    res = run_bass_kernel_spmd(nc, in_maps, core_ids=list(range(B)))
    return np.stack([np.asarray(r["out"], dtype=np.float32) for r in res.results], axis=0)
```

```python
import numpy as np
from contextlib import ExitStack
import concourse.bass as bass
import concourse.mybir as mybir
from concourse.bass_utils import run_bass_kernel_spmd

F32 = mybir.dt.float32
BF16 = mybir.dt.bfloat16
I32 = mybir.dt.int32
AF = mybir.ActivationFunctionType
ALU = mybir.AluOpType
AX = mybir.AxisListType

D = 1024
NC_ = 8
HEADS = 8
DECAY_SCALE = 0.6065306597
GN_EPS = 64e-5
NORM_EPS = 1e-6
ROPE_THETA = 10000.0
IN_COLS = 5152
C_R, C_K, C_V, C_WD, C_AD = 0, 512, 1024, 1536, 1600
C_ZA = 1664
C_CQ = 2176
C_CKV = 2432
C_KPE = 2560
C_ZB = 2592
C_GATE = 3104


class _Rec:
    def __getattr__(self, name):
        def f(*a, **k):
            self.__dict__["call"] = (name, a, k)
            return self
        return f


class Sched:
    ENG = ["pe", "dve", "act", "pool", "sp"]

    def __init__(self, nc, es, n_dma_sems=16):
        self.nc = nc
        self.sem = {e: es.enter_context(nc.semaphore("s_" + e)) for e in self.ENG}
        self.cnt = {e: 0 for e in self.ENG}
        self.dma_sems = [es.enter_context(nc.semaphore("s_dma%d" % i)) for i in range(n_dma_sems)]
        self.dma_cnt = [0] * n_dma_sems
        self.dma_rr = 0
        self.n_sw = 4
        self.dma_rr_sw = 0
        self.waited = {e: {} for e in self.ENG}
        self.ops = []
        self.writers = {}
        self.readers = {}
        self.nops = 0

    max_ops = None
    log = None
    defer = None
    deferred = None

    def op(self, eng, fn, reads=(), writes=(), dma=False, force=False):
        if self.max_ops is not None and len(self.ops) >= self.max_ops and not force:
            return None
        if self.defer is not None:
            rec = _Rec()
            fn(rec)
            self.defer.append((eng, rec.call, tuple(reads), tuple(writes), dma))
            return None
        return self._op(eng, fn, reads, writes, dma)

    def flush(self, n=None):
        lst = self.deferred
        k = len(lst) if n is None else min(n, len(lst))
        for _ in range(k):
            eng, call, reads, writes, dma = lst.pop(0)
            self._op(eng, None, reads, writes, dma, call=call)
        return len(lst)

    def _op(self, eng, fn, reads=(), writes=(), dma=False, call=None):
        idx = len(self.ops)
        deps = set()
        for k in reads:
            deps.update(self.writers.get(k, ()))
        for k in writes:
            deps.update(self.readers.get(k, ()))
            deps.update(self.writers.get(k, ()))
        for k in writes:
            if self.readers.get(k):
                self.writers[k] = [idx]
                self.readers[k] = []
            else:
                lst = self.writers.setdefault(k, [])
                lst.append(idx)
                if len(lst) > 40:
                    del lst[0]
        for k in reads:
            lst = self.readers.setdefault(k, [])
            lst.append(idx)
            if len(lst) > 40:
                del lst[0]
        if call is None:
            rec = _Rec()
            fn(rec)
            call = rec.call
        fn2 = lambda e_, call=call: getattr(e_, call[0])(*call[1], **call[2])
        self.ops.append(dict(eng=eng, fn=fn2, dma=dma, deps=deps, need_inc=False, idx=idx, name=call[0]))
        if self.log is not None:
            self.log.append((idx, eng, call[0], [k for k in writes]))
        return idx

    def finalize_and_emit(self, block, barrier=True):
        ops = self.ops
        self.nops += len(ops)
        for o in ops:
            pd = []
            for d in o["deps"]:
                p = ops[d]
                if p["eng"] == "pe" and o["eng"] == "pe" and not p["dma"] and not o["dma"]:
                    continue
                pd.append(p)
                p["need_inc"] = True
            o["pdeps"] = pd
        for o in ops:
            if o["dma"]:
                if o["eng"] == "pool":
                    s = self.dma_rr_sw % self.n_sw
                    self.dma_rr_sw += 1
                else:
                    s = self.n_sw + self.dma_rr % (len(self.dma_sems) - self.n_sw)
                    self.dma_rr += 1
                o["prev_val"] = self.dma_cnt[s]
                self.dma_cnt[s] += 16
                o["sem"] = self.dma_sems[s]
                o["sem_key"] = "dma%d" % s
                o["val"] = self.dma_cnt[s]
            else:
                if o["need_inc"]:
                    self.cnt[o["eng"]] += 1
                o["sem"] = self.sem[o["eng"]]
                o["sem_key"] = o["eng"]
                o["val"] = self.cnt[o["eng"]] if o["need_inc"] else None
        final = dict(self.cnt)
        final_dma = list(self.dma_cnt)
        by_eng = {e: [o for o in ops if o["eng"] == e] for e in self.ENG}

        def emit(e, eng):
            waited = self.waited[e]
            for o in by_eng[e]:
                need = {}
                for p in o["pdeps"]:
                    k = p["sem_key"]
                    if need.get(k, (None, 0))[1] < p["val"]:
                        need[k] = (p["sem"], p["val"])
                if o["dma"]:
                    k = o["sem_key"]
                    if o["prev_val"] > 0 and need.get(k, (None, 0))[1] < o["prev_val"]:
                        need[k] = (o["sem"], o["prev_val"])
                for k, (s, v) in need.items():
                    if waited.get(k, 0) < v:
                        eng.wait_ge(s, v)
                        waited[k] = v
                ins = o["fn"](eng)
                if o["dma"]:
                    ins.then_inc(o["sem"], 16)
                elif o["need_inc"]:
                    ins.then_inc(o["sem"], 1)
            if barrier:
                for k in self.ENG:
                    if k != e and final[k] > waited.get(k, 0):
                        eng.wait_ge(self.sem[k], final[k])
                        waited[k] = final[k]
                for i, v in enumerate(final_dma):
                    k = "dma%d" % i
                    if v > waited.get(k, 0):
                        eng.wait_ge(self.dma_sems[i], v)
                        waited[k] = v

        @block.tensor
        def _(eng):
            emit("pe", eng)

        @block.vector
        def _(eng):
            emit("dve", eng)

        @block.scalar
        def _(eng):
            emit("act", eng)

        @block.gpsimd
        def _(eng):
            emit("pool", eng)

        @block.sync
        def _(eng):
            emit("sp", eng)

        self.ops = []
        self.writers = {}
        self.readers = {}


class Ctx:
    pass


def _col_load(S, nc, dst, src_vec, ncols, key):
    S.op("sp", lambda e: e.dma_start(out=dst, in_=src_vec.rearrange("(c p) -> p c", p=128)), writes=[key], dma=True)


def make_u(S, C, st):
    t0 = st * 512
    sl = st % len(C.XGS)
    XG, RSTD = C.XGS[sl], C.RSTDS[sl]
    bank = C.bank()
    for dc in range(NC_):
        xs = C.XS[C.xs_i % len(C.XS)]
        xk = ("xs", C.xs_i % len(C.XS))
        C.xs_i += 1
        S.op("sp", lambda e, xs=xs, dc=dc: e.dma_start(out=xs[:], in_=C.xT[dc * 128:(dc + 1) * 128, t0:t0 + 512]),
             writes=[xk], dma=True)
        sq = C.SQ[dc % 2]
        S.op("act", lambda e, xs=xs, sq=sq: e.activation(out=sq[:], in_=xs[:], func=AF.Square),
             reads=[xk], writes=[("sq", dc % 2)])
        S.op("pe", lambda e, sq=sq, dc=dc, bank=bank: e.matmul(C.PA[bank][:, :], lhsT=C.ones_bf[:, :], rhs=sq[:],
                                                             start=(dc == 0), stop=(dc == NC_ - 1)),
             reads=[("sq", dc % 2), "consts"], writes=[("pa", bank)])
        S.op("dve", lambda e, xs=xs, dc=dc: e.tensor_scalar(out=XG[:, dc, :], in0=xs[:], scalar1=C.gpre[:, dc:dc + 1], scalar2=None,
                                                           op0=ALU.mult),
             reads=[xk, "vecs"], writes=[("xg", sl, dc)])
    S.op("act", lambda e, bank=bank: e.activation(out=RSTD[:], in_=C.PA[bank][:, :], func=AF.Ln,
                                                  scale=1.0 / D, bias=C.eps_norm[:, 0:1]),
         reads=[("pa", bank), "consts"], writes=[("rstd", sl)])
    S.op("act", lambda e: e.activation(out=RSTD[:], in_=RSTD[:], func=AF.Exp, scale=-0.5),
         reads=[("rstd", sl)], writes=[("rstd", sl)])
    C.cur_slot = sl
    C.XG, C.RSTD = XG, RSTD


def proj(S, C, W, col0, ncols=128):
    bank = C.bank()
    for dc in range(NC_):
        S.op("pe", lambda e, dc=dc, bank=bank: e.matmul(C.PA[bank][0:ncols, :], lhsT=W[:, dc, col0:col0 + ncols],
                                                       rhs=C.XG[:, dc, :], start=(dc == 0), stop=(dc == NC_ - 1)),
             reads=[("xg", C.cur_slot, dc), "w"], writes=[("pa", bank)])
    return bank


def phase_A(nc, S, C, T, dbg=None):
    NST = T // 512
    with ExitStack() as es:
        def sb(name, shape, dt):
            return es.enter_context(nc.sbuf_tensor(name, shape, dt))

        WA = sb("WA", [128, NC_, 2176], BF16)
        LORA = sb("LORA", [128, 512], BF16)
        C.XS = [sb("xs%d" % i, [128, 512], F32) for i in range(2)]
        C.xs_i = 0
        C.SQ = [sb("sq%d" % i, [128, 512], BF16) for i in range(2)]
        C.XGS = [sb("XG", [128, NC_, 512], BF16)]
        C.RSTDS = [sb("RSTD", [128, 512], F32)]
        MU = sb("MU", [128, 13], F32)
        W0 = sb("W0", [128, 4], F32)
        A0 = sb("A0", [128, 4], F32)
        KK_ = sb("KKv", [128, 4], F32)
        KA = sb("KA", [128, 4], F32)
        OMKA = sb("OMKA", [128, 4], F32)
        RK = sb("RK", [128, 4], F32)
        GNG = sb("GNG", [128, 512], F32)
        GNB = sb("GNB", [128, 512], F32)
        CARRY = sb("CARRY", [128, 13], F32)
        MX = sb("MX", [128, 4, 128], BF16)
        MZ = sb("MZ", [128, 8, 64], BF16)
        I2 = sb("I2", [128, 64], F32)
        HSEL = sb("HSEL", [128, 4, 8], BF16)
        SMASK = sb("SMASK", [128, 512], F32)
        TMPM = sb("TMPM", [128, 2, 64], F32)
        TMPM2 = sb("TMPM2", [128, 2, 64], F32)
        R3 = [sb("R%d" % i, [128, 516], F32) for i in range(2)]
        DD = [sb("DD%d" % i, [128, 512], F32) for i in range(1)]
        FT = [sb("FT%d" % i, [128, 512], F32) for i in range(11)]
        TA = sb("TA", [128, 512], BF16)
        SQK = sb("SQK", [128, 512], BF16)
        AR = sb("AR", [128, 4, 2, 512], BF16)
        BT = sb("BT", [128, 4, 512], BF16)
        KT = sb("KT", [128, 4, 512], BF16)
        VV = sb("VV", [128, 4, 512], BF16)
        RKR = sb("RKR", [128, 4, 512], BF16)
        SZ = sb("SZ", [128, 4, 512], BF16)
        WCS = sb("WCS", [128, 4, 8], F32)
        NB = [sb("NB%d" % i, [128, 8, 192], BF16) for i in range(2)]
        KB = [sb("KB%d" % i, [128, 8, 192], BF16) for i in range(2)]
        N1T = [sb("N1T%d" % i, [128, 8, 64], BF16) for i in range(2)]
        NP = [sb("NP%d" % i, [128, 8, 64], BF16) for i in range(4)]
        NPT = [sb("NPT%d" % i, [128, 8, 64], BF16) for i in range(4)]
        XT = [sb("XT%d" % i, [128, 8, 128], BF16) for i in range(2)]
        VMT = [sb("VMT%d" % i, [128, 8, 64], BF16) for i in range(2)]
        GY = [sb("GY%d" % i, [128, 4, 2, 64], BF16) for i in range(2)]
        GS = [sb("GS%d" % i, [128, 4, 2, 128], BF16) for i in range(2)]
        SLBD = [sb("SLBD%d" % i, [128, 4, 2, 128], F32) for i in range(2)]
        STT = [sb("ST%d" % i, [128, 4, 128], BF16) for i in range(2)]
        STMP = sb("STMP", [128, 4, 128], F32)
        YLOCS = [sb("YLOC%d" % i, [128, 512], F32) for i in range(2)]
        YSB = sb("YSB", [128, 512], F32)
        YN = sb("YN", [128, 512], F32)
        YBF = sb("YBF", [128, 512], BF16)
        STAT = sb("STAT", [128, 48], F32)
        BON = sb("BON", [128, 8], F32)

        C.sbuf_left_A = nc.sbuf_bytes_remaining
        with nc.Block() as block:
            for i, (a, b) in enumerate([(0, 1088), (1088, 2176)]):
                S.op("pool", lambda e, a=a, b=b: e.dma_start(
                    out=WA[:, :, a:b], in_=C.w_in.rearrange("(c p) n -> p c n", p=128)[:, :, a:b]),
                    writes=["w"], dma=True)
            S.op("pool", lambda e: e.dma_start(out=LORA[0:64, :], in_=C.w_decay_up), writes=["w"], dma=True)
            S.op("pool", lambda e: e.dma_start(out=LORA[64:128, :], in_=C.w_iclr_up), writes=["w"], dma=True)
            _col_load(S, nc, MU[:, :], C.mu_shift, 13, "vecs")
            _col_load(S, nc, W0[:, :], C.w0, 4, "vecs")
            _col_load(S, nc, A0[:, :], C.a0, 4, "vecs")
            _col_load(S, nc, KK_[:, :], C.k_k, 4, "vecs")
            _col_load(S, nc, KA[:, :], C.k_a, 4, "vecs")
            _col_load(S, nc, RK[:, :], C.r_k.rearrange("h n -> (h n)"), 4, "vecs")
            S.op("sp", lambda e: e.dma_start(out=GNG[:], in_=C.gn_gain.partition_broadcast(128)), writes=["vecs"], dma=True)
            S.op("sp", lambda e: e.dma_start(out=GNB[:], in_=C.gn_bias.partition_broadcast(128)), writes=["vecs"], dma=True)
            S.op("dve", lambda e: e.tensor_scalar(out=OMKA[:], in0=KA[:], scalar1=-1.0, scalar2=1.0, op0=ALU.mult, op1=ALU.add),
                 reads=["vecs"], writes=["vecs2"])
            S.op("pool", lambda e: e.memset(CARRY[:], 0.0), writes=["carry"])
            for i in range(2):
                S.op("pool", lambda e, i=i: e.memset(R3[i][:], 0.0), writes=[("R", i)])
            S.op("pool", lambda e: e.memset(STT[0][:], 0.0), writes=[("st", 0)])
            for i in range(2):
                S.op("pool", lambda e, i=i: e.memset(GS[i][:], 0.0), writes=[("gs", i)])
                S.op("pool", lambda e, i=i: e.memset(SLBD[i][:], 0.0), writes=[("slbd", i)])
            S.op("pool", lambda e: e.memset(SMASK[:], 1.0), writes=["masks"])
            S.op("pool", lambda e: e.memset(SMASK[:].rearrange("p (c t) -> p c t", t=64)[:, :, 0:1], 0.0), writes=["masks"])
            S.op("pool", lambda e: e.memset(TMPM2[:], 1.0), writes=["tmpm2"])

            def sel(cmp_, sign):
                return lambda e: e.affine_select(out=TMPM[:], in_=TMPM2[:], pattern=[[64 * sign, 2], [sign, 64]],
                                                 compare_op=cmp_, fill=0.0, base=0, channel_multiplier=-sign)

            def halves(dst_fn, bshape):
                for half in range(2):
                    ps_ = slice(half * 64, half * 64 + 64)
                    src = TMPM[ps_, half, :]
                    if bshape is not None:
                        src = src.unsqueeze(1).to_broadcast([64, bshape, 64])
                    S.op("dve", lambda e, ps_=ps_, src=src: e.tensor_copy(out=dst_fn(ps_), in_=src), reads=["tmpm"], writes=["masks"])
            S.op("pool", sel(ALU.is_gt, 1), reads=["tmpm2"], writes=["tmpm"])
            halves(lambda ps_: MX[ps_, :, 0:64], 4)
            S.op("pool", sel(ALU.is_ge, 1), reads=["tmpm2", "masks"], writes=["tmpm"])
            halves(lambda ps_: MX[ps_, :, 64:128], 4)
            S.op("pool", sel(ALU.is_gt, -1), reads=["tmpm2", "masks"], writes=["tmpm"])
            halves(lambda ps_: MZ[ps_, :, :], 8)
            S.op("pool", sel(ALU.is_equal, 1), reads=["tmpm2", "masks"], writes=["tmpm"])
            halves(lambda ps_: I2[ps_, :], None)
            S.op("pool", lambda e: e.memset(HSEL[:], 0.0), writes=["masks"])
            for hp in range(4):
                for half in range(2):
                    S.op("pool", lambda e, hp=hp, half=half: e.memset(HSEL[half * 64:half * 64 + 64, hp, 2 * hp + half:2 * hp + half + 1], 1.0),
                         writes=["masks"])

            st_slot = 0
            for st in range(NST):
                make_u(S, C, st)
                def shifted(ft, col0, out_ap, okey):
                    bank = proj(S, C, WA, col0)
                    ri = ft % 2
                    Rb = R3[ri]
                    S.op("dve", lambda e: e.tensor_tensor(out=Rb[:, 1:513], in0=C.PA[bank][:, :], in1=C.RSTD[:], op=ALU.mult),
                         reads=[("pa", bank), ("rstd", C.cur_slot)], writes=[("R", ri)])
                    S.op("pool", lambda e: e.tensor_copy(out=Rb[:, 0:1], in_=CARRY[:, ft:ft + 1]), reads=["carry"], writes=[("R", ri)])
                    S.op("pool", lambda e: e.tensor_copy(out=CARRY[:, ft:ft + 1], in_=Rb[:, 512:513]), reads=[("R", ri)], writes=["carry"])
                    di = 0
                    S.op("pool", lambda e: e.tensor_tensor(out=DD[di][:], in0=Rb[:, 0:512], in1=Rb[:, 1:513], op=ALU.subtract),
                         reads=[("R", ri)], writes=[("dd", di)])
                    S.op("dve", lambda e: e.scalar_tensor_tensor(out=out_ap, in0=DD[di][:], scalar=MU[:, ft:ft + 1], in1=Rb[:, 1:513],
                                                                op0=ALU.mult, op1=ALU.add),
                         reads=[("dd", di), ("R", ri), "vecs"], writes=[okey])

                shifted(12, C_WD, FT[0][:], ("ft", 0))
                S.op("act", lambda e: e.activation(out=TA[0:64, :], in_=FT[0][0:64, :], func=AF.Tanh), reads=[("ft", 0)], writes=["ta"])
                S.op("act", lambda e: e.activation(out=TA[64:128, :], in_=FT[0][64:128, :], func=AF.Copy), reads=[("ft", 0)], writes=["ta"])
                for hp in range(4):
                    fs = slice(hp * 128, hp * 128 + 128)
                    bw = C.bank()
                    S.op("pe", lambda e, bw=bw, fs=fs: e.matmul(C.PA[bw][:, :], lhsT=LORA[0:64, fs], rhs=TA[0:64, :], start=True, stop=True),
                         reads=["ta", "w"], writes=[("pa", bw)])
                    ba = C.bank()
                    S.op("pe", lambda e, ba=ba, fs=fs: e.matmul(C.PA[ba][:, :], lhsT=LORA[64:128, fs], rhs=TA[64:128, :], start=True, stop=True),
                         reads=["ta", "w"], writes=[("pa", ba)])
                    WS, AS, CUM, CX, WT, WINV, WEX, RR, KKK, KN = [FT[i] for i in range(1, 11)]
                    T1, T2, T3 = WS, CX, CUM
                    kf = lambda i: ("ft", {11: 1, 12: 4, 13: 3}.get(i, i))
                    S.op("act", lambda e, bw=bw, hp=hp: e.activation(out=WS[:], in_=C.PA[bw][:, :], func=AF.Sigmoid, bias=W0[:, hp:hp + 1]),
                         reads=[("pa", bw), "vecs"], writes=[kf(1)])
                    S.op("act", lambda e, ba=ba, hp=hp: e.activation(out=AS[:], in_=C.PA[ba][:, :], func=AF.Sigmoid, bias=A0[:, hp:hp + 1]),
                         reads=[("pa", ba), "vecs"], writes=[kf(2)])
                    S.op("dve", lambda e: e.tensor_tensor_scan(out=CUM[:], data0=SMASK[:], data1=WS[:], initial=0.0, op0=ALU.mult, op1=ALU.add),
                         reads=[kf(1), "masks"], writes=[kf(3)])
                    S.op("pool", lambda e: e.tensor_tensor(out=CX[:], in0=CUM[:], in1=WS[:], op=ALU.subtract), reads=[kf(3), kf(1)], writes=[kf(4)])
                    S.op("act", lambda e: e.activation(out=WT[:], in_=CUM[:], func=AF.Exp, scale=-DECAY_SCALE), reads=[kf(3)], writes=[kf(5)])
                    S.op("act", lambda e: e.activation(out=WINV[:], in_=CUM[:], func=AF.Exp, scale=DECAY_SCALE), reads=[kf(3)], writes=[kf(6)])
                    S.op("act", lambda e: e.activation(out=WEX[:], in_=CX[:], func=AF.Exp, scale=-DECAY_SCALE), reads=[kf(4)], writes=[kf(7)])
                    S.op("pool", lambda e, hp=hp: e.tensor_copy(out=WCS[:, hp, :], in_=WT[:].rearrange("p (c t) -> p c t", t=64)[:, :, 63]),
                         reads=[kf(5)], writes=["wcs"])
                    shifted(hp, C_R + hp * 128, RR[:], kf(8))
                    shifted(4 + hp, C_K + hp * 128, KKK[:], kf(9))
                    shifted(8 + hp, C_V + hp * 128, VV[:, hp, :], ("vv", hp))
                    S.op("act", lambda e, hp=hp: e.activation(out=SQK[:], in_=KKK[:], func=AF.Square, scale=KK_[:, hp:hp + 1]),
                         reads=[kf(9), "vecs"], writes=["sqk"])
                    bs = C.bank()
                    S.op("pe", lambda e, bs=bs: e.matmul(C.PA[bs][:, :], lhsT=C.bones[:, :], rhs=SQK[:], start=True, stop=True),
                         reads=["sqk", "consts"], writes=[("pa", bs)])
                    S.op("dve", lambda e, bs=bs: e.tensor_scalar(out=T1[:], in0=C.PA[bs][:, :], scalar1=1e-24, scalar2=None, op0=ALU.max),
                         reads=[("pa", bs)], writes=[kf(11)])
                    S.op("act", lambda e: e.activation(out=T1[:], in_=T1[:], func=AF.Ln), reads=[kf(11)], writes=[kf(11)])
                    S.op("act", lambda e: e.activation(out=T1[:], in_=T1[:], func=AF.Exp, scale=-0.5), reads=[kf(11)], writes=[kf(11)])
                    S.op("dve", lambda e, hp=hp: e.scalar_tensor_tensor(out=KN[:], in0=KKK[:], scalar=KK_[:, hp:hp + 1], in1=T1[:],
                                                                       op0=ALU.mult, op1=ALU.mult),
                         reads=[kf(9), kf(11), "vecs"], writes=[kf(10)])
                    S.op("dve", lambda e, hp=hp: e.scalar_tensor_tensor(out=AR[:, hp, 0, :], in0=KN[:], scalar=-1.0, in1=WEX[:],
                                                                       op0=ALU.mult, op1=ALU.mult),
                         reads=[kf(10), kf(7)], writes=[("ar", hp)])
                    S.op("pool", lambda e: e.tensor_tensor(out=T2[:], in0=KN[:], in1=AS[:], op=ALU.mult), reads=[kf(10), kf(2)], writes=[kf(12)])
                    S.op("pool", lambda e, hp=hp: e.tensor_tensor(out=BT[:, hp, :], in0=T2[:], in1=WINV[:], op=ALU.mult),
                         reads=[kf(12), kf(6)], writes=[("bt", hp)])
                    S.op("pool", lambda e, hp=hp: e.tensor_scalar(out=T3[:], in0=AS[:], scalar1=KA[:, hp:hp + 1], scalar2=OMKA[:, hp:hp + 1],
                                                                 op0=ALU.mult, op1=ALU.add),
                         reads=[kf(2), "vecs", "vecs2"], writes=[kf(13)])
                    S.op("pool", lambda e: e.tensor_tensor(out=T3[:], in0=T3[:], in1=KKK[:], op=ALU.mult), reads=[kf(13), kf(9)], writes=[kf(13)])
                    S.op("pool", lambda e, hp=hp: e.tensor_tensor(out=KT[:, hp, :], in0=T3[:], in1=WINV[:], op=ALU.mult),
                         reads=[kf(13), kf(6)], writes=[("kt", hp)])
                    S.op("dve", lambda e, hp=hp: e.scalar_tensor_tensor(out=RKR[:, hp, :], in0=T3[:], scalar=RK[:, hp:hp + 1], in1=RR[:],
                                                                       op0=ALU.mult, op1=ALU.mult),
                         reads=[kf(13), kf(8), "vecs"], writes=[("rkr", hp)])
                    S.op("pool", lambda e, hp=hp: e.tensor_tensor(out=AR[:, hp, 1, :], in0=RR[:], in1=WT[:], op=ALU.mult),
                         reads=[kf(8), kf(5)], writes=[("ar", hp)])
                    bz = proj(S, C, WA, C_ZA + hp * 128)
                    S.op("dve", lambda e, bz=bz: e.tensor_tensor(out=T2[:], in0=C.PA[bz][:, :], in1=C.RSTD[:], op=ALU.mult),
                         reads=[("pa", bz), ("rstd", C.cur_slot)], writes=[kf(12)])
                    S.op("act", lambda e, hp=hp: e.activation(out=SZ[:, hp, :], in_=T2[:], func=AF.Silu), reads=[kf(12)], writes=[("sz", hp)])

                def tile_gen(j):
                    nonlocal st_slot
                    tb = j * 128
                    tt = st * 4 + j
                    sl_ = tt % 2
                    nb, kb, xt, vmt, gy, gs, slbd = NB[sl_], KB[sl_], XT[sl_], VMT[sl_], GY[sl_], GS[sl_], SLBD[sl_]
                    kn = lambda name: (name, sl_)
                    for rnd, pair in enumerate([((AR, 0), (BT, None)), ((KT, None), (VV, None))]):
                        for pi, (src, sub) in enumerate(pair):
                            for hp in range(4):
                                in_ap = src[:, hp, 0, tb:tb + 128] if sub is not None else src[:, hp, tb:tb + 128]
                                rk = {id(AR): ("ar", hp), id(BT): ("bt", hp), id(KT): ("kt", hp), id(VV): ("vv", hp)}[id(src)]
                                S.op("pe", lambda e, in_ap=in_ap, pi=pi, hp=hp: e.transpose(
                                    out=C.PT[:, pi * 512 + hp * 128: pi * 512 + hp * 128 + 128], in_=in_ap, identity=C.ident[:, :]),
                                    reads=[rk, "consts"], writes=["pt"])
                        if rnd == 0:
                            S.op("act", lambda e, xt=xt: e.activation(out=xt[:, :, 0:64], in_=C.PT[:, 0:512].rearrange("p (h k) -> p h k", k=64), func=AF.Copy),
                                 reads=["pt"], writes=[kn("xt")])
                            S.op("act", lambda e, nb=nb: e.activation(out=nb[:, :, 128:192], in_=C.PT[:, 512:1024].rearrange("p (h k) -> p h k", k=64), func=AF.Copy),
                                 reads=["pt"], writes=[kn("nb")])
                        else:
                            S.op("act", lambda e, kb=kb: e.activation(out=kb[:, :, 128:192], in_=C.PT[:, 0:512].rearrange("p (h k) -> p h k", k=64), func=AF.Copy),
                                 reads=["pt"], writes=[kn("kb")])
                            S.op("act", lambda e, vmt=vmt: e.activation(out=vmt[:, :, :], in_=C.PT[:, 512:1024].rearrange("p (h k) -> p h k", k=64), func=AF.Copy),
                                 reads=["pt"], writes=[kn("vmt")])
                    yield
                    def rows(h):
                        return slice((h % 2) * 64, (h % 2) * 64 + 64)
                    for par in range(2):
                        for (lsrc, dst, dkey) in ((BT, nb, "nb"), (KT, kb, "kb")):
                            bank = C.bank()
                            pv = C.PA[bank][:, :].rearrange("p (h c) -> p h c", c=128)
                            for hp in range(4):
                                h = 2 * hp + par
                                for e_ in range(2):
                                    c0 = tb + e_ * 64
                                    S.op("pe", lambda e, lsrc=lsrc, h=h, hp=hp, e_=e_, c0=c0, pv=pv: e.matmul(
                                        pv[e_ * 64:e_ * 64 + 64, hp, :], lhsT=lsrc[rows(h), hp, c0:c0 + 64],
                                        rhs=AR[rows(h), hp, :, c0:c0 + 64], start=True, stop=True),
                                        reads=[("ar", hp), ("bt", hp) if lsrc is BT else ("kt", hp)], writes=[("pa", bank)])
                            dview = dst[:].rearrange("p (hp two) c -> p hp two c", two=2)[:, :, par, 0:128]
                            S.op("dve", lambda e, dview=dview, bank=bank: e.tensor_tensor(
                                out=dview, in0=C.PA[bank][:, :].rearrange("p (h c) -> p h c", c=128), in1=MX[:], op=ALU.mult),
                                reads=[("pa", bank), "masks"], writes=[kn(dkey)])
                    n1t = N1T[sl_]
                    for par in range(2):
                        bankz = C.bank()
                        pz = C.PA[bankz][:, 0:256].rearrange("p (h c) -> p h c", c=64)
                        for hp in range(4):
                            h = 2 * hp + par
                            for e_ in range(2):
                                c0 = tb + e_ * 64
                                S.op("pe", lambda e, h=h, hp=hp, e_=e_, c0=c0, pz=pz: e.matmul(
                                    pz[e_ * 64:e_ * 64 + 64, hp, :], lhsT=AR[rows(h), hp, 0, c0:c0 + 64], rhs=BT[rows(h), hp, c0:c0 + 64],
                                    start=True, stop=True), reads=[("ar", hp), ("bt", hp)], writes=[("pa", bankz)])
                        dview = n1t[:].rearrange("p (hp two) c -> p hp two c", two=2)[:, :, par, :]
                        S.op("dve", lambda e, dview=dview, pz=pz: e.tensor_tensor(out=dview, in0=pz, in1=MZ[:, 0:4, :], op=ALU.mult),
                             reads=[("pa", bankz), "masks"], writes=[kn("n1t")])
                    yield
                    bankp = C.bank()
                    pp = C.PA[bankp][:, :].rearrange("p (h c) -> p h c", c=64)
                    for h in range(8):
                        for e_ in range(2):
                            er = slice(e_ * 64, e_ * 64 + 64)
                            S.op("pe", lambda e, h=h, er=er: e.matmul(pp[er, h, :], lhsT=kb[er, h, 0:64], rhs=vmt[er, h, :], start=True, stop=True),
                                 reads=[kn("kb"), kn("vmt")], writes=[("pa", bankp)])
                    S.op("act", lambda e, xt=xt, bankp=bankp: e.activation(out=xt[:, :, 64:128], in_=C.PA[bankp][:, :].rearrange("p (h c) -> p h c", c=64), func=AF.Copy),
                         reads=[("pa", bankp)], writes=[kn("xt")])
                    yield
                    ncur, ntcur = (nb, slice(0, 64)), (n1t, slice(0, 64))
                    ncur_key, ntcur_key = kn("nb"), kn("n1t")
                    for lvl in range(6):
                        for g in range(2):
                            bank = C.bank()
                            pa_ = C.PA[bank][:, :].rearrange("p (h c) -> p h c", c=128)
                            for h4 in range(4):
                                h = g * 4 + h4
                                for e_ in range(2):
                                    er = slice(e_ * 64, e_ * 64 + 64)
                                    S.op("pe", lambda e, h=h, h4=h4, er=er, pa_=pa_, ncur=ncur: e.matmul(
                                        pa_[er, h4, :], lhsT=ncur[0][er, h, ncur[1]], rhs=xt[er, h, :], start=True, stop=True),
                                        reads=[ncur_key, kn("xt")], writes=[("pa", bank)])
                            S.op("dve", lambda e, g=g, bank=bank, xt=xt: e.tensor_tensor(
                                out=xt[:, g * 4:g * 4 + 4, :], in0=C.PA[bank][:, :].rearrange("p (h c) -> p h c", c=128),
                                in1=xt[:, g * 4:g * 4 + 4, :], op=ALU.add),
                                reads=[("pa", bank), kn("xt")], writes=[kn("xt")])
                        if lvl < 5:
                            nn, nnt = NP[sl_ * 2 + lvl % 2], NPT[sl_ * 2 + lvl % 2]
                            nnk, nntk = ("np", sl_ * 2 + lvl % 2), ("npt", sl_ * 2 + lvl % 2)
                            b1 = C.bank()
                            p1 = C.PA[b1][:, :].rearrange("p (h c) -> p h c", c=64)
                            for h in range(8):
                                for e_ in range(2):
                                    er = slice(e_ * 64, e_ * 64 + 64)
                                    S.op("pe", lambda e, h=h, er=er, p1=p1, ncur=ncur, ntcur=ntcur: e.matmul(
                                        p1[er, h, :], lhsT=ntcur[0][er, h, ntcur[1]], rhs=ncur[0][er, h, ncur[1]], start=True, stop=True),
                                        reads=[ncur_key, ntcur_key], writes=[("pa", b1)])
                            S.op("act", lambda e, nn=nn, b1=b1: e.activation(out=nn[:], in_=C.PA[b1][:, :].rearrange("p (h c) -> p h c", c=64), func=AF.Copy),
                                 reads=[("pa", b1)], writes=[nnk])
                            b2 = C.bank()
                            p2 = C.PA[b2][:, :].rearrange("p (h c) -> p h c", c=64)
                            for h in range(8):
                                for e_ in range(2):
                                    er = slice(e_ * 64, e_ * 64 + 64)
                                    S.op("pe", lambda e, h=h, er=er, p2=p2, ncur=ncur, ntcur=ntcur: e.matmul(
                                        p2[er, h, :], lhsT=ncur[0][er, h, ncur[1]], rhs=ntcur[0][er, h, ntcur[1]], start=True, stop=True),
                                        reads=[ncur_key, ntcur_key], writes=[("pa", b2)])
                            S.op("act", lambda e, nnt=nnt, b2=b2: e.activation(out=nnt[:], in_=C.PA[b2][:, :].rearrange("p (h c) -> p h c", c=64), func=AF.Copy),
                                 reads=[("pa", b2)], writes=[nntk])
                            ncur, ntcur = (nn, slice(0, 64)), (nnt, slice(0, 64))
                            ncur_key, ntcur_key = nnk, nntk
                        yield
                    yield
                    for e_ in range(2):
                        er = slice(e_ * 64, e_ * 64 + 64)
                        c0 = tb + e_ * 64
                        bank = C.bank()
                        pg = C.PA[bank][:, :].rearrange("p (h c) -> p h c", c=128)
                        for h in range(8):
                            hp = h // 2
                            S.op("pe", lambda e, h=h, hp=hp, er=er, pg=pg: e.matmul(
                                pg[rows(h), hp, :], lhsT=xt[er, h, 0:64], rhs=nb[er, h, 64:192], start=True, stop=True),
                                reads=[kn("xt"), kn("nb")], writes=[("pa", bank)])
                        S.op("dve", lambda e, e_=e_, c0=c0, pg=pg: e.tensor_tensor(
                            out=gy[:, :, e_, :], in0=pg[:, :, 0:64], in1=AR[:, :, 1, c0:c0 + 64], op=ALU.add),
                            reads=[("pa", bank)] + [("ar", hp) for hp in range(4)], writes=[kn("gy")])
                        for par in range(2):
                            pr = slice(par * 64, par * 64 + 64)
                            S.op("dve", lambda e, e_=e_, pg=pg, pr=pr, par=par: e.tensor_tensor(
                                out=gs[pr, :, e_, par * 64:par * 64 + 64], in0=pg[pr, :, 64:128],
                                in1=I2[pr, :].unsqueeze(1).to_broadcast([64, 4, 64]), op=ALU.add),
                                reads=[("pa", bank), "masks"], writes=[kn("gs")])
                        bl = C.bank()
                        psl = C.PA[bl][:, :].rearrange("p (h c) -> p h c", c=128)
                        for h in range(8):
                            hp = h // 2
                            par = h % 2
                            S.op("pe", lambda e, h=h, hp=hp, par=par, er=er, psl=psl: e.matmul(
                                psl[rows(h), hp, par * 64:par * 64 + 64], lhsT=nb[er, h, 128:192], rhs=xt[er, h, 64:128], start=True, stop=False),
                                reads=[kn("nb"), kn("xt")], writes=[("pa", bl)])
                            S.op("pe", lambda e, h=h, hp=hp, par=par, er=er, psl=psl: e.matmul(
                                psl[rows(h), hp, par * 64:par * 64 + 64], lhsT=kb[er, h, 128:192], rhs=vmt[er, h, :], start=False, stop=True),
                                reads=[kn("kb"), kn("vmt")], writes=[("pa", bl)])
                        for par in range(2):
                            pr = slice(par * 64, par * 64 + 64)
                            S.op("act", lambda e, e_=e_, psl=psl, pr=pr, par=par: e.activation(
                                out=slbd[pr, :, e_, par * 64:par * 64 + 64], in_=psl[pr, :, par * 64:par * 64 + 64], func=AF.Copy),
                                reads=[("pa", bl)], writes=[kn("slbd")])
                    byl = C.bank()
                    pyl = C.PA[byl][:, :].rearrange("p (h c) -> p h c", c=64)
                    for e_ in range(2):
                        er = slice(e_ * 64, e_ * 64 + 64)
                        for h in range(8):
                            S.op("pe", lambda e, h=h, er=er: e.matmul(
                                pyl[er, h, :], lhsT=nb[er, h, 64:128], rhs=xt[er, h, 64:128], start=True, stop=False),
                                reads=[kn("nb"), kn("xt")], writes=[("pa", byl)])
                            S.op("pe", lambda e, h=h, er=er: e.matmul(
                                pyl[er, h, :], lhsT=kb[er, h, 64:128], rhs=vmt[er, h, :], start=False, stop=True),
                                reads=[kn("kb"), kn("vmt")], writes=[("pa", byl)])
                    S.op("act", lambda e: e.activation(out=YLOCS[sl_][:], in_=C.PA[byl][:, :], func=AF.Copy), reads=[("pa", byl)], writes=[kn("yloc")])
                    yield
                    py = C.PY[:, :].rearrange("p (h c) -> p h c", c=128)
                    for e_ in range(2):
                        er = slice(e_ * 64, e_ * 64 + 64)
                        cl = j * 2 + e_
                        stc = STT[st_slot]
                        stn = STT[1 - st_slot]
                        psv = C.PS[:, :].rearrange("p (h c) -> p h c", c=128)
                        for hp in range(4):
                            S.op("pe", lambda e, hp=hp, er=er, stc=stc, e_=e_: e.matmul(
                                py[er, hp, :], lhsT=gy[:, hp, e_, :], rhs=stc[:, hp, :], start=True, stop=True),
                                reads=[kn("gy"), ("st", st_slot)], writes=["py"])
                        for hp in range(4):
                            S.op("pe", lambda e, hp=hp, stc=stc, e_=e_: e.matmul(
                                psv[:, hp, :], lhsT=gs[:, hp, e_, :], rhs=stc[:, hp, :], start=True, stop=True),
                                reads=[kn("gs"), ("st", st_slot)], writes=["ps"])
                        S.op("dve", lambda e, e_=e_: e.tensor_tensor(out=STMP[:], in0=psv, in1=slbd[:, :, e_, :], op=ALU.add),
                             reads=["ps", kn("slbd")], writes=["stmp"])
                        S.op("dve", lambda e, stn=stn, cl=cl: e.tensor_tensor(
                            out=stn[:], in0=STMP[:], in1=WCS[:, :, cl:cl + 1].to_broadcast([128, 4, 128]), op=ALU.mult),
                            reads=["stmp", "wcs"], writes=[("st", 1 - st_slot)])
                        st_slot = 1 - st_slot
                    S.op("dve", lambda e: e.tensor_tensor(out=YSB[:], in0=C.PY[:, :], in1=YLOCS[sl_][:], op=ALU.add), reads=["py", kn("yloc")], writes=["ysb"])
                    S.op("act", lambda e: e.activation(out=YN[:], in_=YSB[:], func=AF.Square), reads=["ysb"], writes=["yn"])
                    S.op("dve", lambda e: e.tensor_reduce(out=STAT[:, 0:8], in_=YSB[:].rearrange("p (h c) -> p h c", c=64), op=ALU.add, axis=AX.X),
                         reads=["ysb"], writes=["stat0"])
                    S.op("dve", lambda e: e.tensor_reduce(out=STAT[:, 8:16], in_=YN[:].rearrange("p (h c) -> p h c", c=64), op=ALU.add, axis=AX.X),
                         reads=["yn"], writes=["stat1"])
                    S.op("dve", lambda e: e.tensor_scalar(out=STAT[:, 16:24], in0=STAT[:, 0:8], scalar1=1.0 / 64, scalar2=None, op0=ALU.mult),
                         reads=["stat0"], writes=["stat2"])
                    S.op("dve", lambda e: e.tensor_tensor(out=STAT[:, 24:32], in0=STAT[:, 16:24], in1=STAT[:, 16:24], op=ALU.mult),
                         reads=["stat2"], writes=["stat3"])
                    S.op("dve", lambda e: e.scalar_tensor_tensor(out=STAT[:, 32:40], in0=STAT[:, 8:16], scalar=1.0 / 64, in1=STAT[:, 24:32],
                                                                op0=ALU.mult, op1=ALU.subtract),
                         reads=["stat1", "stat3"], writes=["stat4"])
                    S.op("act", lambda e: e.activation(out=STAT[:, 32:40], in_=STAT[:, 32:40], func=AF.Ln, bias=C.eps_gn[:, 0:1]),
                         reads=["stat4", "consts"], writes=["stat4"])
                    S.op("act", lambda e: e.activation(out=STAT[:, 32:40], in_=STAT[:, 32:40], func=AF.Exp, scale=-0.5),
                         reads=["stat4"], writes=["stat4"])
                    S.op("dve", lambda e: e.scalar_tensor_tensor(out=STAT[:, 40:48], in0=STAT[:, 16:24], scalar=-1.0, in1=STAT[:, 32:40],
                                                                op0=ALU.mult, op1=ALU.mult),
                         reads=["stat2", "stat4"], writes=["stat5"])
                    v3 = lambda t: t[:].rearrange("p (h c) -> p h c", c=64)
                    S.op("dve", lambda e: e.tensor_tensor(out=v3(YN), in0=v3(YSB), in1=STAT[:, 32:40].unsqueeze(2).to_broadcast([128, 8, 64]), op=ALU.mult),
                         reads=["ysb", "stat4"], writes=["yn"])
                    S.op("pool", lambda e: e.tensor_tensor(out=v3(YN), in0=v3(YN), in1=STAT[:, 40:48].unsqueeze(2).to_broadcast([128, 8, 64]), op=ALU.add),
                         reads=["yn", "stat5"], writes=["yn"])
                    S.op("pool", lambda e: e.tensor_tensor(out=YN[:], in0=YN[:], in1=GNG[:], op=ALU.mult), reads=["yn", "vecs"], writes=["yn"])
                    S.op("pool", lambda e: e.tensor_tensor(out=YN[:], in0=YN[:], in1=GNB[:], op=ALU.add), reads=["yn", "vecs"], writes=["yn"])
                    bb_ = C.bank()
                    for hp in range(4):
                        S.op("pe", lambda e, hp=hp, bb_=bb_: e.matmul(C.PA[bb_][:, 0:8], lhsT=RKR[:, hp, tb:tb + 128], rhs=HSEL[:, hp, :],
                                                                     start=(hp == 0), stop=(hp == 3)),
                             reads=[("rkr", hp), "masks"], writes=[("pa", bb_)])
                    S.op("act", lambda e, bb_=bb_: e.activation(out=BON[:], in_=C.PA[bb_][:, 0:8], func=AF.Copy), reads=[("pa", bb_)], writes=["bon"])
                    S.op("dve", lambda e, vmt=vmt: e.tensor_tensor(out=v3(YLOCS[sl_]), in0=vmt[:], in1=BON[:].unsqueeze(2).to_broadcast([128, 8, 64]), op=ALU.mult),
                         reads=[kn("vmt"), "bon"], writes=[kn("yloc")])
                    S.op("pool", lambda e: e.tensor_tensor(out=YBF[:], in0=YN[:], in1=YLOCS[sl_][:], op=ALU.add), reads=["yn", kn("yloc")], writes=["ybf"])
                    for hp in range(4):
                        S.op("pe", lambda e, hp=hp: e.transpose(out=C.PT[:, hp * 128:hp * 128 + 128], in_=YBF[:, hp * 128:hp * 128 + 128], identity=C.ident[:, :]),
                             reads=["ybf", "consts"], writes=["pt"])
                    t_abs = st * 512 + tb
                    S.op("dve", lambda e, t_abs=t_abs, tb=tb: e.tensor_tensor(
                        out=C.YZ[:, :, t_abs:t_abs + 128], in0=C.PT[:, 0:512].rearrange("p (h c) -> p h c", c=128),
                        in1=SZ[:, :, tb:tb + 128], op=ALU.mult),
                        reads=["pt"] + [("sz", hp) for hp in range(4)], writes=["yz"])
                for pair in ((0, 1), (2, 3)):
                    gens = [tile_gen(j) for j in pair]
                    while gens:
                        for g in list(gens):
                            try:
                                next(g)
                            except StopIteration:
                                gens.remove(g)
            if dbg is not None and "yz" in dbg:
                S.op("sp", lambda e: e.dma_start(out=dbg["yz"].rearrange("(h p) t -> p h t", p=128), in_=C.YZ[:, :, 0:T]), reads=["yz"], dma=True, force=True)
            S.finalize_and_emit(block)


def phase_B(nc, S, C, T, dbg=None):
    NST = T // 512
    NKT = T // 128
    SCALE = 1.0 / float(np.sqrt(96.0))
    TWO_PI = float(2 * np.pi)
    with ExitStack() as es:
        def sb(name, shape, dt):
            return es.enter_context(nc.sbuf_tensor(name, shape, dt))
        WB = sb("WB", [128, NC_, 960], BF16)
        WUQ = sb("WUQ", [128, 2, 768], BF16)
        WUQS = sb("WUQS", [128, 2, 8, 32], BF16)
        WUKV = sb("WUKV", [128, 8, 128], BF16)
        C.XS = [sb("xsb%d" % i, [128, 512], F32) for i in range(2)]
        C.xs_i = 0
        C.SQ = [sb("sqb%d" % i, [128, 512], BF16) for i in range(2)]
        C.XGS = [sb("XGb", [128, NC_, 512], BF16)]
        C.RSTDS = [sb("RSTDb", [128, 512], F32)]
        GQ = sb("GQ", [128, 2], F32)
        GKV = sb("GKV", [128, 1], F32)
        INVF = sb("INVF", [128, 1], F32)
        KT = sb("KTb", [128, 4, T], BF16)
        VT = sb("VTb", [128, NKT, 4, 65], BF16)
        QTS = [sb("QTb%d" % i, [128, 4, 512], BF16) for i in range(2)]
        SZBS = [sb("SZB%d" % i, [128, 4, 512], BF16) for i in range(2)]
        CQ = [sb("CQ%d" % i, [128, 512], F32) for i in range(2)]
        CKV = CQ[0]
        CQN = sb("CQN", [128, 2, 512], BF16)
        CKVN = sb("CKVN", [128, 512], BF16)
        RQ = sb("RQ", [128, 512], F32)
        ANG = sb("ANG", [128, 512], F32)
        TR1 = sb("TR1", [128, 512], F32)
        TR2 = sb("TR2", [128, 512], F32)
        TRI = sb("TRI", [128, 512], I32)
        POSI = TRI
        COS = sb("COS", [128, 512], F32)
        SIN = sb("SIN", [128, 512], F32)
        CR = sb("CR", [128, 512], F32)
        SR = sb("SR", [128, 512], F32)
        KR = sb("KR", [128, 512], BF16)
        PTB = [sb("PTB%d" % i, [128, 512], BF16) for i in range(5)]
        RDEN = sb("RDEN", [128, 512], F32)
        RDENB = sb("RDENB", [128, 512], BF16)
        BCS = sb("BCS", [128, 512], BF16)
        OTMP = sb("OTMP", [128, 512], F32)
        OSTG = [sb("OSTG%d" % i, [128, 512], BF16) for i in range(2)]

        C.sbuf_left_B = nc.sbuf_bytes_remaining
        with nc.Block() as block:
            w_in_v = C.w_in.rearrange("(c p) n -> p c n", p=128)
            S.op("pool", lambda e: e.dma_start(out=WB[:, :, 0:928], in_=w_in_v[:, :, C_CQ:C_CQ + 928]), writes=["w"], dma=True)
            S.op("pool", lambda e: e.dma_start(out=WB[:, :, 928:944], in_=w_in_v[:, :, C_KPE + 16:C_KPE + 32]), writes=["w"], dma=True)
            S.op("pool", lambda e: e.dma_start(out=WB[:, :, 944:960], in_=w_in_v[:, :, C_KPE:C_KPE + 16]), writes=["w"], dma=True)
            S.op("dve", lambda e: e.tensor_scalar(out=WB[:, :, 928:944], in0=WB[:, :, 928:944], scalar1=-1.0, scalar2=None, op0=ALU.mult),
                 reads=["w"], writes=["w"])
            S.op("pool", lambda e: e.dma_start(out=WUQ[:], in_=C.w_uq.rearrange("(c p) n -> p c n", p=128)), writes=["w"], dma=True)
            S.op("pool", lambda e: e.dma_start(out=WUKV[:], in_=C.w_ukv.rearrange("p (h c) -> p h c", c=128)), writes=["w"], dma=True)
            wq4 = WUQ[:].rearrange("p c (h d) -> p c h d", d=96)
            S.op("dve", lambda e: e.tensor_scalar(out=WUQS[:, :, :, 0:16], in0=wq4[:, :, :, 80:96], scalar1=-1.0, scalar2=None, op0=ALU.mult),
                 reads=["w"], writes=["w2"])
            S.op("dve", lambda e: e.tensor_copy(out=WUQS[:, :, :, 16:32], in_=wq4[:, :, :, 64:80]), reads=["w"], writes=["w2"])
            _col_load(S, nc, GQ[:, :], C.g_q, 2, "vecs")
            _col_load(S, nc, GKV[:, :], C.g_kv, 1, "vecs")
            S.op("sp", lambda e: e.dma_start(out=INVF[64:96, :], in_=C.rope_invf.rearrange("(p o) -> p o", o=1)), writes=["vecs"], dma=True)
            S.op("pool", lambda e: e.memset(VT[:, :, :, 64:65], 1.0), writes=["vtones"])
            rr = slice(64, 96)

            def prologue_gen(hg, st):
                t0 = st * 512
                make_u(S, C, st)
                yield
                S.op("sp", lambda e, t0=t0: e.dma_start(out=POSI[rr, :], in_=C.positions[t0:t0 + 512].partition_broadcast(32)),
                     writes=["tri"], dma=True)
                S.op("dve", lambda e: e.tensor_copy(out=ANG[rr, :], in_=POSI[rr, :]), reads=["tri"], writes=["ang"])
                S.op("dve", lambda e: e.tensor_scalar(out=ANG[rr, :], in0=ANG[rr, :], scalar1=INVF[rr, 0:1], scalar2=None, op0=ALU.mult),
                     reads=["ang", "vecs"], writes=["ang"])
                for (dst, off, key) in ((SIN, 0.0, "sin"), (COS, 0.25, "cos")):
                    S.op("dve", lambda e, off=off: e.tensor_scalar(out=TR1[rr, :], in0=ANG[rr, :], scalar1=1.0 / TWO_PI, scalar2=off,
                                                                    op0=ALU.mult, op1=ALU.add), reads=["ang"], writes=["tr1"])
                    S.op("dve", lambda e: e.tensor_copy(out=TRI[rr, :], in_=TR1[rr, :]), reads=["tr1"], writes=["tri"])
                    S.op("dve", lambda e: e.tensor_copy(out=TR2[rr, :], in_=TRI[rr, :]), reads=["tri"], writes=["tr2"])
                    S.op("dve", lambda e: e.tensor_tensor(out=TR1[rr, :], in0=TR1[rr, :], in1=TR2[rr, :], op=ALU.subtract),
                         reads=["tr1", "tr2"], writes=["tr1"])
                    S.op("dve", lambda e: e.tensor_scalar(out=TR2[rr, :], in0=TR1[rr, :], scalar1=0.5, scalar2=None, op0=ALU.is_gt),
                         reads=["tr1"], writes=["tr2"])
                    S.op("dve", lambda e: e.tensor_tensor(out=TR1[rr, :], in0=TR1[rr, :], in1=TR2[rr, :], op=ALU.subtract),
                         reads=["tr1", "tr2"], writes=["tr1"])
                    S.op("dve", lambda e: e.tensor_scalar(out=TR2[rr, :], in0=TR1[rr, :], scalar1=-0.5, scalar2=None, op0=ALU.is_lt),
                         reads=["tr1"], writes=["tr2"])
                    S.op("dve", lambda e: e.tensor_tensor(out=TR1[rr, :], in0=TR1[rr, :], in1=TR2[rr, :], op=ALU.add),
                         reads=["tr1", "tr2"], writes=["tr1"])
                    S.op("act", lambda e, dst=dst: e.activation(out=dst[rr, :], in_=TR1[rr, :], func=AF.Sin, scale=TWO_PI),
                         reads=["tr1"], writes=[key])
                yield
                for i in range(2):
                    b = proj(S, C, WB, i * 128)
                    S.op("dve", lambda e, b=b, i=i: e.tensor_tensor(out=CQ[i][:], in0=C.PA[b][:, :], in1=C.RSTD[:], op=ALU.mult),
                         reads=[("pa", b), ("rstd", C.cur_slot)], writes=[("cq", i)])
                bss = C.bank()
                for i in range(2):
                    S.op("act", lambda e, i=i: e.activation(out=C.SQ[i][:], in_=CQ[i][:], func=AF.Square), reads=[("cq", i)], writes=[("sq", i)])
                    S.op("pe", lambda e, i=i, bss=bss: e.matmul(C.PA[bss][:, :], lhsT=C.ones_bf[:, :], rhs=C.SQ[i][:], start=(i == 0), stop=(i == 1)),
                         reads=[("sq", i), "consts"], writes=[("pa", bss)])
                S.op("act", lambda e, bss=bss: e.activation(out=RQ[:], in_=C.PA[bss][:, :], func=AF.Ln, scale=1.0 / 256, bias=C.eps_norm[:, 0:1]),
                     reads=[("pa", bss), "consts"], writes=["rq"])
                S.op("act", lambda e: e.activation(out=RQ[:], in_=RQ[:], func=AF.Exp, scale=-0.5), reads=["rq"], writes=["rq"])
                for i in range(2):
                    S.op("dve", lambda e, i=i: e.scalar_tensor_tensor(out=CQN[:, i, :], in0=CQ[i][:], scalar=GQ[:, i:i + 1], in1=RQ[:],
                                                                     op0=ALU.mult, op1=ALU.mult),
                         reads=[("cq", i), "rq", "vecs"], writes=[("cqn", i)])
                b = proj(S, C, WB, 256)
                S.op("dve", lambda e, b=b: e.tensor_tensor(out=CKV[:], in0=C.PA[b][:, :], in1=C.RSTD[:], op=ALU.mult),
                     reads=[("pa", b), ("rstd", C.cur_slot)], writes=[("cq", 0)])
                bss = C.bank()
                S.op("act", lambda e: e.activation(out=C.SQ[0][:], in_=CKV[:], func=AF.Square), reads=[("cq", 0)], writes=[("sq", 0)])
                S.op("pe", lambda e, bss=bss: e.matmul(C.PA[bss][:, :], lhsT=C.ones_bf[:, :], rhs=C.SQ[0][:], start=True, stop=True),
                     reads=[("sq", 0), "consts"], writes=[("pa", bss)])
                S.op("act", lambda e, bss=bss: e.activation(out=RQ[:], in_=C.PA[bss][:, :], func=AF.Ln, scale=1.0 / 128, bias=C.eps_norm[:, 0:1]),
                     reads=[("pa", bss), "consts"], writes=["rq"])
                S.op("act", lambda e: e.activation(out=RQ[:], in_=RQ[:], func=AF.Exp, scale=-0.5), reads=["rq"], writes=["rq"])
                S.op("dve", lambda e: e.scalar_tensor_tensor(out=CKVN[:], in0=CKV[:], scalar=GKV[:, 0:1], in1=RQ[:], op0=ALU.mult, op1=ALU.mult),
                     reads=[("cq", 0), "rq", "vecs"], writes=["ckvn"])
                yield
                S.op("dve", lambda e: e.tensor_tensor(out=CR[rr, :], in0=COS[rr, :], in1=C.RSTD[rr, :], op=ALU.mult), reads=["cos", ("rstd", C.cur_slot)], writes=["cr"])
                S.op("dve", lambda e: e.tensor_tensor(out=SR[rr, :], in0=SIN[rr, :], in1=C.RSTD[rr, :], op=ALU.mult), reads=["sin", ("rstd", C.cur_slot)], writes=["sr"])
                bk = C.bank()
                bks = C.bank()
                for dc in range(NC_):
                    S.op("pe", lambda e, dc=dc, bk=bk: e.matmul(C.PA[bk][64:96, :], lhsT=WB[:, dc, 384:416], rhs=C.XG[:, dc, :],
                                                               start=(dc == 0), stop=(dc == NC_ - 1)), reads=[("xg", C.cur_slot, dc), "w"], writes=[("pa", bk)])
                for dc in range(NC_):
                    S.op("pe", lambda e, dc=dc, bks=bks: e.matmul(C.PA[bks][64:96, :], lhsT=WB[:, dc, 928:960], rhs=C.XG[:, dc, :],
                                                                 start=(dc == 0), stop=(dc == NC_ - 1)), reads=[("xg", C.cur_slot, dc), "w"], writes=[("pa", bks)])
                S.op("dve", lambda e, bk=bk: e.tensor_tensor(out=TR1[rr, :], in0=C.PA[bk][rr, :], in1=CR[rr, :], op=ALU.mult),
                     reads=[("pa", bk), "cr"], writes=["tr1"])
                S.op("dve", lambda e, bks=bks: e.tensor_tensor(out=TR2[rr, :], in0=C.PA[bks][rr, :], in1=SR[rr, :], op=ALU.mult),
                     reads=[("pa", bks), "sr"], writes=["tr2"])
                S.op("dve", lambda e: e.tensor_tensor(out=KR[rr, :], in0=TR1[rr, :], in1=TR2[rr, :], op=ALU.add), reads=["tr1", "tr2"], writes=["kr"])
                S.op("dve", lambda e: e.tensor_scalar(out=CR[rr, :], in0=COS[rr, :], scalar1=SCALE, scalar2=None, op0=ALU.mult), reads=["cos"], writes=["cr"])
                S.op("dve", lambda e: e.tensor_scalar(out=SR[rr, :], in0=SIN[rr, :], scalar1=SCALE, scalar2=None, op0=ALU.mult), reads=["sin"], writes=["sr"])
                yield
                for hl in range(4):
                    h = hg * 4 + hl
                    bkn = C.bank()
                    S.op("pe", lambda e, h=h, bkn=bkn: e.matmul(C.PA[bkn][0:64, :], lhsT=WUKV[:, h, 0:64], rhs=CKVN[:], start=True, stop=True),
                         reads=["ckvn", "w"], writes=[("pa", bkn)])
                    S.op("dve", lambda e, hl=hl, bkn=bkn, t0=t0: e.tensor_copy(out=KT[0:64, hl, t0:t0 + 512], in_=C.PA[bkn][0:64, :]),
                         reads=[("pa", bkn)], writes=[("kt", hl, st)])
                    S.op("dve", lambda e, hl=hl, t0=t0: e.tensor_copy(out=KT[rr, hl, t0:t0 + 512], in_=KR[rr, :]), reads=["kr"], writes=[("kt", hl, st)])
                    bq = C.bank()
                    for kc in range(2):
                        S.op("pe", lambda e, kc=kc, h=h, bq=bq: e.matmul(C.PA[bq][0:96, :], lhsT=WUQ[:, kc, h * 96:h * 96 + 96], rhs=CQN[:, kc, :],
                                                                        start=(kc == 0), stop=(kc == 1)),
                             reads=[("cqn", kc), "w"], writes=[("pa", bq)])
                    bqs = C.bank()
                    for kc in range(2):
                        S.op("pe", lambda e, kc=kc, h=h, bqs=bqs: e.matmul(C.PA[bqs][64:96, :], lhsT=WUQS[:, kc, h, :], rhs=CQN[:, kc, :],
                                                                          start=(kc == 0), stop=(kc == 1)),
                             reads=[("cqn", kc), "w2"], writes=[("pa", bqs)])
                    S.op("dve", lambda e, hl=hl, bq=bq: e.tensor_scalar(out=QTS[st % 2][0:64, hl, :], in0=C.PA[bq][0:64, :], scalar1=SCALE, scalar2=None, op0=ALU.mult),
                         reads=[("pa", bq)], writes=[("qt", st % 2, hl)])
                    S.op("dve", lambda e, bq=bq: e.tensor_tensor(out=TR1[rr, :], in0=C.PA[bq][rr, :], in1=CR[rr, :], op=ALU.mult),
                         reads=[("pa", bq), "cr"], writes=["tr1"])
                    S.op("dve", lambda e, bqs=bqs: e.tensor_tensor(out=TR2[rr, :], in0=C.PA[bqs][rr, :], in1=SR[rr, :], op=ALU.mult),
                         reads=[("pa", bqs), "sr"], writes=["tr2"])
                    S.op("dve", lambda e, hl=hl: e.tensor_tensor(out=QTS[st % 2][rr, hl, :], in0=TR1[rr, :], in1=TR2[rr, :], op=ALU.add),
                         reads=["tr1", "tr2"], writes=[("qt", st % 2, hl)])
                    bz = proj(S, C, WB, 416 + h * 64, ncols=64)
                    S.op("dve", lambda e, bz=bz: e.tensor_tensor(out=CQ[1][0:64, :], in0=C.PA[bz][0:64, :], in1=C.RSTD[0:64, :], op=ALU.mult),
                         reads=[("pa", bz), ("rstd", C.cur_slot)], writes=[("cq", 1)])
                    S.op("act", lambda e, hl=hl: e.activation(out=SZBS[st % 2][0:64, hl, :], in_=CQ[1][0:64, :], func=AF.Silu), reads=[("cq", 1)], writes=[("szb", st % 2, hl)])
                    yield
                yield
                for j in range(4):
                    kt = st * 4 + j
                    bv = C.bank()
                    S.op("pe", lambda e, j=j, bv=bv: e.matmul(C.PA[bv][:, 0:256].rearrange("p (h c) -> p h c", c=64),
                                                           lhsT=CKVN[:, j * 128:(j + 1) * 128], rhs=WUKV[:, hg * 4:hg * 4 + 4, 64:128], start=True, stop=True),
                         reads=["ckvn", "w"], writes=[("pa", bv)])
                    S.op("dve", lambda e, kt=kt, bv=bv: e.tensor_copy(out=VT[:, kt, :, 0:64], in_=C.PA[bv][:, 0:256].rearrange("p (h c) -> p h c", c=64)),
                         reads=[("pa", bv)], writes=[("vt", st)])
                yield
            def attention_gen(hg, st):
                t0 = st * 512
                blocks = [(hl, kt) for hl in range(4) for kt in range(st * 4 + 4)]
                nkt = st * 4 + 4
                LOOK = 3
                binfo = {}
                tails1, tails2 = {}, {}

                def emit_qk(i):
                    hl, kt = blocks[i]
                    bs_ = C.bank()
                    S.op("pe", lambda e: e.matmul(C.PA[bs_][:, :], lhsT=KT[0:96, hl, kt * 128:(kt + 1) * 128], rhs=QTS[st % 2][0:96, hl, :],
                                                  start=True, stop=True),
                         reads=[("kt", hl, kt // 4), ("qt", st % 2, hl)], writes=[("pa", bs_)])
                    pi = C.ptb_i % len(PTB)
                    C.ptb_i += 1
                    ptb = PTB[pi]
                    S.op("act", lambda e: e.activation(out=ptb[:], in_=C.PA[bs_][:, :], func=AF.Exp),
                         reads=[("pa", bs_)], writes=[("ptb", pi)])
                    d = kt - st * 4
                    if d >= 0:
                        S.op("pool", lambda e: e.affine_select(out=ptb[:], in_=ptb[:], pattern=[[1, 512]], compare_op=ALU.is_ge,
                                                               fill=0.0, base=-128 * d, channel_multiplier=-1),
                             reads=[("ptb", pi)], writes=[("ptb", pi)])
                    binfo[i] = (ptb, pi)

                def acc_of(hl):
                    return (C.PY, "py") if hl % 2 == 0 else (C.PS, "ps")

                def emit_pv(i):
                    hl, kt = blocks[i]
                    ptb, pi = binfo.pop(i)
                    oacc, okey = acc_of(hl)
                    S.op("pe", lambda e: e.matmul(oacc[0:65, :], lhsT=VT[:, kt, hl, :], rhs=ptb[:], start=(kt == 0), stop=(kt == nkt - 1)),
                         reads=[("vt", kt // 4), "vtones", ("ptb", pi)], writes=[okey])

                def emit_tail1(hl):
                    oacc, okey = acc_of(hl)
                    S.op("act", lambda e: e.activation(out=RDEN[64:65, :], in_=oacc[64:65, :], func=AF.Ln), reads=[okey], writes=["rden"])
                    S.op("act", lambda e: e.activation(out=RDENB[64:65, :], in_=RDEN[64:65, :], func=AF.Exp, scale=-1.0), reads=["rden"], writes=["rdenb"])

                def emit_tail2(hl):
                    h = hg * 4 + hl
                    hp, par = h // 2, h % 2
                    oacc, okey = acc_of(hl)
                    bb_ = C.bank()
                    S.op("pe", lambda e: e.matmul(C.PA[bb_][0:64, :], lhsT=C.ones_bf[64:65, 0:64], rhs=RDENB[64:65, :], start=True, stop=True),
                         reads=["rdenb", "consts"], writes=[("pa", bb_)])
                    S.op("dve", lambda e: e.tensor_copy(out=BCS[0:64, :], in_=C.PA[bb_][0:64, :]), reads=[("pa", bb_)], writes=["bcs"])
                    S.op("dve", lambda e: e.tensor_tensor(out=OTMP[0:64, :], in0=oacc[0:64, :], in1=BCS[0:64, :], op=ALU.mult),
                         reads=[okey, "bcs"], writes=["otmp"])
                    if par == 0:
                        S.op("dve", lambda e: e.tensor_tensor(out=C.OG[0:64, hp, t0:t0 + 512], in0=OTMP[0:64, :], in1=SZBS[st % 2][0:64, hl, :], op=ALU.mult),
                             reads=["otmp", ("szb", st % 2, hl)], writes=["og"])
                    else:
                        oi = C.ostg_i % 2
                        C.ostg_i += 1
                        S.op("dve", lambda e: e.tensor_tensor(out=OSTG[oi][0:64, :], in0=OTMP[0:64, :], in1=SZBS[st % 2][0:64, hl, :], op=ALU.mult),
                             reads=["otmp", ("szb", st % 2, hl)], writes=[("ostg", oi)])
                        S.op("sp", lambda e: e.dma_start(out=C.OG[64:128, hp, t0:t0 + 512], in_=OSTG[oi][0:64, :]),
                             reads=[("ostg", oi)], writes=["og"], dma=True)

                nb_ = len(blocks)
                for i in range(nb_ + LOOK + 6):
                    if i < nb_:
                        emit_qk(i)
                    if 0 <= i - LOOK < nb_:
                        emit_pv(i - LOOK)
                        hl_, kt_ = blocks[i - LOOK]
                        if kt_ == nkt - 1:
                            tails1[i + 1] = hl_
                            tails2[i + 3] = hl_
                    if i in tails1:
                        emit_tail1(tails1[i])
                    if i in tails2:
                        emit_tail2(tails2[i])
                    yield
            def drain(g):
                if g is not None:
                    for _ in g:
                        pass
            prev = None
            for hg in range(2):
                for st in range(NST):
                    if st == 0:
                        drain(prev)
                        prev = None
                    S.deferred = []
                    S.defer = S.deferred
                    C.pool = [0, 1]
                    for _ in prologue_gen(hg, st):
                        pass
                    S.defer = None
                    C.pool = [2, 3, 4]
                    if prev is not None:
                        nblk = (st - 1) * 4 * 4 + 16
                        per = max(2, -(-len(S.deferred) // max(1, nblk - 8)))
                        for _ in prev:
                            S.flush(per)
                    S.flush(None)
                    prev = attention_gen(hg, st)
            drain(prev)
            C.pool = list(range(5))
            if dbg is not None and "og" in dbg:
                S.op("sp", lambda e: e.dma_start(out=dbg["og"].rearrange("(h p) t -> p h t", p=128), in_=C.OG[:, :, 0:T]), reads=["og"], dma=True, force=True)
            S.finalize_and_emit(block)


def phase_C(nc, S, C, T, dbg=None):
    NST = T // 512
    with ExitStack() as es:
        def sb(name, shape, dt):
            return es.enter_context(nc.sbuf_tensor(name, shape, dt))
        WG = sb("WG", [128, NC_, 2048], BF16)
        WOA = sb("WOA", [128, 4, D], BF16)
        WOB = sb("WOB", [128, 4, D], BF16)
        WO = sb("WO", [128, NC_, D], BF16)
        C.XS = [sb("xsc%d" % i, [128, 512], F32) for i in range(3)]
        C.xs_i = 0
        C.SQ = [sb("sqc%d" % i, [128, 512], BF16) for i in range(2)]
        C.XGS = [sb("XGc%d" % i, [128, NC_, 512], BF16) for i in range(2)]
        C.RSTDS = [sb("RSTDc%d" % i, [128, 512], F32) for i in range(2)]
        BG = sb("BG", [128, 16], F32)
        GPB = sb("GPB", [128, D], F32)
        GA = [sb("GA%d" % i, [128, 512], F32) for i in range(2)]
        GB = [sb("GB%d" % i, [128, 512], F32) for i in range(2)]
        MT1 = [sb("MT1_%d" % i, [128, 512], F32) for i in range(2)]
        MT2 = [sb("MT2_%d" % i, [128, 512], F32) for i in range(2)]
        MER = sb("MER", [128, NC_, 512], BF16)
        XTOK = [sb("XTOK%d" % i, [128, D], F32) for i in range(2)]
        OTOK = [sb("OTOK%d" % i, [128, D], F32) for i in range(2)]
        SCR = sb("SCR", [128, 512], F32)
        ST2 = sb("ST2", [128, 8], F32)

        C.sbuf_left_C = nc.sbuf_bytes_remaining
        C.pool = list(range(6))
        with nc.Block() as block:
            w_in_v = C.w_in.rearrange("(c p) n -> p c n", p=128)
            for a in range(0, 2048, 1024):
                S.op("pool", lambda e, a=a: e.dma_start(out=WG[:, :, a:a + 1024], in_=w_in_v[:, :, C_GATE + a:C_GATE + a + 1024]), writes=["w"], dma=True)
            S.op("pool", lambda e: e.dma_start(out=WOA[:], in_=C.w_out_a.rearrange("(c p) n -> p c n", p=128)), writes=["w"], dma=True)
            S.op("pool", lambda e: e.dma_start(out=WOB[:], in_=C.w_out_b.rearrange("(c p) n -> p c n", p=128)), writes=["w"], dma=True)
            S.op("pool", lambda e: e.dma_start(out=WO[:], in_=C.w_o.rearrange("(c p) n -> p c n", p=128)), writes=["w"], dma=True)
            _col_load(S, nc, BG[:, :], C.b_gate, 16, "vecs")
            S.op("sp", lambda e: e.dma_start(out=GPB[:], in_=C.g_post.partition_broadcast(128)), writes=["vecs"], dma=True)
            make_u(S, C, 0)
            for st in range(NST):
                t0 = st * 512
                C.cur_slot = st % 2
                C.XG, C.RSTD = C.XGS[st % 2], C.RSTDS[st % 2]
                S.deferred = []
                if st + 1 < NST:
                    S.defer = S.deferred
                    save = (C.cur_slot, C.XG, C.RSTD)
                    C.pool = [6]
                    make_u(S, C, st + 1)
                    C.pool = list(range(6))
                    C.cur_slot, C.XG, C.RSTD = save
                    S.defer = None
                for ft in range(8):
                    S.flush(6)
                    i2 = ft % 2
                    bga = proj(S, C, WG, ft * 128)
                    S.op("dve", lambda e, bga=bga, i2=i2: e.tensor_tensor(out=GA[i2][:], in0=C.PA[bga][:, :], in1=C.RSTD[:], op=ALU.mult),
                         reads=[("pa", bga), ("rstd", C.cur_slot)], writes=[("ga", i2)])
                    S.op("act", lambda e, i2=i2, ft=ft: e.activation(out=GA[i2][:], in_=GA[i2][:], func=AF.Sigmoid, bias=BG[:, ft:ft + 1]),
                         reads=[("ga", i2), "vecs"], writes=[("ga", i2)])
                    bgb = proj(S, C, WG, 1024 + ft * 128)
                    S.op("dve", lambda e, bgb=bgb, i2=i2: e.tensor_tensor(out=GB[i2][:], in0=C.PA[bgb][:, :], in1=C.RSTD[:], op=ALU.mult),
                         reads=[("pa", bgb), ("rstd", C.cur_slot)], writes=[("gb", i2)])
                    S.op("act", lambda e, i2=i2, ft=ft: e.activation(out=GB[i2][:], in_=GB[i2][:], func=AF.Sigmoid, bias=BG[:, 8 + ft:9 + ft]),
                         reads=[("gb", i2), "vecs"], writes=[("gb", i2)])
                    bya = C.bank()
                    for kc in range(4):
                        S.op("pe", lambda e, kc=kc, ft=ft, bya=bya, t0=t0: e.matmul(C.PA[bya][:, :], lhsT=WOA[:, kc, ft * 128:(ft + 1) * 128],
                                                                                rhs=C.YZ[:, kc, t0:t0 + 512], start=(kc == 0), stop=(kc == 3)),
                             reads=["yz", "w"], writes=[("pa", bya)])
                    S.op("dve", lambda e, bya=bya, i2=i2: e.tensor_tensor(out=MT1[i2][:], in0=C.PA[bya][:, :], in1=GA[i2][:], op=ALU.mult),
                         reads=[("pa", bya), ("ga", i2)], writes=[("mt1", i2)])
                    byb = C.bank()
                    for kc in range(4):
                        S.op("pe", lambda e, kc=kc, ft=ft, byb=byb, t0=t0: e.matmul(C.PA[byb][:, :], lhsT=WOB[:, kc, ft * 128:(ft + 1) * 128],
                                                                                rhs=C.OG[:, kc, t0:t0 + 512], start=(kc == 0), stop=(kc == 3)),
                             reads=["og", "w"], writes=[("pa", byb)])
                    S.op("dve", lambda e, byb=byb, i2=i2: e.tensor_tensor(out=MT2[i2][:], in0=C.PA[byb][:, :], in1=GB[i2][:], op=ALU.mult),
                         reads=[("pa", byb), ("gb", i2)], writes=[("mt2", i2)])
                    S.op("pool", lambda e, i2=i2, ft=ft: e.tensor_tensor(out=MER[:, ft, :], in0=MT1[i2][:], in1=MT2[i2][:], op=ALU.add),
                         reads=[("mt1", i2), ("mt2", i2)], writes=[("mer", ft)])
                S.flush(None)
                for j in range(4):
                    tb = j * 128
                    ta = t0 + tb
                    xi = (st * 4 + j) % 2
                    S.op("sp", lambda e, xi=xi, ta=ta: e.dma_start(out=XTOK[xi][:], in_=C.x[ta:ta + 128, :]), writes=[("xtok", xi)], dma=True)
                    banks = []
                    for half in range(2):
                        bi_ = C.bank()
                        banks.append(bi_)
                        bo = C.PA[bi_]
                        bkey = ("pa", bi_)
                        for ft in range(8):
                            S.op("pe", lambda e, ft=ft, half=half, bo=bo, tb=tb: e.matmul(bo[:, :], lhsT=MER[:, ft, tb:tb + 128], rhs=WO[:, ft, half * 512:(half + 1) * 512],
                                                                                     start=(ft == 0), stop=(ft == 7)),
                                 reads=[("mer", ft), "w"], writes=[bkey])
                        S.op("act", lambda e, half=half, bo=bo: e.activation(out=SCR[:], in_=bo[:, :], func=AF.Square, accum_out=ST2[:, half:half + 1]),
                             reads=[bkey], writes=["scr", ("st2", half)])
                    S.op("dve", lambda e: e.tensor_tensor(out=ST2[:, 2:3], in0=ST2[:, 0:1], in1=ST2[:, 1:2], op=ALU.add),
                         reads=[("st2", 0), ("st2", 1)], writes=[("st2", 2)])
                    S.op("act", lambda e: e.activation(out=ST2[:, 3:4], in_=ST2[:, 2:3], func=AF.Ln, scale=1.0 / D, bias=C.eps_norm[:, 0:1]),
                         reads=[("st2", 2), "consts"], writes=[("st2", 3)])
                    S.op("act", lambda e: e.activation(out=ST2[:, 4:5], in_=ST2[:, 3:4], func=AF.Exp, scale=-0.5), reads=[("st2", 3)], writes=[("st2", 4)])
                    for half in range(2):
                        bo = C.PA[banks[half]]
                        bkey = ("pa", banks[half])
                        hs = slice(half * 512, (half + 1) * 512)
                        S.op("dve", lambda e, bo=bo, hs=hs, xi=xi: e.scalar_tensor_tensor(out=OTOK[xi][:, hs], in0=bo[:, :], scalar=ST2[:, 4:5], in1=GPB[:, hs],
                                                                                     op0=ALU.mult, op1=ALU.mult),
                             reads=[bkey, ("st2", 4), "vecs"], writes=[("otok", xi, half)])
                        S.op("pool", lambda e, hs=hs, xi=xi: e.tensor_tensor(out=OTOK[xi][:, hs], in0=OTOK[xi][:, hs], in1=XTOK[xi][:, hs], op=ALU.add),
                             reads=[("otok", xi, half), ("xtok", xi)], writes=[("otok", xi, half)])
                    S.op("sp", lambda e, xi=xi, ta=ta: e.dma_start(out=C.out[ta:ta + 128, :], in_=OTOK[xi][:]),
                         reads=[("otok", xi, 0), ("otok", xi, 1)], writes=["outdram"], dma=True)
            S.finalize_and_emit(block)


def build(T=4096, phases="ABC", debug=False, max_ops=None):
    nc = bass.Bass("TRN2", target_bir_lowering=False)
    Sched.max_ops = max_ops
    C = Ctx()
    dt = lambda name, shape, dtp=F32: nc.dram_tensor(name, shape, dtp, kind="ExternalInput").ap()
    C.xT = dt("xT", [D, T])
    C.x = dt("x", [T, D])
    C.positions = dt("positions", [T], I32)
    C.g_pre = dt("g_pre", [D])
    C.w_in = dt("w_in", [D, IN_COLS])
    C.b_gate = dt("b_gate", [2 * D])
    C.mu_shift = dt("mu_shift", [1664])
    C.w0 = dt("w0", [512])
    C.w_decay_up = dt("w_decay_up", [64, 512])
    C.a0 = dt("a0", [512])
    C.w_iclr_up = dt("w_iclr_up", [64, 512])
    C.k_k = dt("k_k", [512])
    C.k_a = dt("k_a", [512])
    C.r_k = dt("r_k", [8, 64])
    C.gn_gain = dt("gn_gain", [512])
    C.gn_bias = dt("gn_bias", [512])
    C.w_out_a = dt("w_out_a", [512, D])
    C.g_q = dt("g_q", [256])
    C.w_uq = dt("w_uq", [256, 768])
    C.g_kv = dt("g_kv", [128])
    C.w_ukv = dt("w_ukv", [128, 1024])
    C.w_out_b = dt("w_out_b", [512, D])
    C.w_o = dt("w_o", [D, D])
    C.g_post = dt("g_post", [D])
    C.rope_invf = dt("rope_invf", [32])
    C.out = nc.dram_tensor("out", [T, D], F32, kind="ExternalOutput").ap()
    dbg = None
    if debug:
        dbg = {"yz": nc.dram_tensor("dbg_yz", [512, T], BF16, kind="ExternalOutput").ap(),
               "og": nc.dram_tensor("dbg_og", [512, T], BF16, kind="ExternalOutput").ap()}

    with ExitStack() as es:
        S = Sched(nc, es)
        sb = lambda name, shape, dtp: es.enter_context(nc.sbuf_tensor(name, shape, dtp))
        C.ident = sb("ident", [128, 128], BF16)
        C.identf = sb("identf", [128, 128], F32)
        C.bones = sb("bones", [128, 128], BF16)
        C.ones_bf = sb("ones_bf", [128, 128], BF16)
        C.eps_norm = sb("eps_norm", [128, 1], F32)
        C.eps_gn = sb("eps_gn", [128, 1], F32)
        C.gpre = sb("gpre", [128, NC_], F32)
        C.YZ = sb("YZ", [128, 4, T], BF16)
        C.ptb_i = 0
        C.ostg_i = 0
        C.PA = [es.enter_context(nc.psum_tensor("PA%d" % i, [128, 512], F32)) for i in range(5)]
        C.PT = es.enter_context(nc.psum_tensor("PT", [128, 1024], BF16))
        C.PY = es.enter_context(nc.psum_tensor("PY", [128, 512], F32))
        C.PS = es.enter_context(nc.psum_tensor("PS", [128, 512], F32))
        C.PA = C.PA + [C.PY, C.PS]
        C.bank_i = 0

        C.pool = list(range(5))
        C.pool_ctr = {}

        def bank():
            key = tuple(C.pool)
            i = C.pool_ctr.get(key, 0)
            C.pool_ctr[key] = i + 1
            return C.pool[i % len(C.pool)]
        C.bank = bank

        with nc.allow_non_contiguous_dma(reason="small per-feature vectors"):
            with nc.Block() as block:
                S.op("pool", lambda e: e.memset(C.identf[:], 0.0), writes=["identf"])
                S.op("pool", lambda e: e.affine_select(out=C.identf[:], in_=C.identf[:], pattern=[[-1, 128]],
                                                       compare_op=ALU.not_equal, fill=1.0, base=0, channel_multiplier=1),
                     reads=["identf"], writes=["identf"])
                S.op("dve", lambda e: e.tensor_copy(out=C.ident[:], in_=C.identf[:]), reads=["identf"], writes=["consts"])
                S.op("pool", lambda e: e.memset(C.ones_bf[:], 1.0), writes=["consts"])
                S.op("pool", lambda e: e.memset(C.bones[:], 0.0), writes=["consts"])
                S.op("pool", lambda e: e.memset(C.bones[0:64, 0:64], 1.0), writes=["consts"])
                S.op("pool", lambda e: e.memset(C.bones[64:128, 64:128], 1.0), writes=["consts"])
                S.op("pool", lambda e: e.memset(C.eps_norm[:], NORM_EPS), writes=["consts"])
                S.op("pool", lambda e: e.memset(C.eps_gn[:], GN_EPS), writes=["consts"])
                _col_load(S, nc, C.gpre[:, :], C.g_pre, NC_, "vecs")
                S.finalize_and_emit(block)
            if "A" in phases:
                phase_A(nc, S, C, T, dbg)
            C.OG = sb("OG", [128, 4, T], BF16)
            if "B" in phases:
                phase_B(nc, S, C, T, dbg)
            if "C" in phases:
                phase_C(nc, S, C, T, dbg)
    C.S = S
    nc._ctx = C
    return nc


def rope_invf():
    inv = (np.float32(ROPE_THETA) ** (-np.arange(0, 32, 2, dtype=np.float32) / np.float32(32))).astype(np.float32)
    return np.concatenate([inv, inv]).astype(np.float32)


_CACHE = {}


def kernel(**inputs):
    x = np.ascontiguousarray(np.asarray(inputs["x"], dtype=np.float32))
    B, T, _ = x.shape
    if "nc" not in _CACHE:
        _CACHE["nc"] = build(T=T, phases="ABC", debug=False)
    nc = _CACHE["nc"]
    pos = np.ascontiguousarray(np.asarray(inputs["positions"]).astype(np.int32))
    wnames = ["g_pre", "w_in", "b_gate", "mu_shift", "w0", "w_decay_up", "a0", "w_iclr_up", "k_k", "k_a", "r_k",
              "gn_gain", "gn_bias", "w_out_a", "g_q", "w_uq", "g_kv", "w_ukv", "w_out_b", "w_o", "g_post"]
    shared = {k: np.ascontiguousarray(np.asarray(inputs[k], dtype=np.float32)) for k in wnames}
    shared["rope_invf"] = rope_invf()
    in_maps = []
    for b in range(B):
        m = dict(shared)
        m["x"] = x[b]
        m["xT"] = np.ascontiguousarray(x[b].T)
        m["positions"] = pos[b]
        in_maps.append(m)
    res = run_bass_kernel_spmd(nc, in_maps, core_ids=list(range(B)))
    return np.stack([np.asarray(r["out"], dtype=np.float32) for r in res.results], axis=0)
```

```python
import numpy as np
from contextlib import ExitStack
import concourse.bass as bass
import concourse.mybir as mybir
from concourse.bass_utils import run_bass_kernel_spmd

F32 = mybir.dt.float32
BF16 = mybir.dt.bfloat16
I32 = mybir.dt.int32
AF = mybir.ActivationFunctionType
ALU = mybir.AluOpType
AX = mybir.AxisListType

D = 1024
NC_ = 8
HEADS = 8
DECAY_SCALE = 0.6065306597
GN_EPS = 64e-5
NORM_EPS = 1e-6
ROPE_THETA = 10000.0
IN_COLS = 5152
C_R, C_K, C_V, C_WD, C_AD = 0, 512, 1024, 1536, 1600
C_ZA = 1664
C_CQ = 2176
C_CKV = 2432
C_KPE = 2560
C_ZB = 2592
C_GATE = 3104


class _Rec:
    def __getattr__(self, name):
        def f(*a, **k):
            self.__dict__["call"] = (name, a, k)
            return self
        return f


class Sched:
    ENG = ["pe", "dve", "act", "pool", "sp"]

    def __init__(self, nc, es, n_dma_sems=16):
        self.nc = nc
        self.sem = {e: es.enter_context(nc.semaphore("s_" + e)) for e in self.ENG}
        self.cnt = {e: 0 for e in self.ENG}
        self.dma_sems = [es.enter_context(nc.semaphore("s_dma%d" % i)) for i in range(n_dma_sems)]
        self.dma_cnt = [0] * n_dma_sems
        self.dma_rr = 0
        self.n_sw = 4
        self.dma_rr_sw = 0
        self.waited = {e: {} for e in self.ENG}
        self.ops = []
        self.writers = {}
        self.readers = {}
        self.nops = 0

    max_ops = None
    log = None
    defer = None
    deferred = None

    def op(self, eng, fn, reads=(), writes=(), dma=False, force=False):
        if self.max_ops is not None and len(self.ops) >= self.max_ops and not force:
            return None
        if self.defer is not None:
            rec = _Rec()
            fn(rec)
            self.defer.append((eng, rec.call, tuple(reads), tuple(writes), dma))
            return None
        return self._op(eng, fn, reads, writes, dma)

    def flush(self, n=None):
        lst = self.deferred
        k = len(lst) if n is None else min(n, len(lst))
        for _ in range(k):
            eng, call, reads, writes, dma = lst.pop(0)
            self._op(eng, None, reads, writes, dma, call=call)
        return len(lst)

    def _op(self, eng, fn, reads=(), writes=(), dma=False, call=None):
        idx = len(self.ops)
        deps = set()
        for k in reads:
            deps.update(self.writers.get(k, ()))
        for k in writes:
            deps.update(self.readers.get(k, ()))
            deps.update(self.writers.get(k, ()))
        for k in writes:
            if self.readers.get(k):
                self.writers[k] = [idx]
                self.readers[k] = []
            else:
                lst = self.writers.setdefault(k, [])
                lst.append(idx)
                if len(lst) > 40:
                    del lst[0]
        for k in reads:
            lst = self.readers.setdefault(k, [])
            lst.append(idx)
            if len(lst) > 40:
                del lst[0]
        if call is None:
            rec = _Rec()
            fn(rec)
            call = rec.call
        fn2 = lambda e_, call=call: getattr(e_, call[0])(*call[1], **call[2])
        self.ops.append(dict(eng=eng, fn=fn2, dma=dma, deps=deps, need_inc=False, idx=idx, name=call[0]))
        if self.log is not None:
            self.log.append((idx, eng, call[0], [k for k in writes]))
        return idx

    def finalize_and_emit(self, block, barrier=True):
        ops = self.ops
        self.nops += len(ops)
        for o in ops:
            pd = []
            for d in o["deps"]:
                p = ops[d]
                if p["eng"] == "pe" and o["eng"] == "pe" and not p["dma"] and not o["dma"]:
                    continue
                pd.append(p)
                p["need_inc"] = True
            o["pdeps"] = pd
        for o in ops:
            if o["dma"]:
                if o["eng"] == "pool":
                    s = self.dma_rr_sw % self.n_sw
                    self.dma_rr_sw += 1
                else:
                    s = self.n_sw + self.dma_rr % (len(self.dma_sems) - self.n_sw)
                    self.dma_rr += 1
                o["prev_val"] = self.dma_cnt[s]
                self.dma_cnt[s] += 16
                o["sem"] = self.dma_sems[s]
                o["sem_key"] = "dma%d" % s
                o["val"] = self.dma_cnt[s]
            else:
                if o["need_inc"]:
                    self.cnt[o["eng"]] += 1
                o["sem"] = self.sem[o["eng"]]
                o["sem_key"] = o["eng"]
                o["val"] = self.cnt[o["eng"]] if o["need_inc"] else None
        final = dict(self.cnt)
        final_dma = list(self.dma_cnt)
        by_eng = {e: [o for o in ops if o["eng"] == e] for e in self.ENG}

        def emit(e, eng):
            waited = self.waited[e]
            for o in by_eng[e]:
                need = {}
                for p in o["pdeps"]:
                    k = p["sem_key"]
                    if need.get(k, (None, 0))[1] < p["val"]:
                        need[k] = (p["sem"], p["val"])
                if o["dma"]:
                    k = o["sem_key"]
                    if o["prev_val"] > 0 and need.get(k, (None, 0))[1] < o["prev_val"]:
                        need[k] = (o["sem"], o["prev_val"])
                for k, (s, v) in need.items():
                    if waited.get(k, 0) < v:
                        eng.wait_ge(s, v)
                        waited[k] = v
                ins = o["fn"](eng)
                if o["dma"]:
                    ins.then_inc(o["sem"], 16)
                elif o["need_inc"]:
                    ins.then_inc(o["sem"], 1)
            if barrier:
                for k in self.ENG:
                    if k != e and final[k] > waited.get(k, 0):
                        eng.wait_ge(self.sem[k], final[k])
                        waited[k] = final[k]
                for i, v in enumerate(final_dma):
                    k = "dma%d" % i
                    if v > waited.get(k, 0):
                        eng.wait_ge(self.dma_sems[i], v)
                        waited[k] = v

        @block.tensor
        def _(eng):
            emit("pe", eng)

        @block.vector
        def _(eng):
            emit("dve", eng)

        @block.scalar
        def _(eng):
            emit("act", eng)

        @block.gpsimd
        def _(eng):
            emit("pool", eng)

        @block.sync
        def _(eng):
            emit("sp", eng)

        self.ops = []
        self.writers = {}
        self.readers = {}


class Ctx:
    pass


def _col_load(S, nc, dst, src_vec, ncols, key):
    S.op("sp", lambda e: e.dma_start(out=dst, in_=src_vec.rearrange("(c p) -> p c", p=128)), writes=[key], dma=True)


def make_u(S, C, st):
    t0 = st * 512
    sl = st % len(C.XGS)
    XG, RSTD = C.XGS[sl], C.RSTDS[sl]
    bank = C.bank()
    for dc in range(NC_):
        xs = C.XS[C.xs_i % len(C.XS)]
        xk = ("xs", C.xs_i % len(C.XS))
        C.xs_i += 1
        S.op("sp", lambda e, xs=xs, dc=dc: e.dma_start(out=xs[:], in_=C.xT[dc * 128:(dc + 1) * 128, t0:t0 + 512]),
             writes=[xk], dma=True)
        sq = C.SQ[dc % 2]
        S.op("act", lambda e, xs=xs, sq=sq: e.activation(out=sq[:], in_=xs[:], func=AF.Square),
             reads=[xk], writes=[("sq", dc % 2)])
        S.op("pe", lambda e, sq=sq, dc=dc, bank=bank: e.matmul(C.PA[bank][:, :], lhsT=C.ones_bf[:, :], rhs=sq[:],
                                                             start=(dc == 0), stop=(dc == NC_ - 1)),
             reads=[("sq", dc % 2), "consts"], writes=[("pa", bank)])
        S.op("dve", lambda e, xs=xs, dc=dc: e.tensor_scalar(out=XG[:, dc, :], in0=xs[:], scalar1=C.gpre[:, dc:dc + 1], scalar2=None,
                                                           op0=ALU.mult),
             reads=[xk, "vecs"], writes=[("xg", sl, dc)])
    S.op("act", lambda e, bank=bank: e.activation(out=RSTD[:], in_=C.PA[bank][:, :], func=AF.Ln,
                                                  scale=1.0 / D, bias=C.eps_norm[:, 0:1]),
         reads=[("pa", bank), "consts"], writes=[("rstd", sl)])
    S.op("act", lambda e: e.activation(out=RSTD[:], in_=RSTD[:], func=AF.Exp, scale=-0.5),
         reads=[("rstd", sl)], writes=[("rstd", sl)])
    C.cur_slot = sl
    C.XG, C.RSTD = XG, RSTD


def proj(S, C, W, col0, ncols=128):
    bank = C.bank()
    for dc in range(NC_):
        S.op("pe", lambda e, dc=dc, bank=bank: e.matmul(C.PA[bank][0:ncols, :], lhsT=W[:, dc, col0:col0 + ncols],
                                                       rhs=C.XG[:, dc, :], start=(dc == 0), stop=(dc == NC_ - 1)),
             reads=[("xg", C.cur_slot, dc), "w"], writes=[("pa", bank)])
    return bank


def phase_A(nc, S, C, T, dbg=None):
    NST = T // 512
    with ExitStack() as es:
        def sb(name, shape, dt):
            return es.enter_context(nc.sbuf_tensor(name, shape, dt))

        WA = sb("WA", [128, NC_, 2176], BF16)
        LORA = sb("LORA", [128, 512], BF16)
        C.XS = [sb("xs%d" % i, [128, 512], F32) for i in range(2)]
        C.xs_i = 0
        C.SQ = [sb("sq%d" % i, [128, 512], BF16) for i in range(2)]
        C.XGS = [sb("XG", [128, NC_, 512], BF16)]
        C.RSTDS = [sb("RSTD", [128, 512], F32)]
        MU = sb("MU", [128, 13], F32)
        W0 = sb("W0", [128, 4], F32)
        A0 = sb("A0", [128, 4], F32)
        KK_ = sb("KKv", [128, 4], F32)
        KA = sb("KA", [128, 4], F32)
        OMKA = sb("OMKA", [128, 4], F32)
        RK = sb("RK", [128, 4], F32)
        GNG = sb("GNG", [128, 512], F32)
        GNB = sb("GNB", [128, 512], F32)
        CARRY = sb("CARRY", [128, 13], F32)
        MX = sb("MX", [128, 4, 128], BF16)
        MZ = sb("MZ", [128, 8, 64], BF16)
        I2 = sb("I2", [128, 64], F32)
        HSEL = sb("HSEL", [128, 4, 8], BF16)
        SMASK = sb("SMASK", [128, 512], F32)
        TMPM = sb("TMPM", [128, 2, 64], F32)
        TMPM2 = sb("TMPM2", [128, 2, 64], F32)
        R3 = [sb("R%d" % i, [128, 516], F32) for i in range(2)]
        DD = [sb("DD%d" % i, [128, 512], F32) for i in range(1)]
        FT = [sb("FT%d" % i, [128, 512], F32) for i in range(11)]
        TA = sb("TA", [128, 512], BF16)
        SQK = sb("SQK", [128, 512], BF16)
        AR = sb("AR", [128, 4, 2, 512], BF16)
        BT = sb("BT", [128, 4, 512], BF16)
        KT = sb("KT", [128, 4, 512], BF16)
        VV = sb("VV", [128, 4, 512], BF16)
        RKR = sb("RKR", [128, 4, 512], BF16)
        SZ = sb("SZ", [128, 4, 512], BF16)
        WCS = sb("WCS", [128, 4, 8], F32)
        NB = [sb("NB%d" % i, [128, 8, 192], BF16) for i in range(2)]
        KB = [sb("KB%d" % i, [128, 8, 192], BF16) for i in range(2)]
        N1T = [sb("N1T%d" % i, [128, 8, 64], BF16) for i in range(2)]
        NP = [sb("NP%d" % i, [128, 8, 64], BF16) for i in range(4)]
        NPT = [sb("NPT%d" % i, [128, 8, 64], BF16) for i in range(4)]
        XT = [sb("XT%d" % i, [128, 8, 128], BF16) for i in range(2)]
        VMT = [sb("VMT%d" % i, [128, 8, 64], BF16) for i in range(2)]
        GY = [sb("GY%d" % i, [128, 4, 2, 64], BF16) for i in range(2)]
        GS = [sb("GS%d" % i, [128, 4, 2, 128], BF16) for i in range(2)]
        SLBD = [sb("SLBD%d" % i, [128, 4, 2, 128], F32) for i in range(2)]
        STT = [sb("ST%d" % i, [128, 4, 128], BF16) for i in range(2)]
        STMP = sb("STMP", [128, 4, 128], F32)
        YLOCS = [sb("YLOC%d" % i, [128, 512], F32) for i in range(2)]
        YSB = sb("YSB", [128, 512], F32)
        YSBS = [YSB, DD[0]]
        YSBK = ["ysb", ("dd", 0)]
        YN = sb("YN", [128, 512], F32)
        YBF = sb("YBF", [128, 512], BF16)
        STAT = sb("STAT", [128, 48], F32)
        BONS = [sb("BON%d" % i, [128, 8], F32) for i in range(2)]

        C.sbuf_left_A = nc.sbuf_bytes_remaining
        with nc.Block() as block:
            for i, (a, b) in enumerate([(0, 1088), (1088, 2176)]):
                S.op("pool", lambda e, a=a, b=b: e.dma_start(
                    out=WA[:, :, a:b], in_=C.w_in.rearrange("(c p) n -> p c n", p=128)[:, :, a:b]),
                    writes=["w"], dma=True)
            S.op("pool", lambda e: e.dma_start(out=LORA[0:64, :], in_=C.w_decay_up), writes=["w"], dma=True)
            S.op("pool", lambda e: e.dma_start(out=LORA[64:128, :], in_=C.w_iclr_up), writes=["w"], dma=True)
            _col_load(S, nc, MU[:, :], C.mu_shift, 13, "vecs")
            _col_load(S, nc, W0[:, :], C.w0, 4, "vecs")
            _col_load(S, nc, A0[:, :], C.a0, 4, "vecs")
            _col_load(S, nc, KK_[:, :], C.k_k, 4, "vecs")
            _col_load(S, nc, KA[:, :], C.k_a, 4, "vecs")
            _col_load(S, nc, RK[:, :], C.r_k.rearrange("h n -> (h n)"), 4, "vecs")
            S.op("sp", lambda e: e.dma_start(out=GNG[:], in_=C.gn_gain.partition_broadcast(128)), writes=["vecs"], dma=True)
            S.op("sp", lambda e: e.dma_start(out=GNB[:], in_=C.gn_bias.partition_broadcast(128)), writes=["vecs"], dma=True)
            S.op("dve", lambda e: e.tensor_scalar(out=OMKA[:], in0=KA[:], scalar1=-1.0, scalar2=1.0, op0=ALU.mult, op1=ALU.add),
                 reads=["vecs"], writes=["vecs2"])
            S.op("pool", lambda e: e.memset(CARRY[:], 0.0), writes=["carry"])
            for i in range(2):
                S.op("pool", lambda e, i=i: e.memset(R3[i][:], 0.0), writes=[("R", i)])
            S.op("pool", lambda e: e.memset(STT[0][:], 0.0), writes=[("st", 0)])
            for i in range(2):
                S.op("pool", lambda e, i=i: e.memset(GS[i][:], 0.0), writes=[("gs", i)])
                S.op("pool", lambda e, i=i: e.memset(SLBD[i][:], 0.0), writes=[("slbd", i)])
            S.op("pool", lambda e: e.memset(SMASK[:], 1.0), writes=["masks"])
            S.op("pool", lambda e: e.memset(SMASK[:].rearrange("p (c t) -> p c t", t=64)[:, :, 0:1], 0.0), writes=["masks"])
            S.op("pool", lambda e: e.memset(TMPM2[:], 1.0), writes=["tmpm2"])

            def sel(cmp_, sign):
                return lambda e: e.affine_select(out=TMPM[:], in_=TMPM2[:], pattern=[[64 * sign, 2], [sign, 64]],
                                                 compare_op=cmp_, fill=0.0, base=0, channel_multiplier=-sign)

            def halves(dst_fn, bshape):
                for half in range(2):
                    ps_ = slice(half * 64, half * 64 + 64)
                    src = TMPM[ps_, half, :]
                    if bshape is not None:
                        src = src.unsqueeze(1).to_broadcast([64, bshape, 64])
                    S.op("dve", lambda e, ps_=ps_, src=src: e.tensor_copy(out=dst_fn(ps_), in_=src), reads=["tmpm"], writes=["masks"])
            S.op("pool", sel(ALU.is_gt, 1), reads=["tmpm2"], writes=["tmpm"])
            halves(lambda ps_: MX[ps_, :, 0:64], 4)
            S.op("pool", sel(ALU.is_ge, 1), reads=["tmpm2", "masks"], writes=["tmpm"])
            halves(lambda ps_: MX[ps_, :, 64:128], 4)
            S.op("pool", sel(ALU.is_gt, -1), reads=["tmpm2", "masks"], writes=["tmpm"])
            halves(lambda ps_: MZ[ps_, :, :], 8)
            S.op("pool", sel(ALU.is_equal, 1), reads=["tmpm2", "masks"], writes=["tmpm"])
            halves(lambda ps_: I2[ps_, :], None)
            S.op("pool", lambda e: e.memset(HSEL[:], 0.0), writes=["masks"])
            for hp in range(4):
                for half in range(2):
                    S.op("pool", lambda e, hp=hp, half=half: e.memset(HSEL[half * 64:half * 64 + 64, hp, 2 * hp + half:2 * hp + half + 1], 1.0),
                         writes=["masks"])

            st_slot = 0
            for st in range(NST):
                make_u(S, C, st)
                def shifted(ft, col0, out_ap, okey):
                    bank = proj(S, C, WA, col0)
                    ri = ft % 2
                    Rb = R3[ri]
                    S.op("dve", lambda e: e.tensor_tensor(out=Rb[:, 1:513], in0=C.PA[bank][:, :], in1=C.RSTD[:], op=ALU.mult),
                         reads=[("pa", bank), ("rstd", C.cur_slot)], writes=[("R", ri)])
                    S.op("pool", lambda e: e.tensor_copy(out=Rb[:, 0:1], in_=CARRY[:, ft:ft + 1]), reads=["carry"], writes=[("R", ri)])
                    S.op("pool", lambda e: e.tensor_copy(out=CARRY[:, ft:ft + 1], in_=Rb[:, 512:513]), reads=[("R", ri)], writes=["carry"])
                    di = 0
                    S.op("dve", lambda e: e.tensor_tensor(out=DD[di][:], in0=Rb[:, 0:512], in1=Rb[:, 1:513], op=ALU.subtract),
                         reads=[("R", ri)], writes=[("dd", di)])
                    S.op("dve", lambda e: e.scalar_tensor_tensor(out=out_ap, in0=DD[di][:], scalar=MU[:, ft:ft + 1], in1=Rb[:, 1:513],
                                                                op0=ALU.mult, op1=ALU.add),
                         reads=[("dd", di), ("R", ri), "vecs"], writes=[okey])

                shifted(12, C_WD, FT[0][:], ("ft", 0))
                S.op("act", lambda e: e.activation(out=TA[0:64, :], in_=FT[0][0:64, :], func=AF.Tanh), reads=[("ft", 0)], writes=["ta"])
                S.op("act", lambda e: e.activation(out=TA[64:128, :], in_=FT[0][64:128, :], func=AF.Copy), reads=[("ft", 0)], writes=["ta"])
                for hp in range(4):
                    fs = slice(hp * 128, hp * 128 + 128)
                    bw = C.bank()
                    S.op("pe", lambda e, bw=bw, fs=fs: e.matmul(C.PA[bw][:, :], lhsT=LORA[0:64, fs], rhs=TA[0:64, :], start=True, stop=True),
                         reads=["ta", "w"], writes=[("pa", bw)])
                    ba = C.bank()
                    S.op("pe", lambda e, ba=ba, fs=fs: e.matmul(C.PA[ba][:, :], lhsT=LORA[64:128, fs], rhs=TA[64:128, :], start=True, stop=True),
                         reads=["ta", "w"], writes=[("pa", ba)])
                    WS, AS, CUM, CX, WT, WINV, WEX, RR, KKK, KN = [FT[i] for i in range(1, 11)]
                    T1, T2, T3 = WS, CX, CUM
                    kf = lambda i: ("ft", {11: 1, 12: 4, 13: 3}.get(i, i))
                    S.op("act", lambda e, bw=bw, hp=hp: e.activation(out=WS[:], in_=C.PA[bw][:, :], func=AF.Sigmoid, bias=W0[:, hp:hp + 1]),
                         reads=[("pa", bw), "vecs"], writes=[kf(1)])
                    S.op("act", lambda e, ba=ba, hp=hp: e.activation(out=AS[:], in_=C.PA[ba][:, :], func=AF.Sigmoid, bias=A0[:, hp:hp + 1]),
                         reads=[("pa", ba), "vecs"], writes=[kf(2)])
                    S.op("dve", lambda e: e.tensor_tensor_scan(out=CUM[:], data0=SMASK[:], data1=WS[:], initial=0.0, op0=ALU.mult, op1=ALU.add),
                         reads=[kf(1), "masks"], writes=[kf(3)])
                    S.op("pool", lambda e: e.tensor_tensor(out=CX[:], in0=CUM[:], in1=WS[:], op=ALU.subtract), reads=[kf(3), kf(1)], writes=[kf(4)])
                    S.op("act", lambda e: e.activation(out=WT[:], in_=CUM[:], func=AF.Exp, scale=-DECAY_SCALE), reads=[kf(3)], writes=[kf(5)])
                    S.op("act", lambda e: e.activation(out=WINV[:], in_=CUM[:], func=AF.Exp, scale=DECAY_SCALE), reads=[kf(3)], writes=[kf(6)])
                    S.op("act", lambda e: e.activation(out=WEX[:], in_=CX[:], func=AF.Exp, scale=-DECAY_SCALE), reads=[kf(4)], writes=[kf(7)])
                    S.op("pool", lambda e, hp=hp: e.tensor_copy(out=WCS[:, hp, :], in_=WT[:].rearrange("p (c t) -> p c t", t=64)[:, :, 63]),
                         reads=[kf(5)], writes=["wcs"])
                    shifted(hp, C_R + hp * 128, RR[:], kf(8))
                    shifted(4 + hp, C_K + hp * 128, KKK[:], kf(9))
                    shifted(8 + hp, C_V + hp * 128, VV[:, hp, :], ("vv", hp))
                    S.op("act", lambda e, hp=hp: e.activation(out=SQK[:], in_=KKK[:], func=AF.Square, scale=KK_[:, hp:hp + 1]),
                         reads=[kf(9), "vecs"], writes=["sqk"])
                    bs = C.bank()
                    S.op("pe", lambda e, bs=bs: e.matmul(C.PA[bs][:, :], lhsT=C.bones[:, :], rhs=SQK[:], start=True, stop=True),
                         reads=["sqk", "consts"], writes=[("pa", bs)])
                    S.op("dve", lambda e, bs=bs: e.tensor_scalar(out=T1[:], in0=C.PA[bs][:, :], scalar1=1e-24, scalar2=None, op0=ALU.max),
                         reads=[("pa", bs)], writes=[kf(11)])
                    S.op("act", lambda e: e.activation(out=T1[:], in_=T1[:], func=AF.Ln), reads=[kf(11)], writes=[kf(11)])
                    S.op("act", lambda e: e.activation(out=T1[:], in_=T1[:], func=AF.Exp, scale=-0.5), reads=[kf(11)], writes=[kf(11)])
                    S.op("dve", lambda e, hp=hp: e.scalar_tensor_tensor(out=KN[:], in0=KKK[:], scalar=KK_[:, hp:hp + 1], in1=T1[:],
                                                                       op0=ALU.mult, op1=ALU.mult),
                         reads=[kf(9), kf(11), "vecs"], writes=[kf(10)])
                    S.op("dve", lambda e, hp=hp: e.scalar_tensor_tensor(out=AR[:, hp, 0, :], in0=KN[:], scalar=-1.0, in1=WEX[:],
                                                                       op0=ALU.mult, op1=ALU.mult),
                         reads=[kf(10), kf(7)], writes=[("ar", hp)])
                    S.op("dve", lambda e: e.tensor_tensor(out=T2[:], in0=KN[:], in1=AS[:], op=ALU.mult), reads=[kf(10), kf(2)], writes=[kf(12)])
                    S.op("dve", lambda e, hp=hp: e.tensor_tensor(out=BT[:, hp, :], in0=T2[:], in1=WINV[:], op=ALU.mult),
                         reads=[kf(12), kf(6)], writes=[("bt", hp)])
                    S.op("pool", lambda e, hp=hp: e.tensor_scalar(out=T3[:], in0=AS[:], scalar1=KA[:, hp:hp + 1], scalar2=OMKA[:, hp:hp + 1],
                                                                 op0=ALU.mult, op1=ALU.add),
                         reads=[kf(2), "vecs", "vecs2"], writes=[kf(13)])
                    S.op("dve", lambda e: e.tensor_tensor(out=T3[:], in0=T3[:], in1=KKK[:], op=ALU.mult), reads=[kf(13), kf(9)], writes=[kf(13)])
                    S.op("pool", lambda e, hp=hp: e.tensor_tensor(out=KT[:, hp, :], in0=T3[:], in1=WINV[:], op=ALU.mult),
                         reads=[kf(13), kf(6)], writes=[("kt", hp)])
                    S.op("dve", lambda e, hp=hp: e.scalar_tensor_tensor(out=RKR[:, hp, :], in0=T3[:], scalar=RK[:, hp:hp + 1], in1=RR[:],
                                                                       op0=ALU.mult, op1=ALU.mult),
                         reads=[kf(13), kf(8), "vecs"], writes=[("rkr", hp)])
                    S.op("pool", lambda e, hp=hp: e.tensor_tensor(out=AR[:, hp, 1, :], in0=RR[:], in1=WT[:], op=ALU.mult),
                         reads=[kf(8), kf(5)], writes=[("ar", hp)])
                    bz = proj(S, C, WA, C_ZA + hp * 128)
                    S.op("dve", lambda e, bz=bz: e.tensor_tensor(out=T2[:], in0=C.PA[bz][:, :], in1=C.RSTD[:], op=ALU.mult),
                         reads=[("pa", bz), ("rstd", C.cur_slot)], writes=[kf(12)])
                    S.op("act", lambda e, hp=hp: e.activation(out=SZ[:, hp, :], in_=T2[:], func=AF.Silu), reads=[kf(12)], writes=[("sz", hp)])

                def tile_gen(j):
                    nonlocal st_slot
                    tb = j * 128
                    tt = st * 4 + j
                    sl_ = tt % 2
                    nb, kb, xt, vmt, gy, gs, slbd = NB[sl_], KB[sl_], XT[sl_], VMT[sl_], GY[sl_], GS[sl_], SLBD[sl_]
                    kn = lambda name: (name, sl_)
                    for rnd, pair in enumerate([((AR, 0), (BT, None)), ((KT, None), (VV, None))]):
                        for pi, (src, sub) in enumerate(pair):
                            for hp in range(4):
                                in_ap = src[:, hp, 0, tb:tb + 128] if sub is not None else src[:, hp, tb:tb + 128]
                                rk = {id(AR): ("ar", hp), id(BT): ("bt", hp), id(KT): ("kt", hp), id(VV): ("vv", hp)}[id(src)]
                                S.op("pe", lambda e, in_ap=in_ap, pi=pi, hp=hp: e.transpose(
                                    out=C.PT[:, pi * 512 + hp * 128: pi * 512 + hp * 128 + 128], in_=in_ap, identity=C.ident[:, :]),
                                    reads=[rk, "consts"], writes=["pt"])
                        if rnd == 0:
                            S.op("act", lambda e, xt=xt: e.activation(out=xt[:, :, 0:64], in_=C.PT[:, 0:512].rearrange("p (h k) -> p h k", k=64), func=AF.Copy),
                                 reads=["pt"], writes=[kn("xt")])
                            S.op("act", lambda e, nb=nb: e.activation(out=nb[:, :, 128:192], in_=C.PT[:, 512:1024].rearrange("p (h k) -> p h k", k=64), func=AF.Copy),
                                 reads=["pt"], writes=[kn("nb")])
                        else:
                            S.op("act", lambda e, kb=kb: e.activation(out=kb[:, :, 128:192], in_=C.PT[:, 0:512].rearrange("p (h k) -> p h k", k=64), func=AF.Copy),
                                 reads=["pt"], writes=[kn("kb")])
                            S.op("act", lambda e, vmt=vmt: e.activation(out=vmt[:, :, :], in_=C.PT[:, 512:1024].rearrange("p (h k) -> p h k", k=64), func=AF.Copy),
                                 reads=["pt"], writes=[kn("vmt")])
                    yield
                    def rows(h):
                        return slice((h % 2) * 64, (h % 2) * 64 + 64)
                    for par in range(2):
                        for (lsrc, dst, dkey) in ((BT, nb, "nb"), (KT, kb, "kb")):
                            bank = C.bank()
                            pv = C.PA[bank][:, :].rearrange("p (h c) -> p h c", c=128)
                            for hp in range(4):
                                h = 2 * hp + par
                                for e_ in range(2):
                                    c0 = tb + e_ * 64
                                    S.op("pe", lambda e, lsrc=lsrc, h=h, hp=hp, e_=e_, c0=c0, pv=pv: e.matmul(
                                        pv[e_ * 64:e_ * 64 + 64, hp, :], lhsT=lsrc[rows(h), hp, c0:c0 + 64],
                                        rhs=AR[rows(h), hp, :, c0:c0 + 64], start=True, stop=True),
                                        reads=[("ar", hp), ("bt", hp) if lsrc is BT else ("kt", hp)], writes=[("pa", bank)])
                            dview = dst[:].rearrange("p (hp two) c -> p hp two c", two=2)[:, :, par, 0:128]
                            S.op("dve", lambda e, dview=dview, bank=bank: e.tensor_tensor(
                                out=dview, in0=C.PA[bank][:, :].rearrange("p (h c) -> p h c", c=128), in1=MX[:], op=ALU.mult),
                                reads=[("pa", bank), "masks"], writes=[kn(dkey)])
                    n1t = N1T[sl_]
                    for par in range(2):
                        bankz = C.bank()
                        pz = C.PA[bankz][:, 0:256].rearrange("p (h c) -> p h c", c=64)
                        for hp in range(4):
                            h = 2 * hp + par
                            for e_ in range(2):
                                c0 = tb + e_ * 64
                                S.op("pe", lambda e, h=h, hp=hp, e_=e_, c0=c0, pz=pz: e.matmul(
                                    pz[e_ * 64:e_ * 64 + 64, hp, :], lhsT=AR[rows(h), hp, 0, c0:c0 + 64], rhs=BT[rows(h), hp, c0:c0 + 64],
                                    start=True, stop=True), reads=[("ar", hp), ("bt", hp)], writes=[("pa", bankz)])
                        dview = n1t[:].rearrange("p (hp two) c -> p hp two c", two=2)[:, :, par, :]
                        S.op("dve", lambda e, dview=dview, pz=pz: e.tensor_tensor(out=dview, in0=pz, in1=MZ[:, 0:4, :], op=ALU.mult),
                             reads=[("pa", bankz), "masks"], writes=[kn("n1t")])
                    yield
                    bankp = C.bank()
                    pp = C.PA[bankp][:, :].rearrange("p (h c) -> p h c", c=64)
                    for h in range(8):
                        for e_ in range(2):
                            er = slice(e_ * 64, e_ * 64 + 64)
                            S.op("pe", lambda e, h=h, er=er: e.matmul(pp[er, h, :], lhsT=kb[er, h, 0:64], rhs=vmt[er, h, :], start=True, stop=True),
                                 reads=[kn("kb"), kn("vmt")], writes=[("pa", bankp)])
                    S.op("act", lambda e, xt=xt, bankp=bankp: e.activation(out=xt[:, :, 64:128], in_=C.PA[bankp][:, :].rearrange("p (h c) -> p h c", c=64), func=AF.Copy),
                         reads=[("pa", bankp)], writes=[kn("xt")])
                    yield
                    ncur, ntcur = (nb, slice(0, 64)), (n1t, slice(0, 64))
                    ncur_key, ntcur_key = kn("nb"), kn("n1t")
                    for lvl in range(6):
                        for g in range(2):
                            bank = C.bank()
                            pa_ = C.PA[bank][:, :].rearrange("p (h c) -> p h c", c=128)
                            for h4 in range(4):
                                h = g * 4 + h4
                                for e_ in range(2):
                                    er = slice(e_ * 64, e_ * 64 + 64)
                                    S.op("pe", lambda e, h=h, h4=h4, er=er, pa_=pa_, ncur=ncur: e.matmul(
                                        pa_[er, h4, :], lhsT=ncur[0][er, h, ncur[1]], rhs=xt[er, h, :], start=True, stop=True),
                                        reads=[ncur_key, kn("xt")], writes=[("pa", bank)])
                            S.op("dve", lambda e, g=g, bank=bank, xt=xt: e.tensor_tensor(
                                out=xt[:, g * 4:g * 4 + 4, :], in0=C.PA[bank][:, :].rearrange("p (h c) -> p h c", c=128),
                                in1=xt[:, g * 4:g * 4 + 4, :], op=ALU.add),
                                reads=[("pa", bank), kn("xt")], writes=[kn("xt")])
                        if lvl < 5:
                            nn, nnt = NP[sl_ * 2 + lvl % 2], NPT[sl_ * 2 + lvl % 2]
                            nnk, nntk = ("np", sl_ * 2 + lvl % 2), ("npt", sl_ * 2 + lvl % 2)
                            b1 = C.bank()
                            p1 = C.PA[b1][:, :].rearrange("p (h c) -> p h c", c=64)
                            for h in range(8):
                                for e_ in range(2):
                                    er = slice(e_ * 64, e_ * 64 + 64)
                                    S.op("pe", lambda e, h=h, er=er, p1=p1, ncur=ncur, ntcur=ntcur: e.matmul(
                                        p1[er, h, :], lhsT=ntcur[0][er, h, ntcur[1]], rhs=ncur[0][er, h, ncur[1]], start=True, stop=True),
                                        reads=[ncur_key, ntcur_key], writes=[("pa", b1)])
                            S.op("act", lambda e, nn=nn, b1=b1: e.activation(out=nn[:], in_=C.PA[b1][:, :].rearrange("p (h c) -> p h c", c=64), func=AF.Copy),
                                 reads=[("pa", b1)], writes=[nnk])
                            b2 = C.bank()
                            p2 = C.PA[b2][:, :].rearrange("p (h c) -> p h c", c=64)
                            for h in range(8):
                                for e_ in range(2):
                                    er = slice(e_ * 64, e_ * 64 + 64)
                                    S.op("pe", lambda e, h=h, er=er, p2=p2, ncur=ncur, ntcur=ntcur: e.matmul(
                                        p2[er, h, :], lhsT=ncur[0][er, h, ncur[1]], rhs=ntcur[0][er, h, ntcur[1]], start=True, stop=True),
                                        reads=[ncur_key, ntcur_key], writes=[("pa", b2)])
                            S.op("act", lambda e, nnt=nnt, b2=b2: e.activation(out=nnt[:], in_=C.PA[b2][:, :].rearrange("p (h c) -> p h c", c=64), func=AF.Copy),
                                 reads=[("pa", b2)], writes=[nntk])
                            ncur, ntcur = (nn, slice(0, 64)), (nnt, slice(0, 64))
                            ncur_key, ntcur_key = nnk, nntk
                        yield
                    yield
                    for e_ in range(2):
                        er = slice(e_ * 64, e_ * 64 + 64)
                        c0 = tb + e_ * 64
                        bank = C.bank()
                        pg = C.PA[bank][:, :].rearrange("p (h c) -> p h c", c=128)
                        for h in range(8):
                            hp = h // 2
                            S.op("pe", lambda e, h=h, hp=hp, er=er, pg=pg: e.matmul(
                                pg[rows(h), hp, :], lhsT=xt[er, h, 0:64], rhs=nb[er, h, 64:192], start=True, stop=True),
                                reads=[kn("xt"), kn("nb")], writes=[("pa", bank)])
                        S.op("dve", lambda e, e_=e_, c0=c0, pg=pg: e.tensor_tensor(
                            out=gy[:, :, e_, :], in0=pg[:, :, 0:64], in1=AR[:, :, 1, c0:c0 + 64], op=ALU.add),
                            reads=[("pa", bank)] + [("ar", hp) for hp in range(4)], writes=[kn("gy")])
                        for par in range(2):
                            pr = slice(par * 64, par * 64 + 64)
                            S.op("dve", lambda e, e_=e_, pg=pg, pr=pr, par=par: e.tensor_tensor(
                                out=gs[pr, :, e_, par * 64:par * 64 + 64], in0=pg[pr, :, 64:128],
                                in1=I2[pr, :].unsqueeze(1).to_broadcast([64, 4, 64]), op=ALU.add),
                                reads=[("pa", bank), "masks"], writes=[kn("gs")])
                        bl = C.bank()
                        psl = C.PA[bl][:, :].rearrange("p (h c) -> p h c", c=128)
                        for h in range(8):
                            hp = h // 2
                            par = h % 2
                            S.op("pe", lambda e, h=h, hp=hp, par=par, er=er, psl=psl: e.matmul(
                                psl[rows(h), hp, par * 64:par * 64 + 64], lhsT=nb[er, h, 128:192], rhs=xt[er, h, 64:128], start=True, stop=False),
                                reads=[kn("nb"), kn("xt")], writes=[("pa", bl)])
                            S.op("pe", lambda e, h=h, hp=hp, par=par, er=er, psl=psl: e.matmul(
                                psl[rows(h), hp, par * 64:par * 64 + 64], lhsT=kb[er, h, 128:192], rhs=vmt[er, h, :], start=False, stop=True),
                                reads=[kn("kb"), kn("vmt")], writes=[("pa", bl)])
                        for par in range(2):
                            pr = slice(par * 64, par * 64 + 64)
                            S.op("act", lambda e, e_=e_, psl=psl, pr=pr, par=par: e.activation(
                                out=slbd[pr, :, e_, par * 64:par * 64 + 64], in_=psl[pr, :, par * 64:par * 64 + 64], func=AF.Copy),
                                reads=[("pa", bl)], writes=[kn("slbd")])
                    byl = C.bank()
                    pyl = C.PA[byl][:, :].rearrange("p (h c) -> p h c", c=64)
                    for e_ in range(2):
                        er = slice(e_ * 64, e_ * 64 + 64)
                        for h in range(8):
                            S.op("pe", lambda e, h=h, er=er: e.matmul(
                                pyl[er, h, :], lhsT=nb[er, h, 64:128], rhs=xt[er, h, 64:128], start=True, stop=False),
                                reads=[kn("nb"), kn("xt")], writes=[("pa", byl)])
                            S.op("pe", lambda e, h=h, er=er: e.matmul(
                                pyl[er, h, :], lhsT=kb[er, h, 64:128], rhs=vmt[er, h, :], start=False, stop=True),
                                reads=[kn("kb"), kn("vmt")], writes=[("pa", byl)])
                    S.op("act", lambda e: e.activation(out=YLOCS[sl_][:], in_=C.PA[byl][:, :], func=AF.Copy), reads=[("pa", byl)], writes=[kn("yloc")])
                    yield
                    py = C.PY[:, :].rearrange("p (h c) -> p h c", c=128)
                    for e_ in range(2):
                        er = slice(e_ * 64, e_ * 64 + 64)
                        cl = j * 2 + e_
                        stc = STT[st_slot]
                        stn = STT[1 - st_slot]
                        psv = C.PS[:, :].rearrange("p (h c) -> p h c", c=128)
                        for hp in range(4):
                            S.op("pe", lambda e, hp=hp, er=er, stc=stc, e_=e_: e.matmul(
                                py[er, hp, :], lhsT=gy[:, hp, e_, :], rhs=stc[:, hp, :], start=True, stop=True),
                                reads=[kn("gy"), ("st", st_slot)], writes=["py"])
                        for hp in range(4):
                            S.op("pe", lambda e, hp=hp, stc=stc, e_=e_: e.matmul(
                                psv[:, hp, :], lhsT=gs[:, hp, e_, :], rhs=stc[:, hp, :], start=True, stop=True),
                                reads=[kn("gs"), ("st", st_slot)], writes=["ps"])
                        S.op("dve", lambda e, e_=e_: e.tensor_tensor(out=STMP[:], in0=psv, in1=slbd[:, :, e_, :], op=ALU.add),
                             reads=["ps", kn("slbd")], writes=["stmp"])
                        S.op("dve", lambda e, stn=stn, cl=cl: e.tensor_tensor(
                            out=stn[:], in0=STMP[:], in1=WCS[:, :, cl:cl + 1].to_broadcast([128, 4, 128]), op=ALU.mult),
                            reads=["stmp", "wcs"], writes=[("st", 1 - st_slot)])
                        st_slot = 1 - st_slot
                    ysb, ysbk = YSBS[sl_], YSBK[sl_]
                    bon, bonk = BONS[sl_], ("bon", sl_)
                    v3 = lambda t: t[:].rearrange("p (h c) -> p h c", c=64)
                    S.op("dve", lambda e: e.tensor_tensor(out=ysb[:], in0=C.PY[:, :], in1=YLOCS[sl_][:], op=ALU.add), reads=["py", kn("yloc")], writes=[ysbk])
                    bb_ = C.bank()
                    for hp in range(4):
                        S.op("pe", lambda e, hp=hp, bb_=bb_: e.matmul(C.PA[bb_][:, 0:8], lhsT=RKR[:, hp, tb:tb + 128], rhs=HSEL[:, hp, :],
                                                                     start=(hp == 0), stop=(hp == 3)),
                             reads=[("rkr", hp), "masks"], writes=[("pa", bb_)])
                    S.op("act", lambda e, bb_=bb_: e.activation(out=bon[:], in_=C.PA[bb_][:, 0:8], func=AF.Copy), reads=[("pa", bb_)], writes=[bonk])
                    S.op("dve", lambda e, vmt=vmt: e.tensor_tensor(out=v3(YLOCS[sl_]), in0=vmt[:], in1=bon[:].unsqueeze(2).to_broadcast([128, 8, 64]), op=ALU.mult),
                         reads=[kn("vmt"), bonk], writes=[kn("yloc")])
                    S.defer = S.deferred
                    S.op("act", lambda e: e.activation(out=YN[:], in_=ysb[:], func=AF.Square), reads=[ysbk], writes=["yn"])
                    S.op("dve", lambda e: e.tensor_reduce(out=STAT[:, 0:8], in_=v3(ysb), op=ALU.add, axis=AX.X),
                         reads=[ysbk], writes=["stat0"])
                    S.op("dve", lambda e: e.tensor_reduce(out=STAT[:, 8:16], in_=YN[:].rearrange("p (h c) -> p h c", c=64), op=ALU.add, axis=AX.X),
                         reads=["yn"], writes=["stat1"])
                    S.op("dve", lambda e: e.tensor_scalar(out=STAT[:, 16:24], in0=STAT[:, 0:8], scalar1=1.0 / 64, scalar2=None, op0=ALU.mult),
                         reads=["stat0"], writes=["stat2"])
                    S.op("dve", lambda e: e.tensor_tensor(out=STAT[:, 24:32], in0=STAT[:, 16:24], in1=STAT[:, 16:24], op=ALU.mult),
                         reads=["stat2"], writes=["stat3"])
                    S.op("dve", lambda e: e.scalar_tensor_tensor(out=STAT[:, 32:40], in0=STAT[:, 8:16], scalar=1.0 / 64, in1=STAT[:, 24:32],
                                                                op0=ALU.mult, op1=ALU.subtract),
                         reads=["stat1", "stat3"], writes=["stat4"])
                    S.op("act", lambda e: e.activation(out=STAT[:, 32:40], in_=STAT[:, 32:40], func=AF.Ln, bias=C.eps_gn[:, 0:1]),
                         reads=["stat4", "consts"], writes=["stat4"])
                    S.op("act", lambda e: e.activation(out=STAT[:, 32:40], in_=STAT[:, 32:40], func=AF.Exp, scale=-0.5),
                         reads=["stat4"], writes=["stat4"])
                    S.op("dve", lambda e: e.scalar_tensor_tensor(out=STAT[:, 40:48], in0=STAT[:, 16:24], scalar=-1.0, in1=STAT[:, 32:40],
                                                                op0=ALU.mult, op1=ALU.mult),
                         reads=["stat2", "stat4"], writes=["stat5"])
                    S.op("dve", lambda e: e.tensor_tensor(out=v3(YN), in0=v3(ysb), in1=STAT[:, 32:40].unsqueeze(2).to_broadcast([128, 8, 64]), op=ALU.mult),
                         reads=[ysbk, "stat4"], writes=["yn"])
                    S.op("pool", lambda e: e.tensor_tensor(out=v3(YN), in0=v3(YN), in1=STAT[:, 40:48].unsqueeze(2).to_broadcast([128, 8, 64]), op=ALU.add),
                         reads=["yn", "stat5"], writes=["yn"])
                    S.op("pool", lambda e: e.tensor_tensor(out=YN[:], in0=YN[:], in1=GNG[:], op=ALU.mult), reads=["yn", "vecs"], writes=["yn"])
                    S.op("pool", lambda e: e.tensor_tensor(out=YN[:], in0=YN[:], in1=GNB[:], op=ALU.add), reads=["yn", "vecs"], writes=["yn"])
                    S.op("pool", lambda e: e.tensor_tensor(out=YBF[:], in0=YN[:], in1=YLOCS[sl_][:], op=ALU.add), reads=["yn", kn("yloc")], writes=["ybf"])
                    for hp in range(4):
                        S.op("pe", lambda e, hp=hp: e.transpose(out=C.PT[:, hp * 128:hp * 128 + 128], in_=YBF[:, hp * 128:hp * 128 + 128], identity=C.ident[:, :]),
                             reads=["ybf", "consts"], writes=["pt"])
                    t_abs = st * 512 + tb
                    S.op("dve", lambda e, t_abs=t_abs, tb=tb: e.tensor_tensor(
                        out=C.YZ[:, :, t_abs:t_abs + 128], in0=C.PT[:, 0:512].rearrange("p (h c) -> p h c", c=128),
                        in1=SZ[:, :, tb:tb + 128], op=ALU.mult),
                        reads=["pt"] + [("sz", hp) for hp in range(4)], writes=["yz"])
                    S.defer = None
                S.deferred = []
                for pair in ((0, 1), (2, 3)):
                    gens = [tile_gen(j) for j in pair]
                    while gens:
                        for g in list(gens):
                            try:
                                next(g)
                            except StopIteration:
                                gens.remove(g)
                            S.flush(10)
                S.flush(None)
            if dbg is not None and "yz" in dbg:
                S.op("sp", lambda e: e.dma_start(out=dbg["yz"].rearrange("(h p) t -> p h t", p=128), in_=C.YZ[:, :, 0:T]), reads=["yz"], dma=True, force=True)
            S.finalize_and_emit(block)


def phase_B(nc, S, C, T, dbg=None):
    NST = T // 512
    NKT = T // 128
    SCALE = 1.0 / float(np.sqrt(96.0))
    TWO_PI = float(2 * np.pi)
    with ExitStack() as es:
        def sb(name, shape, dt):
            return es.enter_context(nc.sbuf_tensor(name, shape, dt))
        WB = sb("WB", [128, NC_, 960], BF16)
        WUQ = sb("WUQ", [128, 2, 768], BF16)
        WUQS = sb("WUQS", [128, 2, 8, 32], BF16)
        WUKV = sb("WUKV", [128, 8, 128], BF16)
        C.XS = [sb("xsb%d" % i, [128, 512], F32) for i in range(2)]
        C.xs_i = 0
        C.SQ = [sb("sqb%d" % i, [128, 512], BF16) for i in range(2)]
        C.XGS = [sb("XGb", [128, NC_, 512], BF16)]
        C.RSTDS = [sb("RSTDb", [128, 512], F32)]
        GQ = sb("GQ", [128, 2], F32)
        GKV = sb("GKV", [128, 1], F32)
        INVF = sb("INVF", [128, 1], F32)
        KT = sb("KTb", [128, 4, T], BF16)
        VT = sb("VTb", [128, NKT, 4, 65], BF16)
        QTS = [sb("QTb%d" % i, [128, 4, 512], BF16) for i in range(2)]
        SZBS = [sb("SZB%d" % i, [128, 4, 512], BF16) for i in range(2)]
        CQ = [sb("CQ%d" % i, [128, 512], F32) for i in range(2)]
        CKV = CQ[0]
        CQN = sb("CQN", [128, 2, 512], BF16)
        CKVN = sb("CKVN", [128, 512], BF16)
        RQ = sb("RQ", [128, 512], F32)
        ANG = sb("ANG", [128, 512], F32)
        TR1 = sb("TR1", [128, 512], F32)
        TR2 = sb("TR2", [128, 512], F32)
        TRI = sb("TRI", [128, 512], I32)
        POSI = TRI
        COS = sb("COS", [128, 512], F32)
        SIN = sb("SIN", [128, 512], F32)
        CR = sb("CR", [128, 512], F32)
        SR = sb("SR", [128, 512], F32)
        KR = sb("KR", [128, 512], BF16)
        PTB = [sb("PTB%d" % i, [128, 512], BF16) for i in range(5)]
        RDEN = sb("RDEN", [128, 512], F32)
        RDENB = sb("RDENB", [128, 512], BF16)
        BCS = sb("BCS", [128, 512], BF16)
        OTMP = sb("OTMP", [128, 512], F32)
        OSTG = [sb("OSTG%d" % i, [128, 512], BF16) for i in range(2)]

        C.sbuf_left_B = nc.sbuf_bytes_remaining
        with nc.Block() as block:
            w_in_v = C.w_in.rearrange("(c p) n -> p c n", p=128)
            S.op("pool", lambda e: e.dma_start(out=WB[:, :, 0:928], in_=w_in_v[:, :, C_CQ:C_CQ + 928]), writes=["w"], dma=True)
            S.op("pool", lambda e: e.dma_start(out=WB[:, :, 928:944], in_=w_in_v[:, :, C_KPE + 16:C_KPE + 32]), writes=["w"], dma=True)
            S.op("pool", lambda e: e.dma_start(out=WB[:, :, 944:960], in_=w_in_v[:, :, C_KPE:C_KPE + 16]), writes=["w"], dma=True)
            S.op("dve", lambda e: e.tensor_scalar(out=WB[:, :, 928:944], in0=WB[:, :, 928:944], scalar1=-1.0, scalar2=None, op0=ALU.mult),
                 reads=["w"], writes=["w"])
            S.op("pool", lambda e: e.dma_start(out=WUQ[:], in_=C.w_uq.rearrange("(c p) n -> p c n", p=128)), writes=["w"], dma=True)
            S.op("pool", lambda e: e.dma_start(out=WUKV[:], in_=C.w_ukv.rearrange("p (h c) -> p h c", c=128)), writes=["w"], dma=True)
            wq4 = WUQ[:].rearrange("p c (h d) -> p c h d", d=96)
            S.op("dve", lambda e: e.tensor_scalar(out=WUQS[:, :, :, 0:16], in0=wq4[:, :, :, 80:96], scalar1=-1.0, scalar2=None, op0=ALU.mult),
                 reads=["w"], writes=["w2"])
            S.op("dve", lambda e: e.tensor_copy(out=WUQS[:, :, :, 16:32], in_=wq4[:, :, :, 64:80]), reads=["w"], writes=["w2"])
            _col_load(S, nc, GQ[:, :], C.g_q, 2, "vecs")
            _col_load(S, nc, GKV[:, :], C.g_kv, 1, "vecs")
            S.op("sp", lambda e: e.dma_start(out=INVF[64:96, :], in_=C.rope_invf.rearrange("(p o) -> p o", o=1)), writes=["vecs"], dma=True)
            S.op("pool", lambda e: e.memset(VT[:, :, :, 64:65], 1.0), writes=["vtones"])
            rr = slice(64, 96)

            def prologue_gen(hg, st):
                t0 = st * 512
                make_u(S, C, st)
                yield
                S.op("sp", lambda e, t0=t0: e.dma_start(out=POSI[rr, :], in_=C.positions[t0:t0 + 512].partition_broadcast(32)),
                     writes=["tri"], dma=True)
                S.op("dve", lambda e: e.tensor_copy(out=ANG[rr, :], in_=POSI[rr, :]), reads=["tri"], writes=["ang"])
                S.op("dve", lambda e: e.tensor_scalar(out=ANG[rr, :], in0=ANG[rr, :], scalar1=INVF[rr, 0:1], scalar2=None, op0=ALU.mult),
                     reads=["ang", "vecs"], writes=["ang"])
                for (dst, off, key) in ((SIN, 0.0, "sin"), (COS, 0.25, "cos")):
                    S.op("dve", lambda e, off=off: e.tensor_scalar(out=TR1[rr, :], in0=ANG[rr, :], scalar1=1.0 / TWO_PI, scalar2=off,
                                                                    op0=ALU.mult, op1=ALU.add), reads=["ang"], writes=["tr1"])
                    S.op("dve", lambda e: e.tensor_copy(out=TRI[rr, :], in_=TR1[rr, :]), reads=["tr1"], writes=["tri"])
                    S.op("dve", lambda e: e.tensor_copy(out=TR2[rr, :], in_=TRI[rr, :]), reads=["tri"], writes=["tr2"])
                    S.op("dve", lambda e: e.tensor_tensor(out=TR1[rr, :], in0=TR1[rr, :], in1=TR2[rr, :], op=ALU.subtract),
                         reads=["tr1", "tr2"], writes=["tr1"])
                    S.op("dve", lambda e: e.tensor_scalar(out=TR2[rr, :], in0=TR1[rr, :], scalar1=0.5, scalar2=None, op0=ALU.is_gt),
                         reads=["tr1"], writes=["tr2"])
                    S.op("dve", lambda e: e.tensor_tensor(out=TR1[rr, :], in0=TR1[rr, :], in1=TR2[rr, :], op=ALU.subtract),
                         reads=["tr1", "tr2"], writes=["tr1"])
                    S.op("dve", lambda e: e.tensor_scalar(out=TR2[rr, :], in0=TR1[rr, :], scalar1=-0.5, scalar2=None, op0=ALU.is_lt),
                         reads=["tr1"], writes=["tr2"])
                    S.op("dve", lambda e: e.tensor_tensor(out=TR1[rr, :], in0=TR1[rr, :], in1=TR2[rr, :], op=ALU.add),
                         reads=["tr1", "tr2"], writes=["tr1"])
                    S.op("act", lambda e, dst=dst: e.activation(out=dst[rr, :], in_=TR1[rr, :], func=AF.Sin, scale=TWO_PI),
                         reads=["tr1"], writes=[key])
                yield
                for i in range(2):
                    b = proj(S, C, WB, i * 128)
                    S.op("dve", lambda e, b=b, i=i: e.tensor_tensor(out=CQ[i][:], in0=C.PA[b][:, :], in1=C.RSTD[:], op=ALU.mult),
                         reads=[("pa", b), ("rstd", C.cur_slot)], writes=[("cq", i)])
                bss = C.bank()
                for i in range(2):
                    S.op("act", lambda e, i=i: e.activation(out=C.SQ[i][:], in_=CQ[i][:], func=AF.Square), reads=[("cq", i)], writes=[("sq", i)])
                    S.op("pe", lambda e, i=i, bss=bss: e.matmul(C.PA[bss][:, :], lhsT=C.ones_bf[:, :], rhs=C.SQ[i][:], start=(i == 0), stop=(i == 1)),
                         reads=[("sq", i), "consts"], writes=[("pa", bss)])
                S.op("act", lambda e, bss=bss: e.activation(out=RQ[:], in_=C.PA[bss][:, :], func=AF.Ln, scale=1.0 / 256, bias=C.eps_norm[:, 0:1]),
                     reads=[("pa", bss), "consts"], writes=["rq"])
                S.op("act", lambda e: e.activation(out=RQ[:], in_=RQ[:], func=AF.Exp, scale=-0.5), reads=["rq"], writes=["rq"])
                for i in range(2):
                    S.op("dve", lambda e, i=i: e.scalar_tensor_tensor(out=CQN[:, i, :], in0=CQ[i][:], scalar=GQ[:, i:i + 1], in1=RQ[:],
                                                                     op0=ALU.mult, op1=ALU.mult),
                         reads=[("cq", i), "rq", "vecs"], writes=[("cqn", i)])
                b = proj(S, C, WB, 256)
                S.op("dve", lambda e, b=b: e.tensor_tensor(out=CKV[:], in0=C.PA[b][:, :], in1=C.RSTD[:], op=ALU.mult),
                     reads=[("pa", b), ("rstd", C.cur_slot)], writes=[("cq", 0)])
                bss = C.bank()
                S.op("act", lambda e: e.activation(out=C.SQ[0][:], in_=CKV[:], func=AF.Square), reads=[("cq", 0)], writes=[("sq", 0)])
                S.op("pe", lambda e, bss=bss: e.matmul(C.PA[bss][:, :], lhsT=C.ones_bf[:, :], rhs=C.SQ[0][:], start=True, stop=True),
                     reads=[("sq", 0), "consts"], writes=[("pa", bss)])
                S.op("act", lambda e, bss=bss: e.activation(out=RQ[:], in_=C.PA[bss][:, :], func=AF.Ln, scale=1.0 / 128, bias=C.eps_norm[:, 0:1]),
                     reads=[("pa", bss), "consts"], writes=["rq"])
                S.op("act", lambda e: e.activation(out=RQ[:], in_=RQ[:], func=AF.Exp, scale=-0.5), reads=["rq"], writes=["rq"])
                S.op("dve", lambda e: e.scalar_tensor_tensor(out=CKVN[:], in0=CKV[:], scalar=GKV[:, 0:1], in1=RQ[:], op0=ALU.mult, op1=ALU.mult),
                     reads=[("cq", 0), "rq", "vecs"], writes=["ckvn"])
                yield
                S.op("dve", lambda e: e.tensor_tensor(out=CR[rr, :], in0=COS[rr, :], in1=C.RSTD[rr, :], op=ALU.mult), reads=["cos", ("rstd", C.cur_slot)], writes=["cr"])
                S.op("dve", lambda e: e.tensor_tensor(out=SR[rr, :], in0=SIN[rr, :], in1=C.RSTD[rr, :], op=ALU.mult), reads=["sin", ("rstd", C.cur_slot)], writes=["sr"])
                bk = C.bank()
                bks = C.bank()
                for dc in range(NC_):
                    S.op("pe", lambda e, dc=dc, bk=bk: e.matmul(C.PA[bk][64:96, :], lhsT=WB[:, dc, 384:416], rhs=C.XG[:, dc, :],
                                                               start=(dc == 0), stop=(dc == NC_ - 1)), reads=[("xg", C.cur_slot, dc), "w"], writes=[("pa", bk)])
                for dc in range(NC_):
                    S.op("pe", lambda e, dc=dc, bks=bks: e.matmul(C.PA[bks][64:96, :], lhsT=WB[:, dc, 928:960], rhs=C.XG[:, dc, :],
                                                                 start=(dc == 0), stop=(dc == NC_ - 1)), reads=[("xg", C.cur_slot, dc), "w"], writes=[("pa", bks)])
                S.op("dve", lambda e, bk=bk: e.tensor_tensor(out=TR1[rr, :], in0=C.PA[bk][rr, :], in1=CR[rr, :], op=ALU.mult),
                     reads=[("pa", bk), "cr"], writes=["tr1"])
                S.op("dve", lambda e, bks=bks: e.tensor_tensor(out=TR2[rr, :], in0=C.PA[bks][rr, :], in1=SR[rr, :], op=ALU.mult),
                     reads=[("pa", bks), "sr"], writes=["tr2"])
                S.op("dve", lambda e: e.tensor_tensor(out=KR[rr, :], in0=TR1[rr, :], in1=TR2[rr, :], op=ALU.add), reads=["tr1", "tr2"], writes=["kr"])
                S.op("dve", lambda e: e.tensor_scalar(out=CR[rr, :], in0=COS[rr, :], scalar1=SCALE, scalar2=None, op0=ALU.mult), reads=["cos"], writes=["cr"])
                S.op("dve", lambda e: e.tensor_scalar(out=SR[rr, :], in0=SIN[rr, :], scalar1=SCALE, scalar2=None, op0=ALU.mult), reads=["sin"], writes=["sr"])
                yield
                for hl in range(4):
                    h = hg * 4 + hl
                    bkn = C.bank()
                    S.op("pe", lambda e, h=h, bkn=bkn: e.matmul(C.PA[bkn][0:64, :], lhsT=WUKV[:, h, 0:64], rhs=CKVN[:], start=True, stop=True),
                         reads=["ckvn", "w"], writes=[("pa", bkn)])
                    S.op("dve", lambda e, hl=hl, bkn=bkn, t0=t0: e.tensor_copy(out=KT[0:64, hl, t0:t0 + 512], in_=C.PA[bkn][0:64, :]),
                         reads=[("pa", bkn)], writes=[("kt", hl, st)])
                    S.op("dve", lambda e, hl=hl, t0=t0: e.tensor_copy(out=KT[rr, hl, t0:t0 + 512], in_=KR[rr, :]), reads=["kr"], writes=[("kt", hl, st)])
                    bq = C.bank()
                    for kc in range(2):
                        S.op("pe", lambda e, kc=kc, h=h, bq=bq: e.matmul(C.PA[bq][0:96, :], lhsT=WUQ[:, kc, h * 96:h * 96 + 96], rhs=CQN[:, kc, :],
                                                                        start=(kc == 0), stop=(kc == 1)),
                             reads=[("cqn", kc), "w"], writes=[("pa", bq)])
                    bqs = C.bank()
                    for kc in range(2):
                        S.op("pe", lambda e, kc=kc, h=h, bqs=bqs: e.matmul(C.PA[bqs][64:96, :], lhsT=WUQS[:, kc, h, :], rhs=CQN[:, kc, :],
                                                                          start=(kc == 0), stop=(kc == 1)),
                             reads=[("cqn", kc), "w2"], writes=[("pa", bqs)])
                    S.op("dve", lambda e, hl=hl, bq=bq: e.tensor_scalar(out=QTS[st % 2][0:64, hl, :], in0=C.PA[bq][0:64, :], scalar1=SCALE, scalar2=None, op0=ALU.mult),
                         reads=[("pa", bq)], writes=[("qt", st % 2, hl)])
                    S.op("dve", lambda e, bq=bq: e.tensor_tensor(out=TR1[rr, :], in0=C.PA[bq][rr, :], in1=CR[rr, :], op=ALU.mult),
                         reads=[("pa", bq), "cr"], writes=["tr1"])
                    S.op("dve", lambda e, bqs=bqs: e.tensor_tensor(out=TR2[rr, :], in0=C.PA[bqs][rr, :], in1=SR[rr, :], op=ALU.mult),
                         reads=[("pa", bqs), "sr"], writes=["tr2"])
                    S.op("dve", lambda e, hl=hl: e.tensor_tensor(out=QTS[st % 2][rr, hl, :], in0=TR1[rr, :], in1=TR2[rr, :], op=ALU.add),
                         reads=["tr1", "tr2"], writes=[("qt", st % 2, hl)])
                    bz = proj(S, C, WB, 416 + h * 64, ncols=64)
                    S.op("dve", lambda e, bz=bz: e.tensor_tensor(out=CQ[1][0:64, :], in0=C.PA[bz][0:64, :], in1=C.RSTD[0:64, :], op=ALU.mult),
                         reads=[("pa", bz), ("rstd", C.cur_slot)], writes=[("cq", 1)])
                    S.op("act", lambda e, hl=hl: e.activation(out=SZBS[st % 2][0:64, hl, :], in_=CQ[1][0:64, :], func=AF.Silu), reads=[("cq", 1)], writes=[("szb", st % 2, hl)])
                    yield
                yield
                for j in range(4):
                    kt = st * 4 + j
                    bv = C.bank()
                    S.op("pe", lambda e, j=j, bv=bv: e.matmul(C.PA[bv][:, 0:256].rearrange("p (h c) -> p h c", c=64),
                                                           lhsT=CKVN[:, j * 128:(j + 1) * 128], rhs=WUKV[:, hg * 4:hg * 4 + 4, 64:128], start=True, stop=True),
                         reads=["ckvn", "w"], writes=[("pa", bv)])
                    S.op("dve", lambda e, kt=kt, bv=bv: e.tensor_copy(out=VT[:, kt, :, 0:64], in_=C.PA[bv][:, 0:256].rearrange("p (h c) -> p h c", c=64)),
                         reads=[("pa", bv)], writes=[("vt", st)])
                yield
            def attention_gen(hg, st):
                t0 = st * 512
                blocks = [(hl, kt) for hl in range(4) for kt in range(st * 4 + 4)]
                nkt = st * 4 + 4
                LOOK = 3
                binfo = {}
                tails1, tails2 = {}, {}

                def emit_qk(i):
                    hl, kt = blocks[i]
                    bs_ = C.bank()
                    S.op("pe", lambda e: e.matmul(C.PA[bs_][:, :], lhsT=KT[0:96, hl, kt * 128:(kt + 1) * 128], rhs=QTS[st % 2][0:96, hl, :],
                                                  start=True, stop=True),
                         reads=[("kt", hl, kt // 4), ("qt", st % 2, hl)], writes=[("pa", bs_)])
                    pi = C.ptb_i % len(PTB)
                    C.ptb_i += 1
                    ptb = PTB[pi]
                    S.op("act", lambda e: e.activation(out=ptb[:], in_=C.PA[bs_][:, :], func=AF.Exp),
                         reads=[("pa", bs_)], writes=[("ptb", pi)])
                    d = kt - st * 4
                    if d >= 0:
                        S.op("pool", lambda e: e.affine_select(out=ptb[:], in_=ptb[:], pattern=[[1, 512]], compare_op=ALU.is_ge,
                                                               fill=0.0, base=-128 * d, channel_multiplier=-1),
                             reads=[("ptb", pi)], writes=[("ptb", pi)])
                    binfo[i] = (ptb, pi)

                def acc_of(hl):
                    return (C.PY, "py") if hl % 2 == 0 else (C.PS, "ps")

                def emit_pv(i):
                    hl, kt = blocks[i]
                    ptb, pi = binfo.pop(i)
                    oacc, okey = acc_of(hl)
                    S.op("pe", lambda e: e.matmul(oacc[0:65, :], lhsT=VT[:, kt, hl, :], rhs=ptb[:], start=(kt == 0), stop=(kt == nkt - 1)),
                         reads=[("vt", kt // 4), "vtones", ("ptb", pi)], writes=[okey])

                def emit_tail1(hl):
                    oacc, okey = acc_of(hl)
                    S.op("act", lambda e: e.activation(out=RDEN[64:65, :], in_=oacc[64:65, :], func=AF.Ln), reads=[okey], writes=["rden"])
                    S.op("act", lambda e: e.activation(out=RDENB[64:65, :], in_=RDEN[64:65, :], func=AF.Exp, scale=-1.0), reads=["rden"], writes=["rdenb"])

                def emit_tail2(hl):
                    h = hg * 4 + hl
                    hp, par = h // 2, h % 2
                    oacc, okey = acc_of(hl)
                    bb_ = C.bank()
                    S.op("pe", lambda e: e.matmul(C.PA[bb_][0:64, :], lhsT=C.ones_bf[64:65, 0:64], rhs=RDENB[64:65, :], start=True, stop=True),
                         reads=["rdenb", "consts"], writes=[("pa", bb_)])
                    S.op("dve", lambda e: e.tensor_copy(out=BCS[0:64, :], in_=C.PA[bb_][0:64, :]), reads=[("pa", bb_)], writes=["bcs"])
                    S.op("dve", lambda e: e.tensor_tensor(out=OTMP[0:64, :], in0=oacc[0:64, :], in1=BCS[0:64, :], op=ALU.mult),
                         reads=[okey, "bcs"], writes=["otmp"])
                    if par == 0:
                        S.op("dve", lambda e: e.tensor_tensor(out=C.OG[0:64, hp, t0:t0 + 512], in0=OTMP[0:64, :], in1=SZBS[st % 2][0:64, hl, :], op=ALU.mult),
                             reads=["otmp", ("szb", st % 2, hl)], writes=["og"])
                    else:
                        oi = C.ostg_i % 2
                        C.ostg_i += 1
                        S.op("dve", lambda e: e.tensor_tensor(out=OSTG[oi][0:64, :], in0=OTMP[0:64, :], in1=SZBS[st % 2][0:64, hl, :], op=ALU.mult),
                             reads=["otmp", ("szb", st % 2, hl)], writes=[("ostg", oi)])
                        S.op("sp", lambda e: e.dma_start(out=C.OG[64:128, hp, t0:t0 + 512], in_=OSTG[oi][0:64, :]),
                             reads=[("ostg", oi)], writes=["og"], dma=True)

                nb_ = len(blocks)
                for i in range(nb_ + LOOK + 6):
                    if i < nb_:
                        emit_qk(i)
                    if 0 <= i - LOOK < nb_:
                        emit_pv(i - LOOK)
                        hl_, kt_ = blocks[i - LOOK]
                        if kt_ == nkt - 1:
                            tails1[i + 1] = hl_
                            tails2[i + 3] = hl_
                    if i in tails1:
                        emit_tail1(tails1[i])
                    if i in tails2:
                        emit_tail2(tails2[i])
                    yield
            def drain(g):
                if g is not None:
                    for _ in g:
                        pass
            prev = None
            for hg in range(2):
                for st in range(NST):
                    if st == 0:
                        drain(prev)
                        prev = None
                    S.deferred = []
                    S.defer = S.deferred
                    C.pool = [0, 1]
                    for _ in prologue_gen(hg, st):
                        pass
                    S.defer = None
                    C.pool = [2, 3, 4]
                    if prev is not None:
                        nblk = (st - 1) * 4 * 4 + 16
                        per = max(2, -(-len(S.deferred) // max(1, nblk - 8)))
                        for _ in prev:
                            S.flush(per)
                    S.flush(None)
                    prev = attention_gen(hg, st)
            drain(prev)
            C.pool = list(range(5))
            if dbg is not None and "og" in dbg:
                S.op("sp", lambda e: e.dma_start(out=dbg["og"].rearrange("(h p) t -> p h t", p=128), in_=C.OG[:, :, 0:T]), reads=["og"], dma=True, force=True)
            S.finalize_and_emit(block)


def phase_C(nc, S, C, T, dbg=None):
    NST = T // 512
    with ExitStack() as es:
        def sb(name, shape, dt):
            return es.enter_context(nc.sbuf_tensor(name, shape, dt))
        WG = sb("WG", [128, NC_, 2048], BF16)
        WOA = sb("WOA", [128, 4, D], BF16)
        WOB = sb("WOB", [128, 4, D], BF16)
        WO = sb("WO", [128, NC_, D], BF16)
        C.XS = [sb("xsc%d" % i, [128, 512], F32) for i in range(3)]
        C.xs_i = 0
        C.SQ = [sb("sqc%d" % i, [128, 512], BF16) for i in range(2)]
        C.XGS = [sb("XGc%d" % i, [128, NC_, 512], BF16) for i in range(2)]
        C.RSTDS = [sb("RSTDc%d" % i, [128, 512], F32) for i in range(2)]
        BG = sb("BG", [128, 16], F32)
        GPB = sb("GPB", [128, D], F32)
        GA = [sb("GA%d" % i, [128, 512], F32) for i in range(2)]
        GB = [sb("GB%d" % i, [128, 512], F32) for i in range(2)]
        MT1 = [sb("MT1_%d" % i, [128, 512], F32) for i in range(2)]
        MT2 = [sb("MT2_%d" % i, [128, 512], F32) for i in range(2)]
        MER = sb("MER", [128, NC_, 512], BF16)
        XTOK = [sb("XTOK%d" % i, [128, D], F32) for i in range(2)]
        OTOK = [sb("OTOK%d" % i, [128, D], F32) for i in range(2)]
        SCR = sb("SCR", [128, 512], F32)
        ST2 = sb("ST2", [128, 8], F32)

        C.sbuf_left_C = nc.sbuf_bytes_remaining
        C.pool = list(range(6))
        with nc.Block() as block:
            w_in_v = C.w_in.rearrange("(c p) n -> p c n", p=128)
            for a in range(0, 2048, 1024):
                S.op("pool", lambda e, a=a: e.dma_start(out=WG[:, :, a:a + 1024], in_=w_in_v[:, :, C_GATE + a:C_GATE + a + 1024]), writes=["w"], dma=True)
            S.op("pool", lambda e: e.dma_start(out=WOA[:], in_=C.w_out_a.rearrange("(c p) n -> p c n", p=128)), writes=["w"], dma=True)
            S.op("pool", lambda e: e.dma_start(out=WOB[:], in_=C.w_out_b.rearrange("(c p) n -> p c n", p=128)), writes=["w"], dma=True)
            S.op("pool", lambda e: e.dma_start(out=WO[:], in_=C.w_o.rearrange("(c p) n -> p c n", p=128)), writes=["w"], dma=True)
            _col_load(S, nc, BG[:, :], C.b_gate, 16, "vecs")
            S.op("sp", lambda e: e.dma_start(out=GPB[:], in_=C.g_post.partition_broadcast(128)), writes=["vecs"], dma=True)
            make_u(S, C, 0)
            for st in range(NST):
                t0 = st * 512
                C.cur_slot = st % 2
                C.XG, C.RSTD = C.XGS[st % 2], C.RSTDS[st % 2]
                S.deferred = []
                if st + 1 < NST:
                    S.defer = S.deferred
                    save = (C.cur_slot, C.XG, C.RSTD)
                    C.pool = [6]
                    make_u(S, C, st + 1)
                    C.pool = list(range(6))
                    C.cur_slot, C.XG, C.RSTD = save
                    S.defer = None
                for ft in range(8):
                    S.flush(6)
                    i2 = ft % 2
                    bga = proj(S, C, WG, ft * 128)
                    S.op("dve", lambda e, bga=bga, i2=i2: e.tensor_tensor(out=GA[i2][:], in0=C.PA[bga][:, :], in1=C.RSTD[:], op=ALU.mult),
                         reads=[("pa", bga), ("rstd", C.cur_slot)], writes=[("ga", i2)])
                    S.op("act", lambda e, i2=i2, ft=ft: e.activation(out=GA[i2][:], in_=GA[i2][:], func=AF.Sigmoid, bias=BG[:, ft:ft + 1]),
                         reads=[("ga", i2), "vecs"], writes=[("ga", i2)])
                    bgb = proj(S, C, WG, 1024 + ft * 128)
                    S.op("dve", lambda e, bgb=bgb, i2=i2: e.tensor_tensor(out=GB[i2][:], in0=C.PA[bgb][:, :], in1=C.RSTD[:], op=ALU.mult),
                         reads=[("pa", bgb), ("rstd", C.cur_slot)], writes=[("gb", i2)])
                    S.op("act", lambda e, i2=i2, ft=ft: e.activation(out=GB[i2][:], in_=GB[i2][:], func=AF.Sigmoid, bias=BG[:, 8 + ft:9 + ft]),
                         reads=[("gb", i2), "vecs"], writes=[("gb", i2)])
                    bya = C.bank()
                    for kc in range(4):
                        S.op("pe", lambda e, kc=kc, ft=ft, bya=bya, t0=t0: e.matmul(C.PA[bya][:, :], lhsT=WOA[:, kc, ft * 128:(ft + 1) * 128],
                                                                                rhs=C.YZ[:, kc, t0:t0 + 512], start=(kc == 0), stop=(kc == 3)),
                             reads=["yz", "w"], writes=[("pa", bya)])
                    S.op("dve", lambda e, bya=bya, i2=i2: e.tensor_tensor(out=MT1[i2][:], in0=C.PA[bya][:, :], in1=GA[i2][:], op=ALU.mult),
                         reads=[("pa", bya), ("ga", i2)], writes=[("mt1", i2)])
                    byb = C.bank()
                    for kc in range(4):
                        S.op("pe", lambda e, kc=kc, ft=ft, byb=byb, t0=t0: e.matmul(C.PA[byb][:, :], lhsT=WOB[:, kc, ft * 128:(ft + 1) * 128],
                                                                                rhs=C.OG[:, kc, t0:t0 + 512], start=(kc == 0), stop=(kc == 3)),
                             reads=["og", "w"], writes=[("pa", byb)])
                    S.op("dve", lambda e, byb=byb, i2=i2: e.tensor_tensor(out=MT2[i2][:], in0=C.PA[byb][:, :], in1=GB[i2][:], op=ALU.mult),
                         reads=[("pa", byb), ("gb", i2)], writes=[("mt2", i2)])
                    S.op("pool", lambda e, i2=i2, ft=ft: e.tensor_tensor(out=MER[:, ft, :], in0=MT1[i2][:], in1=MT2[i2][:], op=ALU.add),
                         reads=[("mt1", i2), ("mt2", i2)], writes=[("mer", ft)])
                S.flush(None)
                for j in range(4):
                    tb = j * 128
                    ta = t0 + tb
                    xi = (st * 4 + j) % 2
                    S.op("sp", lambda e, xi=xi, ta=ta: e.dma_start(out=XTOK[xi][:], in_=C.x[ta:ta + 128, :]), writes=[("xtok", xi)], dma=True)
                    banks = []
                    for half in range(2):
                        bi_ = C.bank()
                        banks.append(bi_)
                        bo = C.PA[bi_]
                        bkey = ("pa", bi_)
                        for ft in range(8):
                            S.op("pe", lambda e, ft=ft, half=half, bo=bo, tb=tb: e.matmul(bo[:, :], lhsT=MER[:, ft, tb:tb + 128], rhs=WO[:, ft, half * 512:(half + 1) * 512],
                                                                                     start=(ft == 0), stop=(ft == 7)),
                                 reads=[("mer", ft), "w"], writes=[bkey])
                        S.op("act", lambda e, half=half, bo=bo: e.activation(out=SCR[:], in_=bo[:, :], func=AF.Square, accum_out=ST2[:, half:half + 1]),
                             reads=[bkey], writes=["scr", ("st2", half)])
                    S.op("dve", lambda e: e.tensor_tensor(out=ST2[:, 2:3], in0=ST2[:, 0:1], in1=ST2[:, 1:2], op=ALU.add),
                         reads=[("st2", 0), ("st2", 1)], writes=[("st2", 2)])
                    S.op("act", lambda e: e.activation(out=ST2[:, 3:4], in_=ST2[:, 2:3], func=AF.Ln, scale=1.0 / D, bias=C.eps_norm[:, 0:1]),
                         reads=[("st2", 2), "consts"], writes=[("st2", 3)])
                    S.op("act", lambda e: e.activation(out=ST2[:, 4:5], in_=ST2[:, 3:4], func=AF.Exp, scale=-0.5), reads=[("st2", 3)], writes=[("st2", 4)])
                    for half in range(2):
                        bo = C.PA[banks[half]]
                        bkey = ("pa", banks[half])
                        hs = slice(half * 512, (half + 1) * 512)
                        S.op("dve", lambda e, bo=bo, hs=hs, xi=xi: e.scalar_tensor_tensor(out=OTOK[xi][:, hs], in0=bo[:, :], scalar=ST2[:, 4:5], in1=GPB[:, hs],
                                                                                     op0=ALU.mult, op1=ALU.mult),
                             reads=[bkey, ("st2", 4), "vecs"], writes=[("otok", xi, half)])
                        S.op("pool", lambda e, hs=hs, xi=xi: e.tensor_tensor(out=OTOK[xi][:, hs], in0=OTOK[xi][:, hs], in1=XTOK[xi][:, hs], op=ALU.add),
                             reads=[("otok", xi, half), ("xtok", xi)], writes=[("otok", xi, half)])
                    S.op("sp", lambda e, xi=xi, ta=ta: e.dma_start(out=C.out[ta:ta + 128, :], in_=OTOK[xi][:]),
                         reads=[("otok", xi, 0), ("otok", xi, 1)], writes=["outdram"], dma=True)
            S.finalize_and_emit(block)


def build(T=4096, phases="ABC", debug=False, max_ops=None):
    nc = bass.Bass("TRN2", target_bir_lowering=False)
    Sched.max_ops = max_ops
    C = Ctx()
    dt = lambda name, shape, dtp=F32: nc.dram_tensor(name, shape, dtp, kind="ExternalInput").ap()
    C.xT = dt("xT", [D, T])
    C.x = dt("x", [T, D])
    C.positions = dt("positions", [T], I32)
    C.g_pre = dt("g_pre", [D])
    C.w_in = dt("w_in", [D, IN_COLS])
    C.b_gate = dt("b_gate", [2 * D])
    C.mu_shift = dt("mu_shift", [1664])
    C.w0 = dt("w0", [512])
    C.w_decay_up = dt("w_decay_up", [64, 512])
    C.a0 = dt("a0", [512])
    C.w_iclr_up = dt("w_iclr_up", [64, 512])
    C.k_k = dt("k_k", [512])
    C.k_a = dt("k_a", [512])
    C.r_k = dt("r_k", [8, 64])
    C.gn_gain = dt("gn_gain", [512])
    C.gn_bias = dt("gn_bias", [512])
    C.w_out_a = dt("w_out_a", [512, D])
    C.g_q = dt("g_q", [256])
    C.w_uq = dt("w_uq", [256, 768])
    C.g_kv = dt("g_kv", [128])
    C.w_ukv = dt("w_ukv", [128, 1024])
    C.w_out_b = dt("w_out_b", [512, D])
    C.w_o = dt("w_o", [D, D])
    C.g_post = dt("g_post", [D])
    C.rope_invf = dt("rope_invf", [32])
    C.out = nc.dram_tensor("out", [T, D], F32, kind="ExternalOutput").ap()
    dbg = None
    if debug:
        dbg = {"yz": nc.dram_tensor("dbg_yz", [512, T], BF16, kind="ExternalOutput").ap(),
               "og": nc.dram_tensor("dbg_og", [512, T], BF16, kind="ExternalOutput").ap()}

    with ExitStack() as es:
        S = Sched(nc, es)
        sb = lambda name, shape, dtp: es.enter_context(nc.sbuf_tensor(name, shape, dtp))
        C.ident = sb("ident", [128, 128], BF16)
        C.identf = sb("identf", [128, 128], F32)
        C.bones = sb("bones", [128, 128], BF16)
        C.ones_bf = sb("ones_bf", [128, 128], BF16)
        C.eps_norm = sb("eps_norm", [128, 1], F32)
        C.eps_gn = sb("eps_gn", [128, 1], F32)
        C.gpre = sb("gpre", [128, NC_], F32)
        C.YZ = sb("YZ", [128, 4, T], BF16)
        C.ptb_i = 0
        C.ostg_i = 0
        C.PA = [es.enter_context(nc.psum_tensor("PA%d" % i, [128, 512], F32)) for i in range(5)]
        C.PT = es.enter_context(nc.psum_tensor("PT", [128, 1024], BF16))
        C.PY = es.enter_context(nc.psum_tensor("PY", [128, 512], F32))
        C.PS = es.enter_context(nc.psum_tensor("PS", [128, 512], F32))
        C.PA = C.PA + [C.PY, C.PS]
        C.bank_i = 0

        C.pool = list(range(5))
        C.pool_ctr = {}

        def bank():
            key = tuple(C.pool)
            i = C.pool_ctr.get(key, 0)
            C.pool_ctr[key] = i + 1
            return C.pool[i % len(C.pool)]
        C.bank = bank

        with nc.allow_non_contiguous_dma(reason="small per-feature vectors"):
            with nc.Block() as block:
                S.op("pool", lambda e: e.memset(C.identf[:], 0.0), writes=["identf"])
                S.op("pool", lambda e: e.affine_select(out=C.identf[:], in_=C.identf[:], pattern=[[-1, 128]],
                                                       compare_op=ALU.not_equal, fill=1.0, base=0, channel_multiplier=1),
                     reads=["identf"], writes=["identf"])
                S.op("dve", lambda e: e.tensor_copy(out=C.ident[:], in_=C.identf[:]), reads=["identf"], writes=["consts"])
                S.op("pool", lambda e: e.memset(C.ones_bf[:], 1.0), writes=["consts"])
                S.op("pool", lambda e: e.memset(C.bones[:], 0.0), writes=["consts"])
                S.op("pool", lambda e: e.memset(C.bones[0:64, 0:64], 1.0), writes=["consts"])
                S.op("pool", lambda e: e.memset(C.bones[64:128, 64:128], 1.0), writes=["consts"])
                S.op("pool", lambda e: e.memset(C.eps_norm[:], NORM_EPS), writes=["consts"])
                S.op("pool", lambda e: e.memset(C.eps_gn[:], GN_EPS), writes=["consts"])
                _col_load(S, nc, C.gpre[:, :], C.g_pre, NC_, "vecs")
                S.finalize_and_emit(block)
            if "A" in phases:
                phase_A(nc, S, C, T, dbg)
            C.OG = sb("OG", [128, 4, T], BF16)
            if "B" in phases:
                phase_B(nc, S, C, T, dbg)
            if "C" in phases:
                phase_C(nc, S, C, T, dbg)
    C.S = S
    nc._ctx = C
    return nc


def rope_invf():
    inv = (np.float32(ROPE_THETA) ** (-np.arange(0, 32, 2, dtype=np.float32) / np.float32(32))).astype(np.float32)
    return np.concatenate([inv, inv]).astype(np.float32)


_CACHE = {}


def kernel(**inputs):
    x = np.ascontiguousarray(np.asarray(inputs["x"], dtype=np.float32))
    B, T, _ = x.shape
    if "nc" not in _CACHE:
        _CACHE["nc"] = build(T=T, phases="ABC", debug=False)
    nc = _CACHE["nc"]
    pos = np.ascontiguousarray(np.asarray(inputs["positions"]).astype(np.int32))
    wnames = ["g_pre", "w_in", "b_gate", "mu_shift", "w0", "w_decay_up", "a0", "w_iclr_up", "k_k", "k_a", "r_k",
              "gn_gain", "gn_bias", "w_out_a", "g_q", "w_uq", "g_kv", "w_ukv", "w_out_b", "w_o", "g_post"]
    shared = {k: np.ascontiguousarray(np.asarray(inputs[k], dtype=np.float32)) for k in wnames}
    shared["rope_invf"] = rope_invf()
    in_maps = []
    for b in range(B):
        m = dict(shared)
        m["x"] = x[b]
        m["xT"] = np.ascontiguousarray(x[b].T)
        m["positions"] = pos[b]
        in_maps.append(m)
    res = run_bass_kernel_spmd(nc, in_maps, core_ids=list(range(B)))
    return np.stack([np.asarray(r["out"], dtype=np.float32) for r in res.results], axis=0)
```
